# Optimizing a Trainium2 kernel written in Bass

```python
import math
import jax, jax.numpy as jnp
from jax import lax
import numpy as np

D_MODEL = 2048
BATCH = 4
SEQ = 4096
DEPTH = 4

N_A = DEPTH // 2
N_B = DEPTH - N_A
MEM_LEN = 256
RET_HEADS = 8
RET_DK = D_MODEL // 16
RET_DV = 2 * RET_DK
RET_CHUNK = 128
RET_THETA_BASE = 10000.0
DIL_CONFIG = ((128, 1), (512, 4), (2048, 16))
DIL_HEADS = 16
DIL_DH = D_MODEL // 16
DIL_BLOCK = 128
MEM_HEADS = 4
MEM_DH = D_MODEL // 8
EPS = 1e-6

RET_QK_W = RET_HEADS * RET_DK
RET_V_W = RET_HEADS * RET_DV
DIL_W = DIL_HEADS * DIL_DH
MEM_W = MEM_HEADS * MEM_DH
MIX_W_A = RET_V_W + MEM_W
MIX_W_B = DIL_W + MEM_W
IN_W_A = 2 * RET_QK_W + 2 * RET_V_W + 2 * MEM_W
IN_W_B = len(DIL_CONFIG) * DIL_W + DIL_W + 2 * MEM_W
KV_W = 2 * len(DIL_CONFIG) * DIL_W

kernel_name = "yoco_retention_dilated_hybrid"


def split_cols(a, widths):
    idx, acc = [], 0
    for w in widths[:-1]:
        acc += w
        idx.append(acc)
    return jnp.split(a, idx, axis=-1)


def rmsnorm(x, g):
    xf = x.astype(jnp.float32)
    y = xf * lax.rsqrt(jnp.mean(xf * xf, axis=-1, keepdims=True) + EPS) * g.astype(jnp.float32)
    return y.astype(x.dtype)


def rotary(t):
    S, half = t.shape[1], t.shape[-1] // 2
    inv = 1.0 / (RET_THETA_BASE ** jnp.linspace(0.0, 1.0, half, dtype=jnp.float32))
    ang = jnp.arange(S, dtype=jnp.float32)[:, None] * inv[None, :]
    cos = jnp.cos(ang)[None, :, None, :]
    sin = jnp.sin(ang)[None, :, None, :]
    tf = t.astype(jnp.float32)
    t1, t2 = tf[..., :half], tf[..., half:]
    return jnp.concatenate([t1 * cos - t2 * sin, t1 * sin + t2 * cos], axis=-1)


def retention(q, k, v):
    B, S, H, Dk = q.shape
    Dv = v.shape[-1]
    C = math.gcd(S, RET_CHUNK)
    N = S // C
    log_g = jnp.log1p(-(2.0 ** (-5.0 - jnp.arange(H, dtype=jnp.float32))))
    qc = q.reshape(B, N, C, H, Dk)
    kc = k.reshape(B, N, C, H, Dk)
    vc = v.astype(jnp.float32).reshape(B, N, C, H, Dv)
    pos = jnp.arange(C, dtype=jnp.float32)
    rel = pos[:, None] - pos[None, :]
    decay_in = jnp.where(rel >= 0, jnp.exp(jnp.maximum(rel, 0.0)[None] * log_g[:, None, None]), 0.0)
    xi = jnp.exp((pos[None, :] + 1.0) * log_g[:, None])
    zeta = jnp.exp((C - 1.0 - pos[None, :]) * log_g[:, None])
    g_chunk = jnp.exp(C * log_g)
    s = jnp.einsum('bnqhd,bnkhd->bnhqk', qc, kc) * decay_in[None, None]
    o_in = jnp.einsum('bnhqk,bnkhe->bnqhe', s, vc)
    U = jnp.einsum('bnkhd,hk,bnkhe->bnhde', kc, zeta, vc)

    def step(R, U_n):
        return g_chunk[None, :, None, None] * R + U_n, R

    _, R_prev = lax.scan(step, jnp.zeros_like(U[:, 0]), jnp.moveaxis(U, 1, 0))
    cross = jnp.einsum('bnqhd,nbhde->bnqhe', qc, R_prev) * xi.T[None, None, :, :, None]
    return (o_in + cross).reshape(B, S, H, Dv)


def memory_branch(mq, gate, mem_n, w_mem_kv):
    B, S, _ = mq.shape
    M = mem_n.shape[1]
    mk, mv = split_cols(mem_n @ w_mem_kv, [MEM_W, MEM_W])
    q = mq.reshape(B, S, MEM_HEADS, MEM_DH)
    k = mk.reshape(B, M, MEM_HEADS, MEM_DH)
    v = mv.reshape(B, M, MEM_HEADS, MEM_DH)
    s = jnp.einsum('bshd,bmhd->bhsm', q, k).astype(jnp.float32) * (MEM_DH ** -0.5)
    p = jax.nn.softmax(s, axis=-1)
    o = jnp.einsum('bhsm,bmhd->bshd', p.astype(v.dtype), v).reshape(B, S, MEM_W)
    return o * jax.nn.silu(gate)


def dilated_group(q, k, v, dil, span):
    B, S, H, D = q.shape
    n = S // dil
    Qb = math.gcd(n, DIL_BLOCK)
    nb = n // Qb

    def to_res(t):
        return t.reshape(B, n, dil, H, D).transpose(0, 2, 1, 3, 4)

    qr = to_res(q).reshape(B, dil, nb, Qb, H, D)
    pad = ((0, 0), (0, 0), (span, 0), (0, 0), (0, 0))
    kr = jnp.pad(to_res(k), pad)
    vr = jnp.pad(to_res(v), pad)
    idx = jnp.arange(nb)[:, None] * Qb + jnp.arange(Qb + span)[None, :]
    kb = kr[:, :, idx]
    vb = vr[:, :, idx]
    dist = jnp.arange(Qb)[:, None] + span - jnp.arange(Qb + span)[None, :]
    band = (dist >= 0) & (dist <= span)
    mask = band[None] & ((idx - span)[:, None, :] >= 0)
    s = jnp.einsum('bgnqhd,bgnkhd->bgnhqk', qr, kb).astype(jnp.float32) * (D ** -0.5)
    s = jnp.where(mask[None, None, :, None], s, -1e30)
    m = jnp.max(s, axis=-1, keepdims=True)
    p = jnp.exp(s - m)
    l = jnp.sum(p, axis=-1)
    o = jnp.einsum('bgnhqk,bgnkhd->bgnqhd', p, vb.astype(jnp.float32))
    o = o / jnp.transpose(l, (0, 1, 2, 4, 3))[..., None]
    lse = jnp.transpose(m[..., 0] + jnp.log(l), (0, 1, 2, 4, 3))
    o = o.reshape(B, dil, n, H, D).transpose(0, 2, 1, 3, 4).reshape(B, S, H, D)
    lse = lse.reshape(B, dil, n, H).transpose(0, 2, 1, 3).reshape(B, S, H)
    return o, lse


def retention_layer(x, g, w_in, w_mem_kv, w_out, mem_n):
    B, S, _ = x.shape
    h = rmsnorm(x, g)
    q, k, v, gate_r, mq, gate_m = split_cols(h @ w_in, [RET_QK_W, RET_QK_W, RET_V_W, RET_V_W, MEM_W, MEM_W])
    q = rotary(q.reshape(B, S, RET_HEADS, RET_DK))
    k = rotary(k.reshape(B, S, RET_HEADS, RET_DK)) * (RET_DK ** -0.5)
    o = retention(q, k, v.reshape(B, S, RET_HEADS, RET_DV))
    o = o * lax.rsqrt(jnp.mean(o * o, axis=-1, keepdims=True) + EPS)
    o_ret = o.reshape(B, S, RET_V_W).astype(x.dtype) * jax.nn.silu(gate_r)
    o_mem = memory_branch(mq, gate_m, mem_n, w_mem_kv)
    return x + jnp.concatenate([o_ret, o_mem], axis=-1) @ w_out


def shared_kv(x, g, w_kv):
    B, S, _ = x.shape
    parts = split_cols(rmsnorm(x, g) @ w_kv, [DIL_W] * (2 * len(DIL_CONFIG)))
    return [t.reshape(B, S, DIL_HEADS, DIL_DH) for t in parts]


def dilated_layer(x, g, w_in, w_mem_kv, w_out, mem_n, kv):
    B, S, _ = x.shape
    h = rmsnorm(x, g)
    n_g = len(DIL_CONFIG)
    parts = split_cols(h @ w_in, [DIL_W] * n_g + [DIL_W, MEM_W, MEM_W])
    gate_d, mq, gate_m = parts[n_g], parts[n_g + 1], parts[n_g + 2]
    outs, lses = [], []
    for gi, (window, dil) in enumerate(DIL_CONFIG):
        q = parts[gi].reshape(B, S, DIL_HEADS, DIL_DH)
        o, lse = dilated_group(q, kv[2 * gi], kv[2 * gi + 1], dil, window // dil)
        outs.append(o)
        lses.append(lse)
    alpha = jax.nn.softmax(jnp.stack(lses, axis=0), axis=0)
    o = jnp.sum(alpha[..., None] * jnp.stack(outs, axis=0), axis=0)
    o_dil = o.reshape(B, S, DIL_W).astype(x.dtype) * jax.nn.silu(gate_d)
    o_mem = memory_branch(mq, gate_m, mem_n, w_mem_kv)
    return x + jnp.concatenate([o_dil, o_mem], axis=-1) @ w_out


def setup_inputs(seed: int = 0) -> dict:
    key = jax.random.key(seed)
    ks = jax.random.split(key, 14)
    f32 = jnp.float32

    def w(k, shape, fan_in):
        return jax.random.normal(k, shape, f32) * (fan_in ** -0.5)

    def gain(k, shape):
        return 1.0 + 0.02 * jax.random.normal(k, shape, f32)

    return {
        "x": jax.random.normal(ks[0], (BATCH, SEQ, D_MODEL), f32),
        "mem": jax.random.normal(ks[1], (BATCH, MEM_LEN, D_MODEL), f32),
        "norm_a": gain(ks[2], (N_A, D_MODEL)),
        "w_in_a": w(ks[3], (N_A, D_MODEL, IN_W_A), D_MODEL),
        "w_out_a": w(ks[4], (N_A, MIX_W_A, D_MODEL), MIX_W_A),
        "norm_b": gain(ks[5], (N_B, D_MODEL)),
        "w_in_b": w(ks[6], (N_B, D_MODEL, IN_W_B), D_MODEL),
        "w_out_b": w(ks[7], (N_B, MIX_W_B, D_MODEL), MIX_W_B),
        "w_mem_kv": w(ks[8], (DEPTH, D_MODEL, 2 * MEM_W), D_MODEL),
        "mem_norm_g": gain(ks[9], (D_MODEL,)),
        "kv_norm_g": gain(ks[10], (D_MODEL,)),
        "w_kv": w(ks[11], (D_MODEL, KV_W), D_MODEL),
        "final_norm_g": gain(ks[12], (D_MODEL,)),
    }


def reference(x, mem, norm_a, w_in_a, w_out_a, norm_b, w_in_b, w_out_b, w_mem_kv, mem_norm_g, kv_norm_g, w_kv, final_norm_g):
    mem_n = rmsnorm(mem, mem_norm_g)
    kv = None
    for layer in range(DEPTH):
        if layer < N_A:
            x = retention_layer(x, norm_a[layer], w_in_a[layer], w_mem_kv[layer], w_out_a[layer], mem_n)
            if layer == N_A - 1:
                kv = shared_kv(x, kv_norm_g, w_kv)
        else:
            j = layer - N_A
            x = dilated_layer(x, norm_b[j], w_in_b[j], w_mem_kv[layer], w_out_b[j], mem_n, kv)
    return rmsnorm(x, final_norm_g)
```

```python
import math
import numpy as np
import ml_dtypes
import concourse.bass as bass
import concourse.mybir as mybir
from concourse.bass_utils import run_bass_kernel_spmd

F32 = mybir.dt.float32
BF16 = mybir.dt.bfloat16
AF = mybir.ActivationFunctionType
ALU = mybir.AluOpType

D = 2048
NT = 2048
NCH = NT // 128
EPS = 1e-6
SEM_LIMIT = 8000
DILS = (1, 4, 16)
HOFF = (0, 128, 640)
HSUM = 2688


class Holder:
    def __init__(self, fw, name):
        self.fw = fw
        self.name = name
        self.gen = 0
        self.sem = fw.new_sem(f"{name}_{self.gen}")
        self.count = 0

    def bump(self, n):
        self.count += n
        return (self.sem, self.count)

    def maybe_rotate(self, n=1):
        if self.count + n > SEM_LIMIT:
            self.gen += 1
            self.sem = self.fw.new_sem(f"{self.name}_{self.gen}")
            self.count = 0


class Buf:
    __slots__ = ("w", "r")

    def __init__(self):
        self.w = []
        self.r = []


class Eng:
    def __init__(self, fw, e, name, self_sync):
        self.e = e
        self.name = name
        self.h = Holder(fw, "s_" + name)
        self.seen = {}
        self.self_sync = self_sync

    def wait(self, tok, same_ok=False):
        if tok is None:
            return
        sem, val = tok
        if sem is self.h.sem and (same_ok or not self.self_sync):
            return
        if self.seen.get(sem.num, 0) >= val:
            return
        self.e.wait_ge(sem, val)
        self.seen[sem.num] = val


class FW:
    def __init__(self, nc):
        self.nc = nc
        self._ctx = []
        self.pe = Eng(self, nc.tensor, "pe", False)
        self.act = Eng(self, nc.scalar, "act", True)
        self.dve = Eng(self, nc.vector, "dve", True)
        self.pool = Eng(self, nc.gpsimd, "pool", False)
        self.sp = Eng(self, nc.sync, "sp", False)
        self.lanes_q = {id(self.sp): [Holder(self, f"dls{i}") for i in range(16)],
                        id(self.pool): [Holder(self, f"dlp{i}") for i in range(8)]}
        self.lane_i = {id(self.sp): 0, id(self.pool): 0}
        self.lanes = self.lanes_q[id(self.sp)] + self.lanes_q[id(self.pool)]
        self.cch = Holder(self, "cc")
        self._n = 0

    def new_sem(self, name):
        cm = self.nc.semaphore(name)
        s = cm.__enter__()
        self._ctx.append(cm)
        return s

    def sbuf(self, shape, dt):
        self._n += 1
        cm = self.nc.sbuf_tensor(f"sb{self._n}", list(shape), dt)
        t = cm.__enter__()
        self._ctx.append(cm)
        return t

    def psum(self, shape, dt):
        self._n += 1
        cm = self.nc.psum_tensor(f"ps{self._n}", list(shape), dt)
        t = cm.__enter__()
        self._ctx.append(cm)
        return t

    def mark(self):
        return len(self._ctx)

    def release(self, mark):
        while len(self._ctx) > mark:
            self._ctx.pop().__exit__(None, None, None)

    def close(self):
        self.release(0)

    def op(self, eng, fn, reads=(), writes=()):
        for b in reads:
            for t in b.w:
                eng.wait(t)
        for b in writes:
            for t in b.w:
                eng.wait(t, same_ok=True)
            for t in b.r:
                eng.wait(t, same_ok=True)
        eng.h.maybe_rotate(1)
        ins = fn(eng.e)
        tok = eng.h.bump(1)
        ins.then_inc(tok[0], 1)
        for b in reads:
            b.r.append(tok)
        for b in writes:
            b.w = [tok]
            b.r = []
        return tok

    def dma(self, q, out, in_, reads=(), writes=(), more=False):
        ql = self.lanes_q[id(q)]
        lane = ql[self.lane_i[id(q)]]
        self.lane_i[id(q)] = (self.lane_i[id(q)] + 1) % len(ql)
        if lane.count:
            q.wait((lane.sem, lane.count))
        lane.maybe_rotate(16)
        for b in reads:
            for t in b.w:
                q.wait(t)
        for b in writes:
            if not more:
                for t in b.w:
                    q.wait(t)
            for t in b.r:
                q.wait(t)
        ins = q.e.dma_start(out=out, in_=in_)
        tok = lane.bump(16)
        ins.then_inc(tok[0], 16)
        for b in reads:
            b.r.append(tok)
        for b in writes:
            if more:
                b.w.append(tok)
            else:
                b.w = [tok]
            b.r = []
        return tok

    def allgather(self, in_ap, out_ap, reads, writes):
        q = self.pool
        for b in reads:
            for t in b.w:
                q.wait(t)
        for b in writes:
            for t in b.w + b.r:
                q.wait(t)
        if self.cch.count:
            q.wait((self.cch.sem, self.cch.count))
        self.cch.maybe_rotate(1)
        ins = q.e.collective_compute("AllGather", ALU.bypass, replica_groups=[[0, 1], [2, 3], [4, 5], [6, 7]],
                                     ins=[in_ap], outs=[out_ap])
        tok = self.cch.bump(1)
        ins.then_inc(tok[0], 1)
        for b in reads:
            b.r.append(tok)
        for b in writes:
            b.w = [tok]
            b.r = []
        return tok

    def barrier(self):
        engs = [self.pe, self.act, self.dve, self.sp, self.pool]
        toks = [(e.h.sem, e.h.count) for e in engs if e.h.count]
        toks += [(l.sem, l.count) for l in self.lanes if l.count]
        for e in engs:
            for t in toks:
                e.wait(t, same_ok=True)


class KB:
    def __init__(self, nc, cst_ap):
        self.nc = nc
        self.fw = FW(nc)
        fw = self.fw
        self.banks = [fw.psum([128, 512], F32) for _ in range(8)]
        self.bbufs = [Buf() for _ in range(8)]
        self.bi = 0
        self.cst = fw.sbuf([128, 3, 128], BF16)
        self.b_cst = Buf()
        fw.dma(fw.sp, self.cst[:], cst_ap, writes=[self.b_cst])
        self.ident = self.cst[:, 0, :]
        self.ones = self.cst[:, 1, :]
        self.pmat = self.cst[:, 2, :]
        self.wb = [fw.sbuf([128, 12288], BF16) for _ in range(2)]
        self.b_wb = [Buf(), Buf()]
        self.wi = 0
        self.ev = 0

    def bank(self):
        i = self.bi
        self.bi = (self.bi + 1) % 8
        return self.banks[i], self.bbufs[i]

    def wnext(self, K, ncols):
        i = self.wi
        self.wi = 1 - self.wi
        view = self.wb[i][:, 0:K * ncols].rearrange("p (k c) -> p k c", k=K)
        return view, self.b_wb[i]

    def load_w(self, wview, bw, w2d, col0, ncols, dst0, more=False):
        src = w2d[:, col0:col0 + ncols].rearrange("(k p) c -> p k c", p=128)
        self.fw.dma(self.fw.pool, wview[:, :, dst0:dst0 + ncols], src, writes=[bw], more=more)

    def mm(self, out_ap, pairs, reads, bbuf):
        n = len(pairs)

        def f(e):
            ins = None
            for i, (l, r) in enumerate(pairs):
                ins = e.matmul(out_ap, lhsT=l, rhs=r, start=(i == 0), stop=(i == n - 1))
            return ins
        return self.fw.op(self.fw.pe, f, reads=reads, writes=[bbuf])

    def evac_eng(self):
        self.ev ^= 1
        return self.fw.act if self.ev else self.fw.dve

    def copy(self, eng, out, in_, reads, writes):
        if eng is self.fw.act:
            return self.fw.op(eng, lambda e: e.copy(out=out, in_=in_), reads=reads, writes=writes)
        return self.fw.op(eng, lambda e: e.tensor_copy(out=out, in_=in_), reads=reads, writes=writes)


def phase_norm(kb, x_ap, g_ap, hT, b_hT, nchunks, out_ap=None, out_toks=None):
    fw = kb.fw
    m = fw.mark()
    gB = fw.sbuf([128, D], F32); b_gB = Buf()
    fw.dma(fw.sp, gB[:], g_ap.broadcast_to([128, D]), writes=[b_gB])
    NX = 3
    xt = [fw.sbuf([128, D], F32) for _ in range(NX)]; b_xt = [Buf() for _ in range(NX)]
    odt = BF16 if hT is not None else F32
    hb = [fw.sbuf([128, D], odt) for _ in range(2)]; b_hb = [Buf(), Buf()]
    junk = fw.sbuf([128, D], BF16); b_junk = Buf()
    st = fw.sbuf([128, nchunks, 4], F32); b_st = [Buf() for _ in range(nchunks)]

    def load(c):
        i = c % NX
        fw.dma(fw.sp, xt[i][:], x_ap[c * 128:(c + 1) * 128, :], writes=[b_xt[i]])

    def stage1(c):
        i = c % NX
        j = c % 2
        fw.op(fw.act, lambda e: e.activation(out=junk[:], in_=xt[i][:], func=AF.Square, accum_out=st[:, c, 0:1]),
              reads=[b_xt[i]], writes=[b_junk, b_st[c]])
        fw.op(fw.act, lambda e: e.activation(out=st[:, c, 1:2], in_=st[:, c, 0:1], func=AF.Sqrt, scale=1.0 / D, bias=EPS),
              reads=[b_st[c]], writes=[b_st[c]])
        fw.op(fw.dve, lambda e: e.reciprocal(out=st[:, c, 2:3], in_=st[:, c, 1:2]), reads=[b_st[c]], writes=[b_st[c]])
        fw.op(fw.dve, lambda e: e.scalar_tensor_tensor(out=hb[j][:], in0=xt[i][:], scalar=st[:, c, 2:3], in1=gB[:],
                                                       op0=ALU.mult, op1=ALU.mult),
              reads=[b_xt[i], b_st[c], b_gB], writes=[b_hb[j]])

    def stage2(c):
        j = c % 2
        if hT is not None:
            eng = kb.evac_eng()
            for b4 in range(4):
                ps, bps = kb.bank()
                for jj in range(4):
                    kk = 4 * b4 + jj
                    kb.mm(ps[:, jj * 128:(jj + 1) * 128], [(hb[j][:, kk * 128:(kk + 1) * 128], kb.ident)],
                          [b_hb[j], kb.b_cst], bps)
                kb.copy(eng, hT[:, 4 * b4:4 * b4 + 4, c * 128:(c + 1) * 128],
                        ps[:].rearrange("p (j t) -> p j t", j=4), [bps], [b_hT[c]])
        else:
            t = fw.dma(fw.sp, out_ap[c * 128:(c + 1) * 128, :], hb[j][:], reads=[b_hb[j]])
            out_toks.append(t)

    for c in range(min(NX, nchunks)):
        load(c)
    stage1(0)
    for c in range(nchunks):
        if c + 1 < nchunks:
            stage1(c + 1)
        if c + NX < nchunks:
            load(c + NX)
        stage2(c)
    fw.barrier()
    fw.release(m)


def proj_fm(kb, wcols, bw, hT, b_hT, tb):
    ps, bps = kb.bank()
    pairs = [(wcols[:, k, :], hT[:, k, tb * 512:(tb + 1) * 512]) for k in range(16)]
    kb.mm(ps[:, :], pairs, [bw] + b_hT[4 * tb:4 * tb + 4], bps)
    return ps, bps


def proj_tm(kb, w, bw, ncols, hT, b_hT, c):
    ps, bps = kb.bank()
    pairs = [(hT[:, k, c * 128:(c + 1) * 128], w[:, k, :]) for k in range(16)]
    kb.mm(ps[:, 0:ncols], pairs, [bw, b_hT[c]], bps)
    return ps, bps


def phase_mem(kb, mem_ap, gmem_ap, wmkv, w_in, cq, cg, hT, b_hT, mixT, hooks=None):
    fw = kb.fw
    m0 = fw.mark()
    memT = fw.sbuf([128, 16, 256], BF16); b_memT = [Buf(), Buf()]
    phase_norm(kb, mem_ap, gmem_ap, memT, b_memT, 2)
    mkT = fw.sbuf([128, 8, 256], BF16); b_mkT = Buf()
    mvt = fw.sbuf([128, 2, 1024], BF16); b_mvt = Buf()
    for blk in range(2):
        w, bw = kb.wnext(16, 512)
        kb.load_w(w, bw, wmkv, blk * 512, 512, 0)
        for j in range(4):
            ps, bps = kb.bank()
            kb.mm(ps[:, 0:256], [(w[:, k, j * 128:(j + 1) * 128], memT[:, k, :]) for k in range(16)], [bw] + b_memT, bps)
            kb.copy(kb.evac_eng(), mkT[:, blk * 4 + j, :], ps[:, 0:256], [bps], [b_mkT])
    for blk in range(2):
        w, bw = kb.wnext(16, 512)
        kb.load_w(w, bw, wmkv, 1024 + blk * 512, 512, 0)
        for jc in range(2):
            ps, bps = kb.bank()
            kb.mm(ps[:, :], [(memT[:, k, jc * 128:(jc + 1) * 128], w[:, k, :]) for k in range(16)], [bw] + b_memT, bps)
            kb.copy(kb.evac_eng(), mvt[:, jc, blk * 512:(blk + 1) * 512], ps[:, :], [bps], [b_mvt])
    mqT = fw.sbuf([128, 2, NT], BF16); b_mq = Buf()
    gmT = fw.sbuf([128, 2, NT], BF16); b_gm = Buf()
    pT = fw.sbuf([128, 2, 512], BF16); b_pT = Buf()
    rl = fw.sbuf([128, 512], F32); b_rl = Buf()
    tmp = fw.sbuf([128, 512], F32); b_tmp = Buf()
    mst = fw.sbuf([128, 2, NT], BF16); b_mst = Buf()
    for m in range(4):
        w, bw = kb.wnext(16, 512)
        kb.load_w(w, bw, w_in, cq + m * 256, 256, 0)
        kb.load_w(w, bw, w_in, cg + m * 256, 256, 256, more=True)
        if hooks:
            for hk in hooks[m]:
                hk()
        for tb in range(4):
            for i in range(2):
                ps, bps = proj_fm(kb, w[:, :, i * 128:(i + 1) * 128], bw, hT, b_hT, tb)
                fw.op(fw.act, lambda e: e.copy(out=mqT[:, i, tb * 512:(tb + 1) * 512], in_=ps[:, :]), reads=[bps], writes=[b_mq])
            for i in range(2):
                ps, bps = proj_fm(kb, w[:, :, 256 + i * 128:256 + (i + 1) * 128], bw, hT, b_hT, tb)
                fw.op(fw.act, lambda e: e.activation(out=gmT[:, i, tb * 512:(tb + 1) * 512], in_=ps[:, :], func=AF.Silu),
                      reads=[bps], writes=[b_gm])
        for tb in range(4):
            tsl = slice(tb * 512, (tb + 1) * 512)
            for j in range(2):
                ps, bps = kb.bank()
                kb.mm(ps[:, :], [(mkT[:, 2 * m + i, j * 128:(j + 1) * 128], mqT[:, i, tsl]) for i in range(2)],
                      [b_mkT, b_mq], bps)
                fw.op(fw.act, lambda e: e.activation(out=pT[:, j, :], in_=ps[:, :], func=AF.Exp, scale=1.0 / 16.0),
                      reads=[bps], writes=[b_pT])
            pl, bpl = kb.bank()
            kb.mm(pl[:, :], [(kb.ones, pT[:, j, :]) for j in range(2)], [kb.b_cst, b_pT], bpl)
            fw.op(fw.dve, lambda e: e.reciprocal(out=rl[:], in_=pl[:, :]), reads=[bpl], writes=[b_rl])
            for i in range(2):
                po, bpo = kb.bank()
                kb.mm(po[:, :], [(mvt[:, j, m * 256 + i * 128:m * 256 + (i + 1) * 128], pT[:, j, :]) for j in range(2)],
                      [b_mvt, b_pT], bpo)
                fw.op(fw.dve, lambda e: e.tensor_tensor(out=tmp[:], in0=po[:, :], in1=rl[:], op=ALU.mult),
                      reads=[bpo, b_rl], writes=[b_tmp])
                fw.op(fw.dve, lambda e: e.tensor_tensor(out=mst[:, i, tsl], in0=tmp[:], in1=gmT[:, i, tsl], op=ALU.mult),
                      reads=[b_tmp, b_gm], writes=[b_mst])
        r0 = 2048 + m * 256
        fw.dma(fw.sp, mixT[r0:r0 + 256, :].rearrange("(j p) t -> p j t", p=128), mst[:], reads=[b_mst])
    fw.barrier()
    fw.release(m0)


def phase_out(kb, mixT, w_out, x_in, x_out, hT, out_toks=None):
    fw = kb.fw
    m0 = fw.mark()
    mx = fw.sbuf([128, 8, NT], BF16)
    b_mr = [Buf() for _ in range(6)]
    for i in range(6):
        dst = hT[:, 4 * i:4 * i + 4, :] if i < 4 else mx[:, 4 * (i - 4):4 * (i - 4) + 4, :]
        fw.dma(fw.sp, dst, mixT[i * 512:(i + 1) * 512, :].rearrange("(k p) t -> p k t", p=128), writes=[b_mr[i]])

    def mk(k, c):
        return hT[:, k, c * 128:(c + 1) * 128] if k < 16 else mx[:, k - 16, c * 128:(c + 1) * 128]
    xo = [fw.sbuf([128, 512], F32) for _ in range(4)]; b_xo = [Buf() for _ in range(4)]
    xi = 0
    for n in range(4):
        w, bw = kb.wnext(24, 512)
        kb.load_w(w, bw, w_out, n * 512, 512, 0)
        for c in range(NCH):
            xb = xi % 4
            xi += 1
            fw.dma(fw.sp, xo[xb][:], x_in[c * 128:(c + 1) * 128, n * 512:(n + 1) * 512], writes=[b_xo[xb]])
            ps, bps = kb.bank()
            kb.mm(ps[:, :], [(mk(k, c), w[:, k, :]) for k in range(24)], b_mr + [bw], bps)
            fw.op(fw.dve, lambda e: e.tensor_tensor(out=xo[xb][:], in0=ps[:, :], in1=xo[xb][:], op=ALU.add),
                  reads=[bps, b_xo[xb]], writes=[b_xo[xb]])
            t = fw.dma(fw.sp, x_out[c * 128:(c + 1) * 128, n * 512:(n + 1) * 512], xo[xb][:], reads=[b_xo[xb]])
            if out_toks is not None:
                out_toks.append(t)
    fw.barrier()
    fw.release(m0)


def finish(kb, toks):
    fw = kb.fw
    fw.barrier()
    for t in toks:
        fw.sp.wait(t)
    fw.close()


def heads_A(kb, hT, b_hT, w_in, cs_in, dm_in, gC, mixT, cc_src, cc_dst):
    fw = kb.fw
    m0 = fw.mark()
    cs = fw.sbuf([128, 2, NT], F32); b_cs = Buf()
    fw.dma(fw.sp, cs[:], cs_in, writes=[b_cs])
    dm = fw.sbuf([128, 17, 128], F32); b_dm = Buf()
    fw.dma(fw.sp, dm[:], dm_in, writes=[b_dm])
    qT = fw.sbuf([128, NT], BF16); b_qT = Buf()
    kT = fw.sbuf([128, NT], BF16); b_kT = Buf()
    kz = fw.sbuf([128, NCH, 128], BF16); b_kz = Buf()
    vtm = fw.sbuf([128, NCH, 256], BF16); b_vc = [Buf() for _ in range(NCH)]
    gs = fw.sbuf([128, NCH, 256], BF16); b_gc = [Buf() for _ in range(NCH)]
    raw = [fw.sbuf([128, 512], BF16) for _ in range(2)]; b_raw = [Buf(), Buf()]
    t1 = [fw.sbuf([128, 512], F32) for _ in range(2)]; b_t1 = [Buf(), Buf()]
    t2 = [fw.sbuf([128, 512], F32) for _ in range(2)]; b_t2 = [Buf(), Buf()]
    mth = fw.sbuf([128, 2, NT], BF16); b_mth = Buf()
    R32 = [fw.sbuf([128, 256], F32) for _ in range(2)]; b_R32 = [Buf(), Buf()]
    Rbf = [fw.sbuf([128, 256], BF16) for _ in range(2)]; b_Rbf = [Buf(), Buf()]
    Rg = fw.sbuf([128, 256], F32); b_Rg = Buf()
    sTm = [fw.sbuf([128, 128], BF16) for _ in range(2)]; b_sTm = [Buf(), Buf()]
    mo = [fw.sbuf([128, 256], BF16) for _ in range(3)]; b_mo = [Buf(), Buf(), Buf()]
    junk = fw.sbuf([128, 256], BF16); b_junk = Buf()
    st = fw.sbuf([128, NCH, 4], F32); b_st = [Buf() for _ in range(NCH)]

    def qk_proj(h, which, w1, bw1):
        for tb in range(4):
            tsl = slice(tb * 512, (tb + 1) * 512)
            ps, bps = proj_fm(kb, w1[:, :, which * 128:(which + 1) * 128], bw1, hT, b_hT, tb)
            fw.op(fw.act, lambda e: e.copy(out=raw[which][:], in_=ps[:, :]), reads=[bps], writes=[b_raw[which]])
            p2, bp2 = kb.bank()
            kb.mm(p2[:, :], [(kb.pmat, raw[which][:])], [kb.b_cst, b_raw[which]], bp2)
            fw.op(fw.dve, lambda e: e.tensor_tensor(out=t1[which][:], in0=raw[which][:], in1=cs[:, 0, tsl], op=ALU.mult),
                  reads=[b_raw[which], b_cs], writes=[b_t1[which]])
            fw.op(fw.dve, lambda e: e.tensor_tensor(out=t2[which][:], in0=p2[:, :], in1=cs[:, 1, tsl], op=ALU.mult),
                  reads=[bp2, b_cs], writes=[b_t2[which]])
            if which == 0:
                fw.op(fw.dve, lambda e: e.tensor_tensor(out=t1[0][:], in0=t1[0][:], in1=t2[0][:], op=ALU.add),
                      reads=[b_t1[0], b_t2[0]], writes=[b_t1[0]])
                xib = dm[:, 8 + h, :].unsqueeze(1).broadcast_to([128, 4, 128])
                fw.op(fw.dve, lambda e: e.tensor_tensor(out=qT[:, tsl].rearrange("p (c t) -> p c t", c=4),
                                                        in0=t1[0][:].rearrange("p (c t) -> p c t", c=4), in1=xib, op=ALU.mult),
                      reads=[b_t1[0], b_dm], writes=[b_qT])
            else:
                fw.op(fw.dve, lambda e: e.tensor_tensor(out=kT[:, tsl], in0=t1[1][:], in1=t2[1][:], op=ALU.add),
                      reads=[b_t1[1], b_t2[1]], writes=[b_kT])

    for h in range(8):
        w1, bw1 = kb.wnext(16, 256)
        kb.load_w(w1, bw1, w_in, 1024 + h * 128, 128, 128)
        kb.load_w(w1, bw1, w_in, h * 128, 128, 0, more=True)
        w2, bw2 = kb.wnext(16, 512)
        kb.load_w(w2, bw2, w_in, 2048 + h * 256, 256, 0)
        kb.load_w(w2, bw2, w_in, 4096 + h * 256, 256, 256, more=True)
        qk_proj(h, 1, w1, bw1)
        for c4 in range(4):
            ps, bps = kb.bank()
            for j in range(4):
                c = 4 * c4 + j
                kb.mm(ps[:, j * 128:(j + 1) * 128], [(kT[:, c * 128:(c + 1) * 128], kb.ident)], [b_kT, kb.b_cst], bps)
            fw.op(fw.dve, lambda e: e.tensor_scalar_mul(out=kz[:, 4 * c4:4 * c4 + 4, :],
                                                        in0=ps[:].rearrange("p (j t) -> p j t", j=4),
                                                        scalar1=dm[:, 16, h:h + 1]),
                  reads=[bps, b_dm], writes=[b_kz])
        fw.op(fw.dve, lambda e: e.memset(R32[0][:], 0.0), writes=[b_R32[0]])

        def p1(c):
            cur, nxt = c % 2, 1 - (c % 2)
            ps_u, bu = kb.bank()
            kb.mm(ps_u[:, 0:256], [(kz[:, c, :], vtm[:, c, :])], [b_kz, b_vc[c]], bu)
            fw.op(fw.dve, lambda e: e.scalar_tensor_tensor(out=R32[nxt][:], in0=R32[cur][:], scalar=float(gC[h]), in1=ps_u[:, 0:256],
                                                           op0=ALU.mult, op1=ALU.add),
                  reads=[b_R32[cur], bu], writes=[b_R32[nxt]])
        for c in range(NCH):
            ps, bps = proj_tm(kb, w2, bw2, 512, hT, b_hT, c)
            fw.op(fw.act, lambda e: e.copy(out=vtm[:, c, :], in_=ps[:, 0:256]), reads=[bps], writes=[b_vc[c]])
            fw.op(fw.act, lambda e: e.activation(out=gs[:, c, :], in_=ps[:, 256:512], func=AF.Silu), reads=[bps], writes=[b_gc[c]])
            if c >= 1:
                p1(c - 1)
        p1(NCH - 1)
        b_src, b_dst = Buf(), Buf()
        fw.dma(fw.sp, cc_src[h, :, :], R32[0][:], reads=[b_R32[0]], writes=[b_src])
        fw.allgather(cc_src[h, :, :], cc_dst[h, :, :], [b_src], [b_dst])
        qk_proj(h, 0, w1, bw1)
        fw.dma(fw.sp, Rg[:], cc_dst[h, 0:128, :], reads=[b_dst], writes=[b_Rg])
        fw.op(fw.dve, lambda e: e.tensor_scalar_mul(out=R32[0][:], in0=Rg[:], scalar1=dm[:, 16, 8:9]),
              reads=[b_Rg, b_dm], writes=[b_R32[0]])
        fw.op(fw.act, lambda e: e.copy(out=Rbf[0][:], in_=R32[0][:]), reads=[b_R32[0]], writes=[b_Rbf[0]])
        def S1(c):
            csl = slice(c * 128, (c + 1) * 128)
            ps_s, bs = kb.bank()
            kb.mm(ps_s[:, 0:128], [(kT[:, csl], qT[:, csl])], [b_kT, b_qT], bs)
            fw.op(fw.dve, lambda e: e.tensor_tensor(out=sTm[c % 2][:], in0=ps_s[:, 0:128], in1=dm[:, h, :], op=ALU.mult),
                  reads=[bs, b_dm], writes=[b_sTm[c % 2]])

        def RS(c):
            cur, nxt = c % 2, 1 - (c % 2)
            ps_u, bu = kb.bank()
            kb.mm(ps_u[:, 0:256], [(kz[:, c, :], vtm[:, c, :])], [b_kz, b_vc[c]], bu)
            fw.op(fw.dve, lambda e: e.scalar_tensor_tensor(out=R32[nxt][:], in0=R32[cur][:], scalar=float(gC[h]), in1=ps_u[:, 0:256],
                                                           op0=ALU.mult, op1=ALU.add),
                  reads=[b_R32[cur], bu], writes=[b_R32[nxt]])
            fw.op(fw.act, lambda e: e.copy(out=Rbf[nxt][:], in_=R32[nxt][:]), reads=[b_R32[nxt]], writes=[b_Rbf[nxt]])

        def S2(c):
            cur = c % 2
            csl = slice(c * 128, (c + 1) * 128)
            ps_o, bo = kb.bank()
            kb.mm(ps_o[:, 0:256], [(sTm[cur][:], vtm[:, c, :]), (qT[:, csl], Rbf[cur][:])],
                  [b_sTm[cur], b_vc[c], b_qT, b_Rbf[cur]], bo)
            fw.op(fw.act, lambda e: e.activation(out=junk[:], in_=ps_o[:, 0:256], func=AF.Square, accum_out=st[:, c, 0:1]),
                  reads=[bo], writes=[b_junk, b_st[c]])
            fw.op(fw.act, lambda e: e.activation(out=st[:, c, 1:2], in_=st[:, c, 0:1], func=AF.Sqrt, scale=1.0 / 256, bias=EPS),
                  reads=[b_st[c]], writes=[b_st[c]])
            fw.op(fw.dve, lambda e: e.reciprocal(out=st[:, c, 2:3], in_=st[:, c, 1:2]), reads=[b_st[c]], writes=[b_st[c]])
            fw.op(fw.dve, lambda e: e.scalar_tensor_tensor(out=mo[c % 3][:], in0=ps_o[:, 0:256], scalar=st[:, c, 2:3], in1=gs[:, c, :],
                                                           op0=ALU.mult, op1=ALU.mult),
                  reads=[bo, b_st[c], b_gc[c]], writes=[b_mo[c % 3]])

        def S3(c):
            csl = slice(c * 128, (c + 1) * 128)
            ps_t, bt = kb.bank()
            for i in range(2):
                kb.mm(ps_t[:, i * 128:(i + 1) * 128], [(mo[c % 3][:, i * 128:(i + 1) * 128], kb.ident)], [b_mo[c % 3], kb.b_cst], bt)
            fw.op(fw.act, lambda e: e.copy(out=mth[:, :, csl], in_=ps_t[:, 0:256].rearrange("p (i t) -> p i t", i=2)),
                  reads=[bt], writes=[b_mth])

        S1(0)
        for c in range(NCH):
            if c + 1 < NCH:
                S1(c + 1)
                RS(c)
            S2(c)
            if c >= 2:
                S3(c - 2)
        S3(NCH - 2)
        S3(NCH - 1)
        fw.dma(fw.sp, mixT[h * 256:(h + 1) * 256, :].rearrange("(j p) t -> p j t", p=128), mth[:], reads=[b_mth])
    fw.barrier()
    fw.release(m0)


def phase_kv(kb, hT, b_hT, w_kv, KT, V, KS, VS, KVS, KVG):
    fw = kb.fw
    m0 = fw.mark()
    kst = [fw.sbuf([128, NT], BF16) for _ in range(2)]; b_kst = [Buf(), Buf()]
    vst = [fw.sbuf([128, NCH, 512], BF16) for _ in range(2)]; b_vst = [Buf(), Buf()]
    ki = 0
    vi = 0
    b_KG = [Buf() for _ in range(16)]
    b_VG = [Buf() for _ in range(16)]
    b_kt = [[Buf() for _ in range(4)] for _ in range(3)]
    b_vt = [[Buf() for _ in range(4)] for _ in range(3)]

    def step(d):
        bks, bvs = Buf(), Buf()
        for g in range(3):
            hl = 128 * DILS[g]
            fw.dma(fw.sp, KS[d, :, HOFF[g]:HOFF[g] + hl], KT[g][d * 128:(d + 1) * 128, NT - hl:NT],
                   reads=[b_kt[g][d // 4]], writes=[bks], more=True)
            fw.dma(fw.sp, VS[d, :, HOFF[g]:HOFF[g] + hl].rearrange("j (r c) -> j r c", c=128),
                   V[g][NT - hl:NT, d * 128:(d + 1) * 128].rearrange("(j r) c -> j r c", r=DILS[g]),
                   reads=[b_vt[g][d // 4]], writes=[bvs], more=True)
        fw.allgather(KVS[d, :, :], KVG[d, :, :], [bks, bvs], [b_KG[d], b_VG[d]])
    pending = []

    def trickle():
        if pending:
            pending.pop(0)()
    for blk in range(4):
        for g in range(3):
            w, bw = kb.wnext(16, 512)
            kb.load_w(w, bw, w_kv, (2 * g) * 2048 + blk * 512, 512, 0)
            trickle()
            for j in range(4):
                s = ki % 2
                ki += 1
                for tb in range(4):
                    ps, bps = proj_fm(kb, w[:, :, j * 128:(j + 1) * 128], bw, hT, b_hT, tb)
                    kb.copy(kb.evac_eng(), kst[s][:, tb * 512:(tb + 1) * 512], ps[:, :], [bps], [b_kst[s]])
                r0 = (blk * 4 + j) * 128
                fw.dma(fw.sp, KT[g][r0:r0 + 128, :], kst[s][:], reads=[b_kst[s]], writes=[b_kt[g][blk]], more=True)
            w, bw = kb.wnext(16, 512)
            kb.load_w(w, bw, w_kv, (2 * g + 1) * 2048 + blk * 512, 512, 0)
            trickle()
            s = vi % 2
            vi += 1
            for c in range(NCH):
                ps, bps = proj_tm(kb, w, bw, 512, hT, b_hT, c)
                kb.copy(kb.evac_eng(), vst[s][:, c, :], ps[:, :], [bps], [b_vst[s]])
            fw.dma(fw.sp, V[g][:, blk * 512:(blk + 1) * 512].rearrange("(c p) f -> p c f", p=128), vst[s][:],
                   reads=[b_vst[s]], writes=[b_vt[g][blk]], more=True)
        pending.extend([(lambda d=d: step(d)) for d in range(4 * blk, 4 * blk + 4)])
    fw.barrier()
    fw.release(m0)
    return b_KG, b_VG, pending


def heads_B(kb, hT, b_hT, w_in, msk_in, KT, V, KG, VG, b_KG, b_VG, mixT):
    fw = kb.fw
    m0 = fw.mark()
    msk = fw.sbuf([128, 4, 128], BF16); b_msk = Buf()
    fw.dma(fw.sp, msk[:], msk_in, writes=[b_msk])
    qT = fw.sbuf([128, 3, NT], BF16); b_qT = Buf()
    gdT = [fw.sbuf([128, NT], BF16) for _ in range(2)]; b_gd = [Buf(), Buf()]
    KTt = [fw.sbuf([128, 128 * DILS[g] + NT], BF16) for g in range(3)]; b_K = [Buf() for _ in range(3)]
    Vt = [fw.sbuf([128, DILS[g], 16 // DILS[g] + 1, 128], BF16) for g in range(3)]; b_V = [Buf() for _ in range(3)]
    O32 = fw.sbuf([128, NT], F32); b_O = Buf()
    L32 = fw.sbuf([128, NT], F32); b_L = Buf()
    pT = [fw.sbuf([128, 512], BF16) for _ in range(2)]; b_pT = [Buf(), Buf()]
    mst = fw.sbuf([128, NT], BF16); b_mst = Buf()
    scale = 128.0 ** -0.5
    pi = 0

    def load_kv(d, g):
        hl = 128 * DILS[g]
        fw.dma(fw.sp, KTt[g][:, 0:hl], KG[d, 0:128, HOFF[g]:HOFF[g] + hl], reads=[b_KG[d]], writes=[b_K[g]])
        fw.dma(fw.sp, KTt[g][:, hl:hl + NT], KT[g][d * 128:(d + 1) * 128, :], writes=[b_K[g]], more=True)
        fw.dma(fw.sp, Vt[g][:, :, 0, :], VG[d, 0:128, HOFF[g]:HOFF[g] + hl].rearrange("j (r c) -> j r c", c=128),
               reads=[b_VG[d]], writes=[b_V[g]])
        for r in range(DILS[g]):
            src_o = V[g][:, d * 128:(d + 1) * 128].rearrange("(b j r) c -> j r b c", j=128, r=DILS[g])[:, r, :, :]
            fw.dma(fw.sp, Vt[g][:, r, 1:, :], src_o, writes=[b_V[g]], more=True)

    for d in range(16):
        w, bw = kb.wnext(16, 512)
        for g in range(3):
            kb.load_w(w, bw, w_in, g * 2048 + d * 128, 128, g * 128, more=(g > 0))
        kb.load_w(w, bw, w_in, 6144 + d * 128, 128, 384, more=True)
        if d == 0:
            for g in range(3):
                load_kv(0, g)
        for tb in range(4):
            tsl = slice(tb * 512, (tb + 1) * 512)
            for s in range(4):
                ps, bps = proj_fm(kb, w[:, :, s * 128:(s + 1) * 128], bw, hT, b_hT, tb)
                if s < 3:
                    fw.op(fw.act, lambda e: e.copy(out=qT[:, s, tsl], in_=ps[:, :]), reads=[bps], writes=[b_qT])
                else:
                    fw.op(fw.act, lambda e: e.activation(out=gdT[d % 2][:, tsl], in_=ps[:, :], func=AF.Silu), reads=[bps], writes=[b_gd[d % 2]])
        items = []
        for g in range(3):
            dil = DILS[g]
            nb = 16 // dil
            blocks = [(r, b) for r in range(dil) for b in range(nb)]
            for p0 in range(0, 16, 2):
                items.append((g, blocks[p0:p0 + 2]))
        state = {}

        def emit_S(i):
            g, pair = items[i]
            dil = DILS[g]
            qv = qT[:, g, :].rearrange("p (b j r) -> p r b j", j=128, r=dil)
            kv = KTt[g][:, :].rearrange("p (b j r) -> p r b j", j=128, r=dil)
            u = i % 2
            ps_s, bs = kb.bank()
            for qi, (r, b) in enumerate(pair):
                mo_ = 2 if b == 0 else 0
                kb.mm(ps_s[:, (2 * qi) * 128:(2 * qi + 1) * 128],
                      [(kv[:, r, b + 1, :], qv[:, r, b, :]), (kb.ident, msk[:, mo_, :])], [b_K[g], b_qT, b_msk, kb.b_cst], bs)
                kb.mm(ps_s[:, (2 * qi + 1) * 128:(2 * qi + 2) * 128],
                      [(kv[:, r, b, :], qv[:, r, b, :]), (kb.ident, msk[:, mo_ + 1, :])], [b_K[g], b_qT, b_msk, kb.b_cst], bs)
            fw.op(fw.act, lambda e: e.activation(out=pT[u][:], in_=ps_s[:, :], func=AF.Exp, scale=scale), reads=[bs], writes=[b_pT[u]])

        def emit_PV(i):
            g, pair = items[i]
            dil = DILS[g]
            Ov = O32[:, :].rearrange("p (b j r) -> p r b j", j=128, r=dil)
            Lv = L32[:, :].rearrange("p (b j r) -> p r b j", j=128, r=dil)
            u = i % 2
            ps_o, bo = kb.bank()
            ps_l, bl = kb.bank()
            for qi, (r, b) in enumerate(pair):
                kb.mm(ps_o[:, qi * 128:(qi + 1) * 128],
                      [(Vt[g][:, r, b + 1, :], pT[u][:, (2 * qi) * 128:(2 * qi + 1) * 128]),
                       (Vt[g][:, r, b, :], pT[u][:, (2 * qi + 1) * 128:(2 * qi + 2) * 128])], [b_V[g], b_pT[u]], bo)
                kb.mm(ps_l[:, qi * 128:(qi + 1) * 128],
                      [(kb.ones, pT[u][:, (2 * qi) * 128:(2 * qi + 1) * 128]),
                       (kb.ones, pT[u][:, (2 * qi + 1) * 128:(2 * qi + 2) * 128])], [kb.b_cst, b_pT[u]], bl)
            (r0, b0), (r1, b1) = pair
            if r0 == r1:
                oo = Ov[:, r0, b0:b0 + 2, :]
                ll = Lv[:, r0, b0:b0 + 2, :]
            else:
                oo = Ov[:, r0:r0 + 2, b0, :]
                ll = Lv[:, r0:r0 + 2, b0, :]
            pso = ps_o[:, 0:256].rearrange("p (a j) -> p a j", a=2)
            psl = ps_l[:, 0:256].rearrange("p (a j) -> p a j", a=2)
            if g == 0:
                fw.op(fw.act, lambda e: e.copy(out=oo, in_=pso), reads=[bo], writes=[b_O])
                fw.op(fw.act, lambda e: e.copy(out=ll, in_=psl), reads=[bl], writes=[b_L])
            else:
                fw.op(fw.dve, lambda e: e.tensor_tensor(out=oo, in0=oo, in1=pso, op=ALU.add), reads=[bo, b_O], writes=[b_O])
                fw.op(fw.dve, lambda e: e.tensor_tensor(out=ll, in0=ll, in1=psl, op=ALU.add), reads=[bl, b_L], writes=[b_L])
            if i % 8 == 7 and d + 1 < 16:
                load_kv(d + 1, g)

        for i in range(len(items) + 1):
            if i < len(items):
                emit_S(i)
            if i >= 1:
                emit_PV(i - 1)
        fw.op(fw.dve, lambda e: e.reciprocal(out=L32[:], in_=L32[:]), reads=[b_L], writes=[b_L])
        fw.op(fw.dve, lambda e: e.tensor_tensor(out=O32[:], in0=O32[:], in1=L32[:], op=ALU.mult), reads=[b_L, b_O], writes=[b_O])
        fw.op(fw.dve, lambda e: e.tensor_tensor(out=mst[:], in0=O32[:], in1=gdT[d % 2][:], op=ALU.mult), reads=[b_O, b_gd[d % 2]], writes=[b_mst])
        fw.dma(fw.sp, mixT[d * 128:(d + 1) * 128, :], mst[:], reads=[b_mst])
    fw.barrier()
    fw.release(m0)


def build_fused(gC, stages=5):
    nc = bass.Bass("TRN2", target_bir_lowering=False)
    dt = nc.dram_tensor

    def inp(name, shape, d=F32):
        return dt(name, shape, d, kind="ExternalInput").ap()
    x = inp("x", [NT, D])
    mem = inp("mem", [256, D])
    norm_a = inp("norm_a", [2, D])
    w_in_a = inp("w_in_a", [2, D, 8192])
    w_out_a = inp("w_out_a", [2, 3072, D])
    norm_b = inp("norm_b", [2, D])
    w_in_b = inp("w_in_b", [2, D, 10240])
    w_out_b = inp("w_out_b", [2, 3072, D])
    w_mem_kv = inp("w_mem_kv", [4, D, 2048])
    g_m = inp("g_m", [1, D])
    g_kv = inp("g_kv", [1, D])
    w_kv = inp("w_kv", [D, 12288])
    g_f = inp("g_f", [1, D])
    cs_in = inp("cs", [128, 2, NT])
    dm_in = inp("dm", [128, 17, 128])
    cst_in = inp("cst", [128, 3, 128], BF16)
    msk_in = inp("msk", [128, 4, 128], BF16)
    y = dt("y", [NT, D], F32, kind="ExternalOutput").ap()
    xp = [dt(f"xp{i}", [NT, D], F32).ap() for i in range(2)]
    mixT = dt("mixT", [3072, NT], BF16).ap()
    cc_src = [dt(f"ccs{l}", [8, 128, 256], F32).ap() for l in range(2)]
    cc_dst = [dt(f"ccd{l}", [8, 256, 256], F32).ap() for l in range(2)]
    KT = [dt(f"KT{g}", [D, NT], BF16).ap() for g in range(3)]
    V = [dt(f"V{g}", [NT, D], BF16).ap() for g in range(3)]
    KVS = dt("KVS", [16, 128, 2 * HSUM], BF16).ap()
    KVG = dt("KVG", [16, 256, 2 * HSUM], BF16).ap()
    KS = KVS[:, :, 0:HSUM]
    VS = KVS[:, :, HSUM:2 * HSUM]
    KG = KVG[:, :, 0:HSUM]
    VG = KVG[:, :, HSUM:2 * HSUM]

    kb = KB(nc, cst_in)
    fw = kb.fw
    out_toks = []
    hT = fw.sbuf([128, 16, NT], BF16)
    b_hT = [Buf() for _ in range(NCH)]

    xin = x
    for l in range(min(2, stages)):
        xout = xp[l]
        phase_norm(kb, xin, norm_a[l:l + 1, :], hT, b_hT, NCH)
        heads_A(kb, hT, b_hT, w_in_a[l], cs_in, dm_in, gC, mixT, cc_src[l], cc_dst[l])
        phase_mem(kb, mem, g_m, w_mem_kv[l], w_in_a[l], 6144, 7168, hT, b_hT, mixT)
        phase_out(kb, mixT, w_out_a[l], xin, xout, hT)
        xin = xout
    if stages >= 3:
        phase_norm(kb, xin, g_kv, hT, b_hT, NCH)
        b_KG, b_VG, xsteps = phase_kv(kb, hT, b_hT, w_kv, KT, V, KS, VS, KVS, KVG)
    for l in range(max(0, stages - 3)):
        xout = xp[l]
        hooks = None
        if l == 0:
            nx = len(xsteps)
            hooks = {m: xsteps[m * nx // 4:(m + 1) * nx // 4] for m in range(4)}
        phase_norm(kb, xin, norm_b[l:l + 1, :], hT, b_hT, NCH)
        phase_mem(kb, mem, g_m, w_mem_kv[2 + l], w_in_b[l], 8192, 9216, hT, b_hT, mixT, hooks)
        heads_B(kb, hT, b_hT, w_in_b[l], msk_in, KT, V, KG, VG, b_KG, b_VG, mixT)
        phase_out(kb, mixT, w_out_b[l], xin, xout, hT)
        xin = xout
    phase_norm(kb, xin, g_f, None, None, NCH, out_ap=y, out_toks=out_toks)
    finish(kb, out_toks)
    return nc


def _consts():
    bf = ml_dtypes.bfloat16
    cst = np.zeros((128, 3, 128), np.float32)
    cst[:, 0, :] = np.eye(128)
    cst[:, 1, :] = 1.0
    pm = np.zeros((128, 128), np.float32)
    for i in range(128):
        pm[(i + 64) % 128, i] = 1.0
    cst[:, 2, :] = pm
    half = 64
    inv = (1.0 / (np.float32(10000.0) ** np.linspace(0.0, 1.0, half, dtype=np.float32))).astype(np.float32)
    cs = []
    for hf in range(2):
        pos = (np.arange(NT, dtype=np.float32) + np.float32(hf * NT))
        ang = (pos[:, None] * inv[None, :]).astype(np.float32)
        c = np.cos(ang).astype(np.float32).T
        s = np.sin(ang).astype(np.float32).T
        t = np.zeros((128, 2, NT), np.float32)
        t[:64, 0] = c; t[64:, 0] = c
        t[:64, 1] = -s; t[64:, 1] = s
        cs.append(t)
    hh = np.arange(8, dtype=np.float64)
    log_g = np.log1p(-(2.0 ** (-5.0 - hh)))
    pos = np.arange(128, dtype=np.float64)
    scale = 128.0 ** -0.5
    dms = []
    for hf in range(2):
        dm = np.zeros((128, 17, 128), np.float32)
        for h in range(8):
            kk = pos[:, None]
            qq = pos[None, :]
            dm[:, h, :] = np.where(qq >= kk, scale * np.exp(-(kk + 1.0) * log_g[h]), 0.0)
            dm[:, 8 + h, :] = np.exp((pos + 1.0) * log_g[h])[None, :]
            dm[:, 16, h] = scale * np.exp((127.0 - pos) * log_g[h])
        dm[:, 16, 8] = float(hf)
        dms.append(dm)
    gC = np.exp(128.0 * log_g)
    kk = np.arange(128)[:, None]
    qq = np.arange(128)[None, :]
    NEG = -1.0e4
    cur = np.where(kk <= qq, 0.0, NEG).astype(np.float32)
    prev = np.where(kk >= qq, 0.0, NEG).astype(np.float32)
    dead = np.full((128, 128), NEG, np.float32)
    msk = []
    for hf in range(2):
        m = np.stack([cur, prev, cur, prev if hf == 1 else dead], axis=1)
        msk.append(m.astype(bf))
    return cst.astype(bf), cs, dms, gC, msk


def kernel(x, mem, norm_a, w_in_a, w_out_a, norm_b, w_in_b, w_out_b, w_mem_kv, mem_norm_g, kv_norm_g, w_kv, final_norm_g):
    f32 = np.float32
    x = np.asarray(x, f32); mem = np.asarray(mem, f32)
    shared = {
        "norm_a": np.asarray(norm_a, f32), "w_in_a": np.asarray(w_in_a, f32), "w_out_a": np.asarray(w_out_a, f32),
        "norm_b": np.asarray(norm_b, f32), "w_in_b": np.asarray(w_in_b, f32), "w_out_b": np.asarray(w_out_b, f32),
        "w_mem_kv": np.asarray(w_mem_kv, f32), "g_m": np.asarray(mem_norm_g, f32).reshape(1, D),
        "g_kv": np.asarray(kv_norm_g, f32).reshape(1, D), "w_kv": np.asarray(w_kv, f32),
        "g_f": np.asarray(final_norm_g, f32).reshape(1, D),
    }
    cst, cs, dms, gC, msk = _consts()
    B = x.shape[0]
    nc = build_fused(gC)
    maps = []
    for c in range(2 * B):
        b, hf = c // 2, c % 2
        maps.append(dict(shared, x=np.ascontiguousarray(x[b, hf * NT:(hf + 1) * NT]), mem=mem[b],
                         cs=cs[hf], dm=dms[hf], cst=cst, msk=msk[hf]))
    res = run_bass_kernel_spmd(nc, maps, core_ids=list(range(2 * B))).results
    out = np.stack([np.concatenate([res[2 * b + hf]["y"] for hf in range(2)], axis=0) for b in range(B)], axis=0)
    return out.astype(f32)
```

```python
import math
import numpy as np
import ml_dtypes
import concourse.bass as bass
import concourse.mybir as mybir
from concourse.bass_utils import run_bass_kernel_spmd

F32 = mybir.dt.float32
BF16 = mybir.dt.bfloat16
AF = mybir.ActivationFunctionType
ALU = mybir.AluOpType

D = 2048
NT = 2048
NCH = NT // 128
EPS = 1e-6
SEM_LIMIT = 8000
DILS = (1, 4, 16)
HOFF = (0, 128, 640)
HSUM = 2688


class Holder:
    def __init__(self, fw, name):
        self.fw = fw
        self.name = name
        self.gen = 0
        self.sem = fw.new_sem(f"{name}_{self.gen}")
        self.count = 0

    def bump(self, n):
        self.count += n
        return (self.sem, self.count)

    def maybe_rotate(self, n=1):
        if self.count + n > SEM_LIMIT:
            self.gen += 1
            self.sem = self.fw.new_sem(f"{self.name}_{self.gen}")
            self.count = 0


class Buf:
    __slots__ = ("w", "r")

    def __init__(self):
        self.w = []
        self.r = []


class Eng:
    def __init__(self, fw, e, name, self_sync):
        self.e = e
        self.name = name
        self.h = Holder(fw, "s_" + name)
        self.seen = {}
        self.self_sync = self_sync

    def wait(self, tok, same_ok=False):
        if tok is None:
            return
        sem, val = tok
        if sem is self.h.sem and (same_ok or not self.self_sync):
            return
        if self.seen.get(sem.num, 0) >= val:
            return
        self.e.wait_ge(sem, val)
        self.seen[sem.num] = val


class FW:
    def __init__(self, nc):
        self.nc = nc
        self._ctx = []
        self.pe = Eng(self, nc.tensor, "pe", False)
        self.act = Eng(self, nc.scalar, "act", True)
        self.dve = Eng(self, nc.vector, "dve", True)
        self.pool = Eng(self, nc.gpsimd, "pool", False)
        self.sp = Eng(self, nc.sync, "sp", False)
        self.lanes_q = {id(self.sp): [Holder(self, f"dls{i}") for i in range(16)],
                        id(self.pool): [Holder(self, f"dlp{i}") for i in range(8)]}
        self.lane_i = {id(self.sp): 0, id(self.pool): 0}
        self.lanes = self.lanes_q[id(self.sp)] + self.lanes_q[id(self.pool)]
        self.cch = Holder(self, "cc")
        self._n = 0

    def new_sem(self, name):
        cm = self.nc.semaphore(name)
        s = cm.__enter__()
        self._ctx.append(cm)
        return s

    def sbuf(self, shape, dt):
        self._n += 1
        cm = self.nc.sbuf_tensor(f"sb{self._n}", list(shape), dt)
        t = cm.__enter__()
        self._ctx.append(cm)
        return t

    def psum(self, shape, dt):
        self._n += 1
        cm = self.nc.psum_tensor(f"ps{self._n}", list(shape), dt)
        t = cm.__enter__()
        self._ctx.append(cm)
        return t

    def mark(self):
        return len(self._ctx)

    def release(self, mark):
        while len(self._ctx) > mark:
            self._ctx.pop().__exit__(None, None, None)

    def close(self):
        self.release(0)

    def op(self, eng, fn, reads=(), writes=()):
        for b in reads:
            for t in b.w:
                eng.wait(t)
        for b in writes:
            for t in b.w:
                eng.wait(t, same_ok=True)
            for t in b.r:
                eng.wait(t, same_ok=True)
        eng.h.maybe_rotate(1)
        ins = fn(eng.e)
        tok = eng.h.bump(1)
        ins.then_inc(tok[0], 1)
        for b in reads:
            b.r.append(tok)
        for b in writes:
            b.w = [tok]
            b.r = []
        return tok

    def dma(self, q, out, in_, reads=(), writes=(), more=False):
        ql = self.lanes_q[id(q)]
        lane = ql[self.lane_i[id(q)]]
        self.lane_i[id(q)] = (self.lane_i[id(q)] + 1) % len(ql)
        if lane.count:
            q.wait((lane.sem, lane.count))
        lane.maybe_rotate(16)
        for b in reads:
            for t in b.w:
                q.wait(t)
        for b in writes:
            if not more:
                for t in b.w:
                    q.wait(t)
            for t in b.r:
                q.wait(t)
        ins = q.e.dma_start(out=out, in_=in_)
        tok = lane.bump(16)
        ins.then_inc(tok[0], 16)
        for b in reads:
            b.r.append(tok)
        for b in writes:
            if more:
                b.w.append(tok)
            else:
                b.w = [tok]
            b.r = []
        return tok

    def allgather(self, in_ap, out_ap, reads, writes):
        q = self.pool
        for b in reads:
            for t in b.w:
                q.wait(t)
        for b in writes:
            for t in b.w + b.r:
                q.wait(t)
        if self.cch.count:
            q.wait((self.cch.sem, self.cch.count))
        self.cch.maybe_rotate(1)
        ins = q.e.collective_compute("AllGather", ALU.bypass, replica_groups=[[0, 1], [2, 3], [4, 5], [6, 7]],
                                     ins=[in_ap], outs=[out_ap])
        tok = self.cch.bump(1)
        ins.then_inc(tok[0], 1)
        for b in reads:
            b.r.append(tok)
        for b in writes:
            b.w = [tok]
            b.r = []
        return tok

    def barrier(self):
        engs = [self.pe, self.act, self.dve, self.sp, self.pool]
        toks = [(e.h.sem, e.h.count) for e in engs if e.h.count]
        toks += [(l.sem, l.count) for l in self.lanes if l.count]
        for e in engs:
            for t in toks:
                e.wait(t, same_ok=True)


class KB:
    def __init__(self, nc, cst_ap):
        self.nc = nc
        self.fw = FW(nc)
        fw = self.fw
        self.banks = [fw.psum([128, 512], F32) for _ in range(8)]
        self.bbufs = [Buf() for _ in range(8)]
        self.bi = 0
        self.cst = fw.sbuf([128, 3, 128], BF16)
        self.b_cst = Buf()
        fw.dma(fw.sp, self.cst[:], cst_ap, writes=[self.b_cst])
        self.ident = self.cst[:, 0, :]
        self.ones = self.cst[:, 1, :]
        self.pmat = self.cst[:, 2, :]
        self.wb = [fw.sbuf([128, 12288], BF16) for _ in range(2)]
        self.b_wb = [Buf(), Buf()]
        self.wi = 0
        self.ev = 0

    def bank(self):
        i = self.bi
        self.bi = (self.bi + 1) % 8
        return self.banks[i], self.bbufs[i]

    def wnext(self, K, ncols):
        i = self.wi
        self.wi = 1 - self.wi
        view = self.wb[i][:, 0:K * ncols].rearrange("p (k c) -> p k c", k=K)
        return view, self.b_wb[i]

    def load_w(self, wview, bw, w2d, col0, ncols, dst0, more=False):
        src = w2d[:, col0:col0 + ncols].rearrange("(k p) c -> p k c", p=128)
        self.fw.dma(self.fw.pool, wview[:, :, dst0:dst0 + ncols], src, writes=[bw], more=more)

    def mm(self, out_ap, pairs, reads, bbuf):
        n = len(pairs)

        def f(e):
            ins = None
            for i, (l, r) in enumerate(pairs):
                ins = e.matmul(out_ap, lhsT=l, rhs=r, start=(i == 0), stop=(i == n - 1))
            return ins
        return self.fw.op(self.fw.pe, f, reads=reads, writes=[bbuf])

    def evac_eng(self):
        self.ev ^= 1
        return self.fw.act if self.ev else self.fw.dve

    def copy(self, eng, out, in_, reads, writes):
        if eng is self.fw.act:
            return self.fw.op(eng, lambda e: e.copy(out=out, in_=in_), reads=reads, writes=writes)
        return self.fw.op(eng, lambda e: e.tensor_copy(out=out, in_=in_), reads=reads, writes=writes)


def phase_norm(kb, x_ap, g_ap, hT, b_hT, nchunks, out_ap=None, out_toks=None):
    fw = kb.fw
    m = fw.mark()
    gB = fw.sbuf([128, D], F32); b_gB = Buf()
    fw.dma(fw.sp, gB[:], g_ap.broadcast_to([128, D]), writes=[b_gB])
    NX = 3
    xt = [fw.sbuf([128, D], F32) for _ in range(NX)]; b_xt = [Buf() for _ in range(NX)]
    odt = BF16 if hT is not None else F32
    hb = [fw.sbuf([128, D], odt) for _ in range(2)]; b_hb = [Buf(), Buf()]
    junk = fw.sbuf([128, D], BF16); b_junk = Buf()
    st = fw.sbuf([128, nchunks, 4], F32); b_st = [Buf() for _ in range(nchunks)]

    def load(c):
        i = c % NX
        fw.dma(fw.sp, xt[i][:], x_ap[c * 128:(c + 1) * 128, :], writes=[b_xt[i]])

    def stage1(c):
        i = c % NX
        j = c % 2
        fw.op(fw.act, lambda e: e.activation(out=junk[:], in_=xt[i][:], func=AF.Square, accum_out=st[:, c, 0:1]),
              reads=[b_xt[i]], writes=[b_junk, b_st[c]])
        fw.op(fw.act, lambda e: e.activation(out=st[:, c, 1:2], in_=st[:, c, 0:1], func=AF.Sqrt, scale=1.0 / D, bias=EPS),
              reads=[b_st[c]], writes=[b_st[c]])
        fw.op(fw.dve, lambda e: e.reciprocal(out=st[:, c, 2:3], in_=st[:, c, 1:2]), reads=[b_st[c]], writes=[b_st[c]])
        fw.op(fw.dve, lambda e: e.scalar_tensor_tensor(out=hb[j][:], in0=xt[i][:], scalar=st[:, c, 2:3], in1=gB[:],
                                                       op0=ALU.mult, op1=ALU.mult),
              reads=[b_xt[i], b_st[c], b_gB], writes=[b_hb[j]])

    def stage2(c):
        j = c % 2
        if hT is not None:
            eng = kb.evac_eng()
            for b4 in range(4):
                ps, bps = kb.bank()
                for jj in range(4):
                    kk = 4 * b4 + jj
                    kb.mm(ps[:, jj * 128:(jj + 1) * 128], [(hb[j][:, kk * 128:(kk + 1) * 128], kb.ident)],
                          [b_hb[j], kb.b_cst], bps)
                kb.copy(eng, hT[:, 4 * b4:4 * b4 + 4, c * 128:(c + 1) * 128],
                        ps[:].rearrange("p (j t) -> p j t", j=4), [bps], [b_hT[c]])
        else:
            t = fw.dma(fw.sp, out_ap[c * 128:(c + 1) * 128, :], hb[j][:], reads=[b_hb[j]])
            out_toks.append(t)

    for c in range(min(NX, nchunks)):
        load(c)
    stage1(0)
    for c in range(nchunks):
        if c + 1 < nchunks:
            stage1(c + 1)
        if c + NX < nchunks:
            load(c + NX)
        stage2(c)
    fw.barrier()
    fw.release(m)


def proj_fm(kb, wcols, bw, hT, b_hT, tb):
    ps, bps = kb.bank()
    pairs = [(wcols[:, k, :], hT[:, k, tb * 512:(tb + 1) * 512]) for k in range(16)]
    kb.mm(ps[:, :], pairs, [bw] + b_hT[4 * tb:4 * tb + 4], bps)
    return ps, bps


def proj_tm(kb, w, bw, ncols, hT, b_hT, c):
    ps, bps = kb.bank()
    pairs = [(hT[:, k, c * 128:(c + 1) * 128], w[:, k, :]) for k in range(16)]
    kb.mm(ps[:, 0:ncols], pairs, [bw, b_hT[c]], bps)
    return ps, bps


def phase_mem(kb, mem_ap, gmem_ap, wmkv, w_in, cq, cg, hT, b_hT, mixT, hooks=None):
    fw = kb.fw
    m0 = fw.mark()
    memT = fw.sbuf([128, 16, 256], BF16); b_memT = [Buf(), Buf()]
    phase_norm(kb, mem_ap, gmem_ap, memT, b_memT, 2)
    mkT = fw.sbuf([128, 8, 256], BF16); b_mkT = Buf()
    mvt = fw.sbuf([128, 2, 1024], BF16); b_mvt = Buf()
    for blk in range(2):
        w, bw = kb.wnext(16, 512)
        kb.load_w(w, bw, wmkv, blk * 512, 512, 0)
        for j in range(4):
            ps, bps = kb.bank()
            kb.mm(ps[:, 0:256], [(w[:, k, j * 128:(j + 1) * 128], memT[:, k, :]) for k in range(16)], [bw] + b_memT, bps)
            kb.copy(kb.evac_eng(), mkT[:, blk * 4 + j, :], ps[:, 0:256], [bps], [b_mkT])
    for blk in range(2):
        w, bw = kb.wnext(16, 512)
        kb.load_w(w, bw, wmkv, 1024 + blk * 512, 512, 0)
        for jc in range(2):
            ps, bps = kb.bank()
            kb.mm(ps[:, :], [(memT[:, k, jc * 128:(jc + 1) * 128], w[:, k, :]) for k in range(16)], [bw] + b_memT, bps)
            kb.copy(kb.evac_eng(), mvt[:, jc, blk * 512:(blk + 1) * 512], ps[:, :], [bps], [b_mvt])
    mqT = fw.sbuf([128, 2, NT], BF16); b_mq = Buf()
    gmT = fw.sbuf([128, 2, NT], BF16); b_gm = Buf()
    pT = fw.sbuf([128, 2, 512], BF16); b_pT = Buf()
    rl = fw.sbuf([128, 512], F32); b_rl = Buf()
    tmp = fw.sbuf([128, 512], F32); b_tmp = Buf()
    mst = fw.sbuf([128, 2, NT], BF16); b_mst = Buf()
    for m in range(4):
        w, bw = kb.wnext(16, 512)
        kb.load_w(w, bw, w_in, cq + m * 256, 256, 0)
        kb.load_w(w, bw, w_in, cg + m * 256, 256, 256, more=True)
        if hooks:
            for hk in hooks[m]:
                hk()
        for tb in range(4):
            for i in range(2):
                ps, bps = proj_fm(kb, w[:, :, i * 128:(i + 1) * 128], bw, hT, b_hT, tb)
                fw.op(fw.act, lambda e: e.copy(out=mqT[:, i, tb * 512:(tb + 1) * 512], in_=ps[:, :]), reads=[bps], writes=[b_mq])
            for i in range(2):
                ps, bps = proj_fm(kb, w[:, :, 256 + i * 128:256 + (i + 1) * 128], bw, hT, b_hT, tb)
                fw.op(fw.act, lambda e: e.activation(out=gmT[:, i, tb * 512:(tb + 1) * 512], in_=ps[:, :], func=AF.Silu),
                      reads=[bps], writes=[b_gm])
        for tb in range(4):
            tsl = slice(tb * 512, (tb + 1) * 512)
            for j in range(2):
                ps, bps = kb.bank()
                kb.mm(ps[:, :], [(mkT[:, 2 * m + i, j * 128:(j + 1) * 128], mqT[:, i, tsl]) for i in range(2)],
                      [b_mkT, b_mq], bps)
                fw.op(fw.act, lambda e: e.activation(out=pT[:, j, :], in_=ps[:, :], func=AF.Exp, scale=1.0 / 16.0),
                      reads=[bps], writes=[b_pT])
            pl, bpl = kb.bank()
            kb.mm(pl[:, :], [(kb.ones, pT[:, j, :]) for j in range(2)], [kb.b_cst, b_pT], bpl)
            fw.op(fw.dve, lambda e: e.reciprocal(out=rl[:], in_=pl[:, :]), reads=[bpl], writes=[b_rl])
            for i in range(2):
                po, bpo = kb.bank()
                kb.mm(po[:, :], [(mvt[:, j, m * 256 + i * 128:m * 256 + (i + 1) * 128], pT[:, j, :]) for j in range(2)],
                      [b_mvt, b_pT], bpo)
                fw.op(fw.dve, lambda e: e.tensor_tensor(out=tmp[:], in0=po[:, :], in1=rl[:], op=ALU.mult),
                      reads=[bpo, b_rl], writes=[b_tmp])
                fw.op(fw.dve, lambda e: e.tensor_tensor(out=mst[:, i, tsl], in0=tmp[:], in1=gmT[:, i, tsl], op=ALU.mult),
                      reads=[b_tmp, b_gm], writes=[b_mst])
        r0 = 2048 + m * 256
        fw.dma(fw.sp, mixT[r0:r0 + 256, :].rearrange("(j p) t -> p j t", p=128), mst[:], reads=[b_mst])
    fw.barrier()
    fw.release(m0)


def phase_out(kb, mixT, w_out, x_in, x_out, hT, out_toks=None):
    fw = kb.fw
    m0 = fw.mark()
    mx = fw.sbuf([128, 8, NT], BF16)
    b_mr = [Buf() for _ in range(6)]
    for i in range(6):
        dst = hT[:, 4 * i:4 * i + 4, :] if i < 4 else mx[:, 4 * (i - 4):4 * (i - 4) + 4, :]
        fw.dma(fw.sp, dst, mixT[i * 512:(i + 1) * 512, :].rearrange("(k p) t -> p k t", p=128), writes=[b_mr[i]])

    def mk(k, c):
        return hT[:, k, c * 128:(c + 1) * 128] if k < 16 else mx[:, k - 16, c * 128:(c + 1) * 128]
    xo = [fw.sbuf([128, 512], F32) for _ in range(4)]; b_xo = [Buf() for _ in range(4)]
    xi = 0
    for n in range(4):
        w, bw = kb.wnext(24, 512)
        kb.load_w(w, bw, w_out, n * 512, 512, 0)
        for c in range(NCH):
            xb = xi % 4
            xi += 1
            fw.dma(fw.sp, xo[xb][:], x_in[c * 128:(c + 1) * 128, n * 512:(n + 1) * 512], writes=[b_xo[xb]])
            ps, bps = kb.bank()
            kb.mm(ps[:, :], [(mk(k, c), w[:, k, :]) for k in range(24)], b_mr + [bw], bps)
            fw.op(fw.dve, lambda e: e.tensor_tensor(out=xo[xb][:], in0=ps[:, :], in1=xo[xb][:], op=ALU.add),
                  reads=[bps, b_xo[xb]], writes=[b_xo[xb]])
            t = fw.dma(fw.sp, x_out[c * 128:(c + 1) * 128, n * 512:(n + 1) * 512], xo[xb][:], reads=[b_xo[xb]])
            if out_toks is not None:
                out_toks.append(t)
    fw.barrier()
    fw.release(m0)


def finish(kb, toks):
    fw = kb.fw
    fw.barrier()
    for t in toks:
        fw.sp.wait(t)
    fw.close()


def heads_A(kb, hT, b_hT, w_in, cs_in, dm_in, gC, mixT, cc_src, cc_dst):
    fw = kb.fw
    m0 = fw.mark()
    cs = fw.sbuf([128, 2, NT], F32); b_cs = Buf()
    fw.dma(fw.sp, cs[:], cs_in, writes=[b_cs])
    dm = fw.sbuf([128, 17, 128], F32); b_dm = Buf()
    fw.dma(fw.sp, dm[:], dm_in, writes=[b_dm])
    qT = fw.sbuf([128, NT], BF16); b_qT = Buf()
    kT = fw.sbuf([128, NT], BF16); b_kT = Buf()
    kz = fw.sbuf([128, NCH, 128], BF16); b_kz = Buf()
    vtm = fw.sbuf([128, NCH, 256], BF16); b_vc = [Buf() for _ in range(NCH)]
    gs = fw.sbuf([128, NCH, 256], BF16); b_gc = [Buf() for _ in range(NCH)]
    raw = [fw.sbuf([128, 512], BF16) for _ in range(2)]; b_raw = [Buf(), Buf()]
    t1 = [fw.sbuf([128, 512], F32) for _ in range(2)]; b_t1 = [Buf(), Buf()]
    t2 = [fw.sbuf([128, 512], F32) for _ in range(2)]; b_t2 = [Buf(), Buf()]
    mth = fw.sbuf([128, 2, NT], BF16); b_mth = Buf()
    R32 = [fw.sbuf([128, 256], F32) for _ in range(2)]; b_R32 = [Buf(), Buf()]
    Rbf = [fw.sbuf([128, 256], BF16) for _ in range(4)]; b_Rbf = [Buf() for _ in range(4)]
    Rg = fw.sbuf([128, 256], F32); b_Rg = Buf()
    sTm = [fw.sbuf([128, 128], BF16) for _ in range(4)]; b_sTm = [Buf() for _ in range(4)]
    mo = [fw.sbuf([128, 256], BF16) for _ in range(4)]; b_mo = [Buf() for _ in range(4)]
    junk = fw.sbuf([128, 256], BF16); b_junk = Buf()
    st = fw.sbuf([128, NCH, 4], F32); b_st = [Buf() for _ in range(NCH)]

    def qk_proj(h, which, w1, bw1):
        for tb in range(4):
            tsl = slice(tb * 512, (tb + 1) * 512)
            ps, bps = proj_fm(kb, w1[:, :, which * 128:(which + 1) * 128], bw1, hT, b_hT, tb)
            fw.op(fw.act, lambda e: e.copy(out=raw[which][:], in_=ps[:, :]), reads=[bps], writes=[b_raw[which]])
            p2, bp2 = kb.bank()
            kb.mm(p2[:, :], [(kb.pmat, raw[which][:])], [kb.b_cst, b_raw[which]], bp2)
            fw.op(fw.dve, lambda e: e.tensor_tensor(out=t1[which][:], in0=raw[which][:], in1=cs[:, 0, tsl], op=ALU.mult),
                  reads=[b_raw[which], b_cs], writes=[b_t1[which]])
            fw.op(fw.dve, lambda e: e.tensor_tensor(out=t2[which][:], in0=p2[:, :], in1=cs[:, 1, tsl], op=ALU.mult),
                  reads=[bp2, b_cs], writes=[b_t2[which]])
            if which == 0:
                fw.op(fw.dve, lambda e: e.tensor_tensor(out=t1[0][:], in0=t1[0][:], in1=t2[0][:], op=ALU.add),
                      reads=[b_t1[0], b_t2[0]], writes=[b_t1[0]])
                xib = dm[:, 8 + h, :].unsqueeze(1).broadcast_to([128, 4, 128])
                fw.op(fw.dve, lambda e: e.tensor_tensor(out=qT[:, tsl].rearrange("p (c t) -> p c t", c=4),
                                                        in0=t1[0][:].rearrange("p (c t) -> p c t", c=4), in1=xib, op=ALU.mult),
                      reads=[b_t1[0], b_dm], writes=[b_qT])
            else:
                fw.op(fw.dve, lambda e: e.tensor_tensor(out=kT[:, tsl], in0=t1[1][:], in1=t2[1][:], op=ALU.add),
                      reads=[b_t1[1], b_t2[1]], writes=[b_kT])

    for h in range(8):
        w1, bw1 = kb.wnext(16, 256)
        kb.load_w(w1, bw1, w_in, 1024 + h * 128, 128, 128)
        kb.load_w(w1, bw1, w_in, h * 128, 128, 0, more=True)
        w2, bw2 = kb.wnext(16, 512)
        kb.load_w(w2, bw2, w_in, 2048 + h * 256, 256, 0)
        kb.load_w(w2, bw2, w_in, 4096 + h * 256, 256, 256, more=True)
        qk_proj(h, 1, w1, bw1)
        for c4 in range(4):
            ps, bps = kb.bank()
            for j in range(4):
                c = 4 * c4 + j
                kb.mm(ps[:, j * 128:(j + 1) * 128], [(kT[:, c * 128:(c + 1) * 128], kb.ident)], [b_kT, kb.b_cst], bps)
            fw.op(fw.dve, lambda e: e.tensor_scalar_mul(out=kz[:, 4 * c4:4 * c4 + 4, :],
                                                        in0=ps[:].rearrange("p (j t) -> p j t", j=4),
                                                        scalar1=dm[:, 16, h:h + 1]),
                  reads=[bps, b_dm], writes=[b_kz])
        fw.op(fw.dve, lambda e: e.memset(R32[0][:], 0.0), writes=[b_R32[0]])

        def p1(c):
            cur, nxt = c % 2, 1 - (c % 2)
            ps_u, bu = kb.bank()
            kb.mm(ps_u[:, 0:256], [(kz[:, c, :], vtm[:, c, :])], [b_kz, b_vc[c]], bu)
            fw.op(fw.dve, lambda e: e.scalar_tensor_tensor(out=R32[nxt][:], in0=R32[cur][:], scalar=float(gC[h]), in1=ps_u[:, 0:256],
                                                           op0=ALU.mult, op1=ALU.add),
                  reads=[b_R32[cur], bu], writes=[b_R32[nxt]])
        for c in range(NCH):
            ps, bps = proj_tm(kb, w2, bw2, 512, hT, b_hT, c)
            fw.op(fw.act, lambda e: e.copy(out=vtm[:, c, :], in_=ps[:, 0:256]), reads=[bps], writes=[b_vc[c]])
            fw.op(fw.act, lambda e: e.activation(out=gs[:, c, :], in_=ps[:, 256:512], func=AF.Silu), reads=[bps], writes=[b_gc[c]])
            if c >= 1:
                p1(c - 1)
        p1(NCH - 1)
        b_src, b_dst = Buf(), Buf()
        fw.dma(fw.sp, cc_src[h, :, :], R32[0][:], reads=[b_R32[0]], writes=[b_src])
        fw.allgather(cc_src[h, :, :], cc_dst[h, :, :], [b_src], [b_dst])
        qk_proj(h, 0, w1, bw1)
        fw.dma(fw.sp, Rg[:], cc_dst[h, 0:128, :], reads=[b_dst], writes=[b_Rg])
        fw.op(fw.dve, lambda e: e.tensor_scalar_mul(out=R32[0][:], in0=Rg[:], scalar1=dm[:, 16, 8:9]),
              reads=[b_Rg, b_dm], writes=[b_R32[0]])
        fw.op(fw.act, lambda e: e.copy(out=Rbf[0][:], in_=R32[0][:]), reads=[b_R32[0]], writes=[b_Rbf[0]])
        pbanks = [kb.bank() for _ in range(8)]
        o_sl = [(pbanks[i][0][:, 0:256], pbanks[i][1]) for i in (0, 1, 2)]
        u_sl = [(pbanks[i][0][:, 0:256], pbanks[i][1]) for i in (3, 4)]
        t_sl = [(pbanks[i][0][:, 0:256], pbanks[i][1]) for i in (5,)]
        s_sl = [(pbanks[i][0][:, 0:128], pbanks[i][1]) for i in (6, 7)]

        def S1(c):
            csl = slice(c * 128, (c + 1) * 128)
            ps_s, bs = s_sl[c % len(s_sl)]
            kb.mm(ps_s, [(kT[:, csl], qT[:, csl])], [b_kT, b_qT], bs)
            fw.op(fw.dve, lambda e: e.tensor_tensor(out=sTm[c % 4][:], in0=ps_s, in1=dm[:, h, :], op=ALU.mult),
                  reads=[bs, b_dm], writes=[b_sTm[c % 4]])

        def RS(c):
            cur, nxt = c % 2, 1 - (c % 2)
            ps_u, bu = u_sl[c % len(u_sl)]
            kb.mm(ps_u, [(kz[:, c, :], vtm[:, c, :])], [b_kz, b_vc[c]], bu)
            fw.op(fw.dve, lambda e: e.scalar_tensor_tensor(out=R32[nxt][:], in0=R32[cur][:], scalar=float(gC[h]), in1=ps_u,
                                                           op0=ALU.mult, op1=ALU.add),
                  reads=[b_R32[cur], bu], writes=[b_R32[nxt]])
            fw.op(fw.dve, lambda e: e.tensor_copy(out=Rbf[(c + 1) % 4][:], in_=R32[nxt][:]), reads=[b_R32[nxt]], writes=[b_Rbf[(c + 1) % 4]])

        def S2(c):
            csl = slice(c * 128, (c + 1) * 128)
            ps_o, bo = o_sl[c % len(o_sl)]
            kb.mm(ps_o, [(sTm[c % 4][:], vtm[:, c, :]), (qT[:, csl], Rbf[c % 4][:])],
                  [b_sTm[c % 4], b_vc[c], b_qT, b_Rbf[c % 4]], bo)
            fw.op(fw.act, lambda e: e.activation(out=junk[:], in_=ps_o, func=AF.Square, accum_out=st[:, c, 0:1]),
                  reads=[bo], writes=[b_junk, b_st[c]])
            fw.op(fw.act, lambda e: e.activation(out=st[:, c, 1:2], in_=st[:, c, 0:1], func=AF.Sqrt, scale=1.0 / 256, bias=EPS),
                  reads=[b_st[c]], writes=[b_st[c]])
            fw.op(fw.dve, lambda e: e.reciprocal(out=st[:, c, 2:3], in_=st[:, c, 1:2]), reads=[b_st[c]], writes=[b_st[c]])
            fw.op(fw.dve, lambda e: e.scalar_tensor_tensor(out=mo[c % 4][:], in0=ps_o, scalar=st[:, c, 2:3], in1=gs[:, c, :],
                                                           op0=ALU.mult, op1=ALU.mult),
                  reads=[bo, b_st[c], b_gc[c]], writes=[b_mo[c % 4]])

        def S3(c):
            csl = slice(c * 128, (c + 1) * 128)
            ps_t, bt = t_sl[c % len(t_sl)]
            for i in range(2):
                kb.mm(ps_t[:, i * 128:(i + 1) * 128], [(mo[c % 4][:, i * 128:(i + 1) * 128], kb.ident)], [b_mo[c % 4], kb.b_cst], bt)
            fw.op(fw.act, lambda e: e.copy(out=mth[:, :, csl], in_=ps_t.rearrange("p (i t) -> p i t", i=2)),
                  reads=[bt], writes=[b_mth])

        S1(0)
        S1(1)
        RS(0)
        for c in range(NCH):
            if c + 2 < NCH:
                S1(c + 2)
            if c + 2 < NCH:
                RS(c + 1)
            S2(c)
            if c >= 2:
                S3(c - 2)
        S3(NCH - 2)
        S3(NCH - 1)
        fw.dma(fw.sp, mixT[h * 256:(h + 1) * 256, :].rearrange("(j p) t -> p j t", p=128), mth[:], reads=[b_mth])
    fw.barrier()
    fw.release(m0)


def phase_kv(kb, hT, b_hT, w_kv, KT, V, KS, VS, KVS, KVG):
    fw = kb.fw
    m0 = fw.mark()
    kst = [fw.sbuf([128, NT], BF16) for _ in range(2)]; b_kst = [Buf(), Buf()]
    vst = [fw.sbuf([128, NCH, 512], BF16) for _ in range(2)]; b_vst = [Buf(), Buf()]
    ki = 0
    vi = 0
    b_KG = [Buf() for _ in range(16)]
    b_VG = [Buf() for _ in range(16)]
    b_kt = [[Buf() for _ in range(4)] for _ in range(3)]
    b_vt = [[Buf() for _ in range(4)] for _ in range(3)]

    def step(d):
        bks, bvs = Buf(), Buf()
        for g in range(3):
            hl = 128 * DILS[g]
            fw.dma(fw.sp, KS[d, :, HOFF[g]:HOFF[g] + hl], KT[g][d * 128:(d + 1) * 128, NT - hl:NT],
                   reads=[b_kt[g][d // 4]], writes=[bks], more=True)
            fw.dma(fw.sp, VS[d, :, HOFF[g]:HOFF[g] + hl].rearrange("j (r c) -> j r c", c=128),
                   V[g][NT - hl:NT, d * 128:(d + 1) * 128].rearrange("(j r) c -> j r c", r=DILS[g]),
                   reads=[b_vt[g][d // 4]], writes=[bvs], more=True)
        fw.allgather(KVS[d, :, :], KVG[d, :, :], [bks, bvs], [b_KG[d], b_VG[d]])
    pending = []

    def trickle():
        if pending:
            pending.pop(0)()
    for blk in range(4):
        for g in range(3):
            w, bw = kb.wnext(16, 512)
            kb.load_w(w, bw, w_kv, (2 * g) * 2048 + blk * 512, 512, 0)
            trickle()
            for j in range(4):
                s = ki % 2
                ki += 1
                for tb in range(4):
                    ps, bps = proj_fm(kb, w[:, :, j * 128:(j + 1) * 128], bw, hT, b_hT, tb)
                    kb.copy(kb.evac_eng(), kst[s][:, tb * 512:(tb + 1) * 512], ps[:, :], [bps], [b_kst[s]])
                r0 = (blk * 4 + j) * 128
                fw.dma(fw.sp, KT[g][r0:r0 + 128, :], kst[s][:], reads=[b_kst[s]], writes=[b_kt[g][blk]], more=True)
            w, bw = kb.wnext(16, 512)
            kb.load_w(w, bw, w_kv, (2 * g + 1) * 2048 + blk * 512, 512, 0)
            trickle()
            s = vi % 2
            vi += 1
            for c in range(NCH):
                ps, bps = proj_tm(kb, w, bw, 512, hT, b_hT, c)
                kb.copy(kb.evac_eng(), vst[s][:, c, :], ps[:, :], [bps], [b_vst[s]])
            fw.dma(fw.sp, V[g][:, blk * 512:(blk + 1) * 512].rearrange("(c p) f -> p c f", p=128), vst[s][:],
                   reads=[b_vst[s]], writes=[b_vt[g][blk]], more=True)
        pending.extend([(lambda d=d: step(d)) for d in range(4 * blk, 4 * blk + 4)])
    fw.barrier()
    fw.release(m0)
    return b_KG, b_VG, pending


def heads_B(kb, hT, b_hT, w_in, msk_in, KT, V, KG, VG, b_KG, b_VG, mixT):
    fw = kb.fw
    m0 = fw.mark()
    msk = fw.sbuf([128, 3, 512], BF16); b_msk = Buf()
    fw.dma(fw.sp, msk[:], msk_in, writes=[b_msk])
    qT = fw.sbuf([128, 3, NT], BF16); b_qT = Buf()
    gdT = [fw.sbuf([128, NT], BF16) for _ in range(2)]; b_gd = [Buf(), Buf()]
    KTt = [fw.sbuf([128, 128 * DILS[g] + NT], BF16) for g in range(3)]; b_K = [Buf() for _ in range(3)]
    Vt = [fw.sbuf([128, DILS[g], 16 // DILS[g] + 1, 128], BF16) for g in range(3)]; b_V = [Buf() for _ in range(3)]
    O32 = fw.sbuf([128, NT], F32); b_O = Buf()
    L32 = fw.sbuf([128, NT], F32); b_L = Buf()
    pT = [fw.sbuf([128, 512], BF16) for _ in range(2)]; b_pT = [Buf(), Buf()]
    mst = fw.sbuf([128, NT], BF16); b_mst = Buf()
    scale = 128.0 ** -0.5
    pi = 0

    def load_kv(d, g):
        hl = 128 * DILS[g]
        fw.dma(fw.sp, KTt[g][:, 0:hl], KG[d, 0:128, HOFF[g]:HOFF[g] + hl], reads=[b_KG[d]], writes=[b_K[g]])
        fw.dma(fw.sp, KTt[g][:, hl:hl + NT], KT[g][d * 128:(d + 1) * 128, :], writes=[b_K[g]], more=True)
        fw.dma(fw.sp, Vt[g][:, :, 0, :], VG[d, 0:128, HOFF[g]:HOFF[g] + hl].rearrange("j (r c) -> j r c", c=128),
               reads=[b_VG[d]], writes=[b_V[g]])
        for r in range(DILS[g]):
            src_o = V[g][:, d * 128:(d + 1) * 128].rearrange("(b j r) c -> j r b c", j=128, r=DILS[g])[:, r, :, :]
            fw.dma(fw.sp, Vt[g][:, r, 1:, :], src_o, writes=[b_V[g]], more=True)

    for d in range(16):
        w, bw = kb.wnext(16, 512)
        for g in range(3):
            kb.load_w(w, bw, w_in, g * 2048 + d * 128, 128, g * 128, more=(g > 0))
        kb.load_w(w, bw, w_in, 6144 + d * 128, 128, 384, more=True)
        if d == 0:
            for g in range(3):
                load_kv(0, g)
        for tb in range(4):
            tsl = slice(tb * 512, (tb + 1) * 512)
            for s in range(4):
                ps, bps = proj_fm(kb, w[:, :, s * 128:(s + 1) * 128], bw, hT, b_hT, tb)
                if s < 3:
                    fw.op(fw.act, lambda e: e.copy(out=qT[:, s, tsl], in_=ps[:, :]), reads=[bps], writes=[b_qT])
                else:
                    fw.op(fw.act, lambda e: e.activation(out=gdT[d % 2][:, tsl], in_=ps[:, :], func=AF.Silu), reads=[bps], writes=[b_gd[d % 2]])
        items = []
        for g in range(3):
            dil = DILS[g]
            nb = 16 // dil
            blocks = [(r, b) for r in range(dil) for b in range(nb)]
            for p0 in range(0, 16, 2):
                items.append((g, blocks[p0:p0 + 2]))
        state = {}

        def emit_S(i):
            g, pair = items[i]
            dil = DILS[g]
            qv = qT[:, g, :].rearrange("p (b j r) -> p r b j", j=128, r=dil)
            kv = KTt[g][:, :].rearrange("p (b j r) -> p r b j", j=128, r=dil)
            u = i % 2
            ps_s, bs = kb.bank()
            var = (1 if pair[0][1] == 0 else 0) + (1 if pair[1][1] == 0 else 0)

            def f(e):
                ins = e.matmul(ps_s[:, :], lhsT=kb.ident, rhs=msk[:, var, :], start=True, stop=False)
                n = 0
                for qi, (r, b) in enumerate(pair):
                    for sl, kb_ in ((2 * qi, b + 1), (2 * qi + 1, b)):
                        n += 1
                        ins = e.matmul(ps_s[:, sl * 128:(sl + 1) * 128], lhsT=kv[:, r, kb_, :], rhs=qv[:, r, b, :],
                                       start=False, stop=(n == 4))
                return ins
            fw.op(fw.pe, f, reads=[b_K[g], b_qT, b_msk, kb.b_cst], writes=[bs])
            fw.op(fw.act, lambda e: e.activation(out=pT[u][:], in_=ps_s[:, :], func=AF.Exp, scale=scale), reads=[bs], writes=[b_pT[u]])

        def emit_PV(i):
            g, pair = items[i]
            dil = DILS[g]
            Ov = O32[:, :].rearrange("p (b j r) -> p r b j", j=128, r=dil)
            Lv = L32[:, :].rearrange("p (b j r) -> p r b j", j=128, r=dil)
            u = i % 2
            ps_o, bo = kb.bank()
            ps_l, bl = kb.bank()
            for qi, (r, b) in enumerate(pair):
                kb.mm(ps_o[:, qi * 128:(qi + 1) * 128],
                      [(Vt[g][:, r, b + 1, :], pT[u][:, (2 * qi) * 128:(2 * qi + 1) * 128]),
                       (Vt[g][:, r, b, :], pT[u][:, (2 * qi + 1) * 128:(2 * qi + 2) * 128])], [b_V[g], b_pT[u]], bo)
                kb.mm(ps_l[:, qi * 128:(qi + 1) * 128],
                      [(kb.ones, pT[u][:, (2 * qi) * 128:(2 * qi + 1) * 128]),
                       (kb.ones, pT[u][:, (2 * qi + 1) * 128:(2 * qi + 2) * 128])], [kb.b_cst, b_pT[u]], bl)
            (r0, b0), (r1, b1) = pair
            if r0 == r1:
                oo = Ov[:, r0, b0:b0 + 2, :]
                ll = Lv[:, r0, b0:b0 + 2, :]
            else:
                oo = Ov[:, r0:r0 + 2, b0, :]
                ll = Lv[:, r0:r0 + 2, b0, :]
            pso = ps_o[:, 0:256].rearrange("p (a j) -> p a j", a=2)
            psl = ps_l[:, 0:256].rearrange("p (a j) -> p a j", a=2)
            if g == 0:
                fw.op(fw.act, lambda e: e.copy(out=oo, in_=pso), reads=[bo], writes=[b_O])
                fw.op(fw.act, lambda e: e.copy(out=ll, in_=psl), reads=[bl], writes=[b_L])
            else:
                fw.op(fw.dve, lambda e: e.tensor_tensor(out=oo, in0=oo, in1=pso, op=ALU.add), reads=[bo, b_O], writes=[b_O])
                fw.op(fw.dve, lambda e: e.tensor_tensor(out=ll, in0=ll, in1=psl, op=ALU.add), reads=[bl, b_L], writes=[b_L])
            if i % 8 == 7 and d + 1 < 16:
                load_kv(d + 1, g)

        for i in range(len(items) + 1):
            if i < len(items):
                emit_S(i)
            if i >= 1:
                emit_PV(i - 1)
        fw.op(fw.dve, lambda e: e.reciprocal(out=L32[:], in_=L32[:]), reads=[b_L], writes=[b_L])
        fw.op(fw.dve, lambda e: e.tensor_tensor(out=O32[:], in0=O32[:], in1=L32[:], op=ALU.mult), reads=[b_L, b_O], writes=[b_O])
        fw.op(fw.dve, lambda e: e.tensor_tensor(out=mst[:], in0=O32[:], in1=gdT[d % 2][:], op=ALU.mult), reads=[b_O, b_gd[d % 2]], writes=[b_mst])
        fw.dma(fw.sp, mixT[d * 128:(d + 1) * 128, :], mst[:], reads=[b_mst])
    fw.barrier()
    fw.release(m0)


def build_fused(gC, stages=5):
    nc = bass.Bass("TRN2", target_bir_lowering=False)
    dt = nc.dram_tensor

    def inp(name, shape, d=F32):
        return dt(name, shape, d, kind="ExternalInput").ap()
    x = inp("x", [NT, D])
    mem = inp("mem", [256, D])
    norm_a = inp("norm_a", [2, D])
    w_in_a = inp("w_in_a", [2, D, 8192])
    w_out_a = inp("w_out_a", [2, 3072, D])
    norm_b = inp("norm_b", [2, D])
    w_in_b = inp("w_in_b", [2, D, 10240])
    w_out_b = inp("w_out_b", [2, 3072, D])
    w_mem_kv = inp("w_mem_kv", [4, D, 2048])
    g_m = inp("g_m", [1, D])
    g_kv = inp("g_kv", [1, D])
    w_kv = inp("w_kv", [D, 12288])
    g_f = inp("g_f", [1, D])
    cs_in = inp("cs", [128, 2, NT])
    dm_in = inp("dm", [128, 17, 128])
    cst_in = inp("cst", [128, 3, 128], BF16)
    msk_in = inp("msk", [128, 3, 512], BF16)
    y = dt("y", [NT, D], F32, kind="ExternalOutput").ap()
    xp = [dt(f"xp{i}", [NT, D], F32).ap() for i in range(2)]
    mixT = dt("mixT", [3072, NT], BF16).ap()
    cc_src = [dt(f"ccs{l}", [8, 128, 256], F32).ap() for l in range(2)]
    cc_dst = [dt(f"ccd{l}", [8, 256, 256], F32).ap() for l in range(2)]
    KT = [dt(f"KT{g}", [D, NT], BF16).ap() for g in range(3)]
    V = [dt(f"V{g}", [NT, D], BF16).ap() for g in range(3)]
    KVS = dt("KVS", [16, 128, 2 * HSUM], BF16).ap()
    KVG = dt("KVG", [16, 256, 2 * HSUM], BF16).ap()
    KS = KVS[:, :, 0:HSUM]
    VS = KVS[:, :, HSUM:2 * HSUM]
    KG = KVG[:, :, 0:HSUM]
    VG = KVG[:, :, HSUM:2 * HSUM]

    kb = KB(nc, cst_in)
    fw = kb.fw
    out_toks = []
    hT = fw.sbuf([128, 16, NT], BF16)
    b_hT = [Buf() for _ in range(NCH)]

    xin = x
    for l in range(min(2, stages)):
        xout = xp[l]
        phase_norm(kb, xin, norm_a[l:l + 1, :], hT, b_hT, NCH)
        heads_A(kb, hT, b_hT, w_in_a[l], cs_in, dm_in, gC, mixT, cc_src[l], cc_dst[l])
        phase_mem(kb, mem, g_m, w_mem_kv[l], w_in_a[l], 6144, 7168, hT, b_hT, mixT)
        phase_out(kb, mixT, w_out_a[l], xin, xout, hT)
        xin = xout
    if stages >= 3:
        phase_norm(kb, xin, g_kv, hT, b_hT, NCH)
        b_KG, b_VG, xsteps = phase_kv(kb, hT, b_hT, w_kv, KT, V, KS, VS, KVS, KVG)
    for l in range(max(0, stages - 3)):
        xout = xp[l]
        hooks = None
        if l == 0:
            nx = len(xsteps)
            hooks = {m: xsteps[m * nx // 4:(m + 1) * nx // 4] for m in range(4)}
        phase_norm(kb, xin, norm_b[l:l + 1, :], hT, b_hT, NCH)
        phase_mem(kb, mem, g_m, w_mem_kv[2 + l], w_in_b[l], 8192, 9216, hT, b_hT, mixT, hooks)
        heads_B(kb, hT, b_hT, w_in_b[l], msk_in, KT, V, KG, VG, b_KG, b_VG, mixT)
        phase_out(kb, mixT, w_out_b[l], xin, xout, hT)
        xin = xout
    phase_norm(kb, xin, g_f, None, None, NCH, out_ap=y, out_toks=out_toks)
    finish(kb, out_toks)
    return nc


def _consts():
    bf = ml_dtypes.bfloat16
    cst = np.zeros((128, 3, 128), np.float32)
    cst[:, 0, :] = np.eye(128)
    cst[:, 1, :] = 1.0
    pm = np.zeros((128, 128), np.float32)
    for i in range(128):
        pm[(i + 64) % 128, i] = 1.0
    cst[:, 2, :] = pm
    half = 64
    inv = (1.0 / (np.float32(10000.0) ** np.linspace(0.0, 1.0, half, dtype=np.float32))).astype(np.float32)
    cs = []
    for hf in range(2):
        pos = (np.arange(NT, dtype=np.float32) + np.float32(hf * NT))
        ang = (pos[:, None] * inv[None, :]).astype(np.float32)
        c = np.cos(ang).astype(np.float32).T
        s = np.sin(ang).astype(np.float32).T
        t = np.zeros((128, 2, NT), np.float32)
        t[:64, 0] = c; t[64:, 0] = c
        t[:64, 1] = -s; t[64:, 1] = s
        cs.append(t)
    hh = np.arange(8, dtype=np.float64)
    log_g = np.log1p(-(2.0 ** (-5.0 - hh)))
    pos = np.arange(128, dtype=np.float64)
    scale = 128.0 ** -0.5
    dms = []
    for hf in range(2):
        dm = np.zeros((128, 17, 128), np.float32)
        for h in range(8):
            kk = pos[:, None]
            qq = pos[None, :]
            dm[:, h, :] = np.where(qq >= kk, scale * np.exp(-(kk + 1.0) * log_g[h]), 0.0)
            dm[:, 8 + h, :] = np.exp((pos + 1.0) * log_g[h])[None, :]
            dm[:, 16, h] = scale * np.exp((127.0 - pos) * log_g[h])
        dm[:, 16, 8] = float(hf)
        dms.append(dm)
    gC = np.exp(128.0 * log_g)
    kk = np.arange(128)[:, None]
    qq = np.arange(128)[None, :]
    NEG = -1.0e4
    cur = np.where(kk <= qq, 0.0, NEG).astype(np.float32)
    prev = np.where(kk >= qq, 0.0, NEG).astype(np.float32)
    dead = np.full((128, 128), NEG, np.float32)
    msk = []
    for hf in range(2):
        ph = prev if hf == 1 else dead
        m = np.stack([np.concatenate([cur, prev, cur, prev], axis=1),
                      np.concatenate([cur, ph, cur, prev], axis=1),
                      np.concatenate([cur, ph, cur, ph], axis=1)], axis=1)
        msk.append(m.astype(bf))
    return cst.astype(bf), cs, dms, gC, msk


def kernel(x, mem, norm_a, w_in_a, w_out_a, norm_b, w_in_b, w_out_b, w_mem_kv, mem_norm_g, kv_norm_g, w_kv, final_norm_g):
    f32 = np.float32
    x = np.asarray(x, f32); mem = np.asarray(mem, f32)
    shared = {
        "norm_a": np.asarray(norm_a, f32), "w_in_a": np.asarray(w_in_a, f32), "w_out_a": np.asarray(w_out_a, f32),
        "norm_b": np.asarray(norm_b, f32), "w_in_b": np.asarray(w_in_b, f32), "w_out_b": np.asarray(w_out_b, f32),
        "w_mem_kv": np.asarray(w_mem_kv, f32), "g_m": np.asarray(mem_norm_g, f32).reshape(1, D),
        "g_kv": np.asarray(kv_norm_g, f32).reshape(1, D), "w_kv": np.asarray(w_kv, f32),
        "g_f": np.asarray(final_norm_g, f32).reshape(1, D),
    }
    cst, cs, dms, gC, msk = _consts()
    B = x.shape[0]
    nc = build_fused(gC)
    maps = []
    for c in range(2 * B):
        b, hf = c // 2, c % 2
        maps.append(dict(shared, x=np.ascontiguousarray(x[b, hf * NT:(hf + 1) * NT]), mem=mem[b],
                         cs=cs[hf], dm=dms[hf], cst=cst, msk=msk[hf]))
    res = run_bass_kernel_spmd(nc, maps, core_ids=list(range(2 * B))).results
    out = np.stack([np.concatenate([res[2 * b + hf]["y"] for hf in range(2)], axis=0) for b in range(B)], axis=0)
    return out.astype(f32)
```

```python
import math
import numpy as np
import ml_dtypes
import concourse.bass as bass
import concourse.mybir as mybir
from concourse.bass_utils import run_bass_kernel_spmd

F32 = mybir.dt.float32
BF16 = mybir.dt.bfloat16
AF = mybir.ActivationFunctionType
ALU = mybir.AluOpType

D = 2048
NT = 2048
NCH = NT // 128
EPS = 1e-6
SEM_LIMIT = 8000
DILS = (1, 4, 16)
HOFF = (0, 128, 640)
HSUM = 2688


class Holder:
    def __init__(self, fw, name):
        self.fw = fw
        self.name = name
        self.gen = 0
        self.sem = fw.new_sem(f"{name}_{self.gen}")
        self.count = 0

    def bump(self, n):
        self.count += n
        return (self.sem, self.count)

    def maybe_rotate(self, n=1):
        if self.count + n > SEM_LIMIT:
            self.gen += 1
            self.sem = self.fw.new_sem(f"{self.name}_{self.gen}")
            self.count = 0


class Buf:
    __slots__ = ("w", "r")

    def __init__(self):
        self.w = []
        self.r = []


class Eng:
    def __init__(self, fw, e, name, self_sync):
        self.e = e
        self.name = name
        self.h = Holder(fw, "s_" + name)
        self.seen = {}
        self.self_sync = self_sync

    def wait(self, tok, same_ok=False):
        if tok is None:
            return
        sem, val = tok
        if sem is self.h.sem and (same_ok or not self.self_sync):
            return
        if self.seen.get(sem.num, 0) >= val:
            return
        self.e.wait_ge(sem, val)
        self.seen[sem.num] = val


class FW:
    def __init__(self, nc):
        self.nc = nc
        self._ctx = []
        self.pe = Eng(self, nc.tensor, "pe", False)
        self.act = Eng(self, nc.scalar, "act", True)
        self.dve = Eng(self, nc.vector, "dve", True)
        self.pool = Eng(self, nc.gpsimd, "pool", False)
        self.sp = Eng(self, nc.sync, "sp", False)
        self.lanes_q = {id(self.sp): [Holder(self, f"dls{i}") for i in range(16)],
                        id(self.pool): [Holder(self, f"dlp{i}") for i in range(8)]}
        self.lane_i = {id(self.sp): 0, id(self.pool): 0}
        self.lanes = self.lanes_q[id(self.sp)] + self.lanes_q[id(self.pool)]
        self.cch = Holder(self, "cc")
        self._n = 0

    def new_sem(self, name):
        cm = self.nc.semaphore(name)
        s = cm.__enter__()
        self._ctx.append(cm)
        return s

    def sbuf(self, shape, dt):
        self._n += 1
        cm = self.nc.sbuf_tensor(f"sb{self._n}", list(shape), dt)
        t = cm.__enter__()
        self._ctx.append(cm)
        return t

    def psum(self, shape, dt):
        self._n += 1
        cm = self.nc.psum_tensor(f"ps{self._n}", list(shape), dt)
        t = cm.__enter__()
        self._ctx.append(cm)
        return t

    def mark(self):
        return len(self._ctx)

    def release(self, mark):
        while len(self._ctx) > mark:
            self._ctx.pop().__exit__(None, None, None)

    def close(self):
        self.release(0)

    def op(self, eng, fn, reads=(), writes=()):
        for b in reads:
            for t in b.w:
                eng.wait(t)
        for b in writes:
            for t in b.w:
                eng.wait(t, same_ok=True)
            for t in b.r:
                eng.wait(t, same_ok=True)
        eng.h.maybe_rotate(1)
        ins = fn(eng.e)
        tok = eng.h.bump(1)
        ins.then_inc(tok[0], 1)
        for b in reads:
            b.r.append(tok)
        for b in writes:
            b.w = [tok]
            b.r = []
        return tok

    def dma(self, q, out, in_, reads=(), writes=(), more=False):
        ql = self.lanes_q[id(q)]
        lane = ql[self.lane_i[id(q)]]
        self.lane_i[id(q)] = (self.lane_i[id(q)] + 1) % len(ql)
        if lane.count:
            q.wait((lane.sem, lane.count))
        lane.maybe_rotate(16)
        for b in reads:
            for t in b.w:
                q.wait(t)
        for b in writes:
            if not more:
                for t in b.w:
                    q.wait(t)
            for t in b.r:
                q.wait(t)
        ins = q.e.dma_start(out=out, in_=in_)
        tok = lane.bump(16)
        ins.then_inc(tok[0], 16)
        for b in reads:
            b.r.append(tok)
        for b in writes:
            if more:
                b.w.append(tok)
            else:
                b.w = [tok]
            b.r = []
        return tok

    def allgather(self, in_ap, out_ap, reads, writes):
        q = self.pool
        for b in reads:
            for t in b.w:
                q.wait(t)
        for b in writes:
            for t in b.w + b.r:
                q.wait(t)
        if self.cch.count:
            q.wait((self.cch.sem, self.cch.count))
        self.cch.maybe_rotate(1)
        ins = q.e.collective_compute("AllGather", ALU.bypass, replica_groups=[[0, 1], [2, 3], [4, 5], [6, 7]],
                                     ins=[in_ap], outs=[out_ap])
        tok = self.cch.bump(1)
        ins.then_inc(tok[0], 1)
        for b in reads:
            b.r.append(tok)
        for b in writes:
            b.w = [tok]
            b.r = []
        return tok

    def barrier(self):
        engs = [self.pe, self.act, self.dve, self.sp, self.pool]
        toks = [(e.h.sem, e.h.count) for e in engs if e.h.count]
        toks += [(l.sem, l.count) for l in self.lanes if l.count]
        for e in engs:
            for t in toks:
                e.wait(t, same_ok=True)


class KB:
    def __init__(self, nc, cst_ap):
        self.nc = nc
        self.fw = FW(nc)
        fw = self.fw
        self.banks = [fw.psum([128, 512], F32) for _ in range(8)]
        self.bbufs = [Buf() for _ in range(8)]
        self.bi = 0
        self.cst = fw.sbuf([128, 3, 128], BF16)
        self.b_cst = Buf()
        fw.dma(fw.sp, self.cst[:], cst_ap, writes=[self.b_cst])
        self.ident = self.cst[:, 0, :]
        self.ones = self.cst[:, 1, :]
        self.pmat = self.cst[:, 2, :]
        self.wb = [fw.sbuf([128, 12288], BF16) for _ in range(2)]
        self.b_wb = [Buf(), Buf()]
        self.wi = 0
        self.ev = 0

    def bank(self):
        i = self.bi
        self.bi = (self.bi + 1) % 8
        return self.banks[i], self.bbufs[i]

    def wnext(self, K, ncols):
        i = self.wi
        self.wi = 1 - self.wi
        view = self.wb[i][:, 0:K * ncols].rearrange("p (k c) -> p k c", k=K)
        return view, self.b_wb[i]

    def load_w(self, wview, bw, w2d, col0, ncols, dst0, more=False):
        src = w2d[:, col0:col0 + ncols].rearrange("(k p) c -> p k c", p=128)
        self.fw.dma(self.fw.pool, wview[:, :, dst0:dst0 + ncols], src, writes=[bw], more=more)

    def mm(self, out_ap, pairs, reads, bbuf):
        n = len(pairs)

        def f(e):
            ins = None
            for i, (l, r) in enumerate(pairs):
                ins = e.matmul(out_ap, lhsT=l, rhs=r, start=(i == 0), stop=(i == n - 1))
            return ins
        return self.fw.op(self.fw.pe, f, reads=reads, writes=[bbuf])

    def evac_eng(self):
        self.ev ^= 1
        return self.fw.act if self.ev else self.fw.dve

    def copy(self, eng, out, in_, reads, writes):
        if eng is self.fw.act:
            return self.fw.op(eng, lambda e: e.copy(out=out, in_=in_), reads=reads, writes=writes)
        return self.fw.op(eng, lambda e: e.tensor_copy(out=out, in_=in_), reads=reads, writes=writes)


def phase_norm(kb, x_ap, g_ap, hT, b_hT, nchunks, out_ap=None, out_toks=None):
    fw = kb.fw
    m = fw.mark()
    gB = fw.sbuf([128, D], F32); b_gB = Buf()
    fw.dma(fw.sp, gB[:], g_ap.broadcast_to([128, D]), writes=[b_gB])
    NX = 3
    xt = [fw.sbuf([128, D], F32) for _ in range(NX)]; b_xt = [Buf() for _ in range(NX)]
    odt = BF16 if hT is not None else F32
    hb = [fw.sbuf([128, D], odt) for _ in range(2)]; b_hb = [Buf(), Buf()]
    junk = fw.sbuf([128, D], BF16); b_junk = Buf()
    st = fw.sbuf([128, nchunks, 4], F32); b_st = [Buf() for _ in range(nchunks)]

    def load(c):
        i = c % NX
        fw.dma(fw.sp, xt[i][:], x_ap[c * 128:(c + 1) * 128, :], writes=[b_xt[i]])

    def stage1(c):
        i = c % NX
        j = c % 2
        fw.op(fw.act, lambda e: e.activation(out=junk[:], in_=xt[i][:], func=AF.Square, accum_out=st[:, c, 0:1]),
              reads=[b_xt[i]], writes=[b_junk, b_st[c]])
        fw.op(fw.act, lambda e: e.activation(out=st[:, c, 1:2], in_=st[:, c, 0:1], func=AF.Sqrt, scale=1.0 / D, bias=EPS),
              reads=[b_st[c]], writes=[b_st[c]])
        fw.op(fw.dve, lambda e: e.reciprocal(out=st[:, c, 2:3], in_=st[:, c, 1:2]), reads=[b_st[c]], writes=[b_st[c]])
        fw.op(fw.dve, lambda e: e.scalar_tensor_tensor(out=hb[j][:], in0=xt[i][:], scalar=st[:, c, 2:3], in1=gB[:],
                                                       op0=ALU.mult, op1=ALU.mult),
              reads=[b_xt[i], b_st[c], b_gB], writes=[b_hb[j]])

    def stage2(c):
        j = c % 2
        if hT is not None:
            eng = kb.evac_eng()
            for b4 in range(4):
                ps, bps = kb.bank()
                for jj in range(4):
                    kk = 4 * b4 + jj
                    kb.mm(ps[:, jj * 128:(jj + 1) * 128], [(hb[j][:, kk * 128:(kk + 1) * 128], kb.ident)],
                          [b_hb[j], kb.b_cst], bps)
                kb.copy(eng, hT[:, 4 * b4:4 * b4 + 4, c * 128:(c + 1) * 128],
                        ps[:].rearrange("p (j t) -> p j t", j=4), [bps], [b_hT[c]])
        else:
            t = fw.dma(fw.sp, out_ap[c * 128:(c + 1) * 128, :], hb[j][:], reads=[b_hb[j]])
            out_toks.append(t)

    for c in range(min(NX, nchunks)):
        load(c)
    stage1(0)
    for c in range(nchunks):
        if c + 1 < nchunks:
            stage1(c + 1)
        if c + NX < nchunks:
            load(c + NX)
        stage2(c)
    fw.barrier()
    fw.release(m)


def proj_fm(kb, wcols, bw, hT, b_hT, tb):
    ps, bps = kb.bank()
    pairs = [(wcols[:, k, :], hT[:, k, tb * 512:(tb + 1) * 512]) for k in range(16)]
    kb.mm(ps[:, :], pairs, [bw] + b_hT[4 * tb:4 * tb + 4], bps)
    return ps, bps


def proj_tm(kb, w, bw, ncols, hT, b_hT, c):
    ps, bps = kb.bank()
    pairs = [(hT[:, k, c * 128:(c + 1) * 128], w[:, k, :]) for k in range(16)]
    kb.mm(ps[:, 0:ncols], pairs, [bw, b_hT[c]], bps)
    return ps, bps


def phase_mem(kb, mem_ap, gmem_ap, wmkv, w_in, cq, cg, hT, b_hT, mixT, hooks=None):
    fw = kb.fw
    m0 = fw.mark()
    memT = fw.sbuf([128, 16, 256], BF16); b_memT = [Buf(), Buf()]
    phase_norm(kb, mem_ap, gmem_ap, memT, b_memT, 2)
    mkT = fw.sbuf([128, 8, 256], BF16); b_mkT = Buf()
    mvt = fw.sbuf([128, 2, 1024], BF16); b_mvt = Buf()
    for blk in range(2):
        w, bw = kb.wnext(16, 512)
        kb.load_w(w, bw, wmkv, blk * 512, 512, 0)
        for j in range(4):
            ps, bps = kb.bank()
            kb.mm(ps[:, 0:256], [(w[:, k, j * 128:(j + 1) * 128], memT[:, k, :]) for k in range(16)], [bw] + b_memT, bps)
            kb.copy(kb.evac_eng(), mkT[:, blk * 4 + j, :], ps[:, 0:256], [bps], [b_mkT])
    for blk in range(2):
        w, bw = kb.wnext(16, 512)
        kb.load_w(w, bw, wmkv, 1024 + blk * 512, 512, 0)
        for jc in range(2):
            ps, bps = kb.bank()
            kb.mm(ps[:, :], [(memT[:, k, jc * 128:(jc + 1) * 128], w[:, k, :]) for k in range(16)], [bw] + b_memT, bps)
            kb.copy(kb.evac_eng(), mvt[:, jc, blk * 512:(blk + 1) * 512], ps[:, :], [bps], [b_mvt])
    mqT = fw.sbuf([128, 2, NT], BF16); b_mq = Buf()
    gmT = fw.sbuf([128, 2, NT], BF16); b_gm = Buf()
    pT = fw.sbuf([128, 2, 512], BF16); b_pT = Buf()
    rl = fw.sbuf([128, 512], F32); b_rl = Buf()
    tmp = fw.sbuf([128, 512], F32); b_tmp = Buf()
    mst = fw.sbuf([128, 2, NT], BF16); b_mst = Buf()
    for m in range(4):
        w, bw = kb.wnext(16, 512)
        kb.load_w(w, bw, w_in, cq + m * 256, 256, 0)
        kb.load_w(w, bw, w_in, cg + m * 256, 256, 256, more=True)
        if hooks:
            for hk in hooks[m]:
                hk()
        for tb in range(4):
            for i in range(2):
                ps, bps = proj_fm(kb, w[:, :, i * 128:(i + 1) * 128], bw, hT, b_hT, tb)
                fw.op(fw.act, lambda e: e.copy(out=mqT[:, i, tb * 512:(tb + 1) * 512], in_=ps[:, :]), reads=[bps], writes=[b_mq])
            for i in range(2):
                ps, bps = proj_fm(kb, w[:, :, 256 + i * 128:256 + (i + 1) * 128], bw, hT, b_hT, tb)
                fw.op(fw.act, lambda e: e.activation(out=gmT[:, i, tb * 512:(tb + 1) * 512], in_=ps[:, :], func=AF.Silu),
                      reads=[bps], writes=[b_gm])
        for tb in range(4):
            tsl = slice(tb * 512, (tb + 1) * 512)
            for j in range(2):
                ps, bps = kb.bank()
                kb.mm(ps[:, :], [(mkT[:, 2 * m + i, j * 128:(j + 1) * 128], mqT[:, i, tsl]) for i in range(2)],
                      [b_mkT, b_mq], bps)
                fw.op(fw.act, lambda e: e.activation(out=pT[:, j, :], in_=ps[:, :], func=AF.Exp, scale=1.0 / 16.0),
                      reads=[bps], writes=[b_pT])
            pl, bpl = kb.bank()
            kb.mm(pl[:, :], [(kb.ones, pT[:, j, :]) for j in range(2)], [kb.b_cst, b_pT], bpl)
            fw.op(fw.dve, lambda e: e.reciprocal(out=rl[:], in_=pl[:, :]), reads=[bpl], writes=[b_rl])
            for i in range(2):
                po, bpo = kb.bank()
                kb.mm(po[:, :], [(mvt[:, j, m * 256 + i * 128:m * 256 + (i + 1) * 128], pT[:, j, :]) for j in range(2)],
                      [b_mvt, b_pT], bpo)
                fw.op(fw.dve, lambda e: e.tensor_tensor(out=tmp[:], in0=po[:, :], in1=rl[:], op=ALU.mult),
                      reads=[bpo, b_rl], writes=[b_tmp])
                fw.op(fw.dve, lambda e: e.tensor_tensor(out=mst[:, i, tsl], in0=tmp[:], in1=gmT[:, i, tsl], op=ALU.mult),
                      reads=[b_tmp, b_gm], writes=[b_mst])
        r0 = 2048 + m * 256
        fw.dma(fw.sp, mixT[r0:r0 + 256, :].rearrange("(j p) t -> p j t", p=128), mst[:], reads=[b_mst])
    fw.barrier()
    fw.release(m0)


def phase_out(kb, mixT, w_out, x_in, x_out, hT, out_toks=None):
    fw = kb.fw
    m0 = fw.mark()
    mx = fw.sbuf([128, 8, NT], BF16)
    b_mr = [Buf() for _ in range(6)]
    for i in range(6):
        dst = hT[:, 4 * i:4 * i + 4, :] if i < 4 else mx[:, 4 * (i - 4):4 * (i - 4) + 4, :]
        fw.dma(fw.sp, dst, mixT[i * 512:(i + 1) * 512, :].rearrange("(k p) t -> p k t", p=128), writes=[b_mr[i]])

    def mk(k, c):
        return hT[:, k, c * 128:(c + 1) * 128] if k < 16 else mx[:, k - 16, c * 128:(c + 1) * 128]
    xo = [fw.sbuf([128, 512], F32) for _ in range(8)]; b_xo = [Buf() for _ in range(8)]
    xi = 0

    def epilogue(ps, bps, n, c, xb):
        fw.op(fw.dve, lambda e: e.tensor_tensor(out=xo[xb][:], in0=ps[:, :], in1=xo[xb][:], op=ALU.add),
              reads=[bps, b_xo[xb]], writes=[b_xo[xb]])
        t = fw.dma(fw.sp, x_out[c * 128:(c + 1) * 128, n * 512:(n + 1) * 512], xo[xb][:], reads=[b_xo[xb]])
        if out_toks is not None:
            out_toks.append(t)

    for n in range(4):
        w, bw = kb.wnext(24, 512)
        kb.load_w(w, bw, w_out, n * 512, 512, 0)
        c_start = 0
        if n == 0:
            c_start = 8
            b8 = [kb.bank() for _ in range(8)]
            for c in range(8):
                fw.dma(fw.sp, xo[c][:], x_in[c * 128:(c + 1) * 128, 0:512], writes=[b_xo[c]])
            for i in range(6):
                for c in range(8):
                    ps, bps = b8[c]

                    def f(e, i=i, c=c, ps=ps):
                        ins = None
                        for kk in range(4):
                            k = 4 * i + kk
                            ins = e.matmul(ps[:, :], lhsT=mk(k, c), rhs=w[:, k, :], start=(k == 0), stop=(k == 23))
                        return ins
                    fw.op(fw.pe, f, reads=[b_mr[i], bw], writes=[bps])
            for c in range(8):
                epilogue(b8[c][0], b8[c][1], 0, c, c)
            xi = 8
        for c in range(c_start, NCH):
            xb = xi % 8
            xi += 1
            fw.dma(fw.sp, xo[xb][:], x_in[c * 128:(c + 1) * 128, n * 512:(n + 1) * 512], writes=[b_xo[xb]])
            ps, bps = kb.bank()
            kb.mm(ps[:, :], [(mk(k, c), w[:, k, :]) for k in range(24)], b_mr + [bw], bps)
            epilogue(ps, bps, n, c, xb)
    fw.barrier()
    fw.release(m0)


def finish(kb, toks):
    fw = kb.fw
    fw.barrier()
    for t in toks:
        fw.sp.wait(t)
    fw.close()


def heads_A(kb, hT, b_hT, w_in, cs_in, dm_in, gC, mixT, cc_src, cc_dst):
    fw = kb.fw
    m0 = fw.mark()
    cs = fw.sbuf([128, 2, NT], F32); b_cs = Buf()
    fw.dma(fw.sp, cs[:], cs_in, writes=[b_cs])
    dm = fw.sbuf([128, 17, 128], F32); b_dm = Buf()
    fw.dma(fw.sp, dm[:], dm_in, writes=[b_dm])
    qT = fw.sbuf([128, NT], BF16); b_qT = Buf()
    kT = fw.sbuf([128, NT], BF16); b_kT = Buf()
    kz = fw.sbuf([128, NCH, 128], BF16); b_kz = Buf()
    vtm = fw.sbuf([128, NCH, 256], BF16); b_vc = [Buf() for _ in range(NCH)]
    gs = fw.sbuf([128, NCH, 256], BF16); b_gc = [Buf() for _ in range(NCH)]
    raw = [fw.sbuf([128, 512], BF16) for _ in range(2)]; b_raw = [Buf(), Buf()]
    t1 = [fw.sbuf([128, 512], F32) for _ in range(2)]; b_t1 = [Buf(), Buf()]
    t2 = [fw.sbuf([128, 512], F32) for _ in range(2)]; b_t2 = [Buf(), Buf()]
    mth = fw.sbuf([128, 2, NT], BF16); b_mth = Buf()
    R32 = [fw.sbuf([128, 256], F32) for _ in range(2)]; b_R32 = [Buf(), Buf()]
    Rbf = [fw.sbuf([128, 256], BF16) for _ in range(4)]; b_Rbf = [Buf() for _ in range(4)]
    Rg = fw.sbuf([128, 256], F32); b_Rg = Buf()
    sTm = [fw.sbuf([128, 128], BF16) for _ in range(4)]; b_sTm = [Buf() for _ in range(4)]
    mo = [fw.sbuf([128, 256], BF16) for _ in range(4)]; b_mo = [Buf() for _ in range(4)]
    junk = fw.sbuf([128, 256], BF16); b_junk = Buf()
    st = fw.sbuf([128, NCH, 4], F32); b_st = [Buf() for _ in range(NCH)]

    def qk_proj(h, which, w1, bw1):
        for tb in range(4):
            tsl = slice(tb * 512, (tb + 1) * 512)
            ps, bps = proj_fm(kb, w1[:, :, which * 128:(which + 1) * 128], bw1, hT, b_hT, tb)
            fw.op(fw.act, lambda e: e.copy(out=raw[which][:], in_=ps[:, :]), reads=[bps], writes=[b_raw[which]])
            p2, bp2 = kb.bank()
            kb.mm(p2[:, :], [(kb.pmat, raw[which][:])], [kb.b_cst, b_raw[which]], bp2)
            fw.op(fw.dve, lambda e: e.tensor_tensor(out=t1[which][:], in0=raw[which][:], in1=cs[:, 0, tsl], op=ALU.mult),
                  reads=[b_raw[which], b_cs], writes=[b_t1[which]])
            fw.op(fw.dve, lambda e: e.tensor_tensor(out=t2[which][:], in0=p2[:, :], in1=cs[:, 1, tsl], op=ALU.mult),
                  reads=[bp2, b_cs], writes=[b_t2[which]])
            if which == 0:
                fw.op(fw.dve, lambda e: e.tensor_tensor(out=t1[0][:], in0=t1[0][:], in1=t2[0][:], op=ALU.add),
                      reads=[b_t1[0], b_t2[0]], writes=[b_t1[0]])
                xib = dm[:, 8 + h, :].unsqueeze(1).broadcast_to([128, 4, 128])
                fw.op(fw.dve, lambda e: e.tensor_tensor(out=qT[:, tsl].rearrange("p (c t) -> p c t", c=4),
                                                        in0=t1[0][:].rearrange("p (c t) -> p c t", c=4), in1=xib, op=ALU.mult),
                      reads=[b_t1[0], b_dm], writes=[b_qT])
            else:
                fw.op(fw.dve, lambda e: e.tensor_tensor(out=kT[:, tsl], in0=t1[1][:], in1=t2[1][:], op=ALU.add),
                      reads=[b_t1[1], b_t2[1]], writes=[b_kT])

    for h in range(8):
        w1, bw1 = kb.wnext(16, 256)
        kb.load_w(w1, bw1, w_in, 1024 + h * 128, 128, 128)
        kb.load_w(w1, bw1, w_in, h * 128, 128, 0, more=True)
        w2, bw2 = kb.wnext(16, 512)
        kb.load_w(w2, bw2, w_in, 2048 + h * 256, 256, 0)
        kb.load_w(w2, bw2, w_in, 4096 + h * 256, 256, 256, more=True)
        qk_proj(h, 1, w1, bw1)
        for c4 in range(4):
            ps, bps = kb.bank()
            for j in range(4):
                c = 4 * c4 + j
                kb.mm(ps[:, j * 128:(j + 1) * 128], [(kT[:, c * 128:(c + 1) * 128], kb.ident)], [b_kT, kb.b_cst], bps)
            fw.op(fw.dve, lambda e: e.tensor_scalar_mul(out=kz[:, 4 * c4:4 * c4 + 4, :],
                                                        in0=ps[:].rearrange("p (j t) -> p j t", j=4),
                                                        scalar1=dm[:, 16, h:h + 1]),
                  reads=[bps, b_dm], writes=[b_kz])
        fw.op(fw.dve, lambda e: e.memset(R32[0][:], 0.0), writes=[b_R32[0]])

        def p1(c):
            cur, nxt = c % 2, 1 - (c % 2)
            ps_u, bu = kb.bank()
            kb.mm(ps_u[:, 0:256], [(kz[:, c, :], vtm[:, c, :])], [b_kz, b_vc[c]], bu)
            fw.op(fw.dve, lambda e: e.scalar_tensor_tensor(out=R32[nxt][:], in0=R32[cur][:], scalar=float(gC[h]), in1=ps_u[:, 0:256],
                                                           op0=ALU.mult, op1=ALU.add),
                  reads=[b_R32[cur], bu], writes=[b_R32[nxt]])
        for c in range(NCH):
            ps, bps = proj_tm(kb, w2, bw2, 512, hT, b_hT, c)
            fw.op(fw.act, lambda e: e.copy(out=vtm[:, c, :], in_=ps[:, 0:256]), reads=[bps], writes=[b_vc[c]])
            fw.op(fw.act, lambda e: e.activation(out=gs[:, c, :], in_=ps[:, 256:512], func=AF.Silu), reads=[bps], writes=[b_gc[c]])
            if c >= 1:
                p1(c - 1)
        p1(NCH - 1)
        b_src, b_dst = Buf(), Buf()
        fw.dma(fw.sp, cc_src[h, :, :], R32[0][:], reads=[b_R32[0]], writes=[b_src])
        fw.allgather(cc_src[h, :, :], cc_dst[h, :, :], [b_src], [b_dst])
        qk_proj(h, 0, w1, bw1)
        fw.dma(fw.sp, Rg[:], cc_dst[h, 0:128, :], reads=[b_dst], writes=[b_Rg])
        fw.op(fw.dve, lambda e: e.tensor_scalar_mul(out=R32[0][:], in0=Rg[:], scalar1=dm[:, 16, 8:9]),
              reads=[b_Rg, b_dm], writes=[b_R32[0]])
        fw.op(fw.act, lambda e: e.copy(out=Rbf[0][:], in_=R32[0][:]), reads=[b_R32[0]], writes=[b_Rbf[0]])
        pbanks = [kb.bank() for _ in range(8)]
        o_sl = [(pbanks[i][0][:, 0:256], pbanks[i][1]) for i in (0, 1, 2)]
        u_sl = [(pbanks[i][0][:, 0:256], pbanks[i][1]) for i in (3, 4)]
        t_sl = [(pbanks[i][0][:, 0:256], pbanks[i][1]) for i in (5,)]
        s_sl = [(pbanks[i][0][:, 0:128], pbanks[i][1]) for i in (6, 7)]

        def S1(c):
            csl = slice(c * 128, (c + 1) * 128)
            ps_s, bs = s_sl[c % len(s_sl)]
            kb.mm(ps_s, [(kT[:, csl], qT[:, csl])], [b_kT, b_qT], bs)
            fw.op(fw.dve, lambda e: e.tensor_tensor(out=sTm[c % 4][:], in0=ps_s, in1=dm[:, h, :], op=ALU.mult),
                  reads=[bs, b_dm], writes=[b_sTm[c % 4]])

        def RS(c):
            cur, nxt = c % 2, 1 - (c % 2)
            ps_u, bu = u_sl[c % len(u_sl)]
            kb.mm(ps_u, [(kz[:, c, :], vtm[:, c, :])], [b_kz, b_vc[c]], bu)
            fw.op(fw.dve, lambda e: e.scalar_tensor_tensor(out=R32[nxt][:], in0=R32[cur][:], scalar=float(gC[h]), in1=ps_u,
                                                           op0=ALU.mult, op1=ALU.add),
                  reads=[b_R32[cur], bu], writes=[b_R32[nxt]])
            fw.op(fw.dve, lambda e: e.tensor_copy(out=Rbf[(c + 1) % 4][:], in_=R32[nxt][:]), reads=[b_R32[nxt]], writes=[b_Rbf[(c + 1) % 4]])

        def S2(c):
            csl = slice(c * 128, (c + 1) * 128)
            ps_o, bo = o_sl[c % len(o_sl)]
            kb.mm(ps_o, [(sTm[c % 4][:], vtm[:, c, :]), (qT[:, csl], Rbf[c % 4][:])],
                  [b_sTm[c % 4], b_vc[c], b_qT, b_Rbf[c % 4]], bo)
            fw.op(fw.act, lambda e: e.activation(out=junk[:], in_=ps_o, func=AF.Square, accum_out=st[:, c, 0:1]),
                  reads=[bo], writes=[b_junk, b_st[c]])
            fw.op(fw.act, lambda e: e.activation(out=st[:, c, 1:2], in_=st[:, c, 0:1], func=AF.Sqrt, scale=1.0 / 256, bias=EPS),
                  reads=[b_st[c]], writes=[b_st[c]])
            fw.op(fw.dve, lambda e: e.reciprocal(out=st[:, c, 2:3], in_=st[:, c, 1:2]), reads=[b_st[c]], writes=[b_st[c]])
            fw.op(fw.dve, lambda e: e.scalar_tensor_tensor(out=mo[c % 4][:], in0=ps_o, scalar=st[:, c, 2:3], in1=gs[:, c, :],
                                                           op0=ALU.mult, op1=ALU.mult),
                  reads=[bo, b_st[c], b_gc[c]], writes=[b_mo[c % 4]])

        def S3(c):
            csl = slice(c * 128, (c + 1) * 128)
            ps_t, bt = t_sl[c % len(t_sl)]
            for i in range(2):
                kb.mm(ps_t[:, i * 128:(i + 1) * 128], [(mo[c % 4][:, i * 128:(i + 1) * 128], kb.ident)], [b_mo[c % 4], kb.b_cst], bt)
            fw.op(fw.act, lambda e: e.copy(out=mth[:, :, csl], in_=ps_t.rearrange("p (i t) -> p i t", i=2)),
                  reads=[bt], writes=[b_mth])

        S1(0)
        S1(1)
        RS(0)
        for c in range(NCH):
            if c + 2 < NCH:
                S1(c + 2)
            if c + 2 < NCH:
                RS(c + 1)
            S2(c)
            if c >= 2:
                S3(c - 2)
        S3(NCH - 2)
        S3(NCH - 1)
        fw.dma(fw.sp, mixT[h * 256:(h + 1) * 256, :].rearrange("(j p) t -> p j t", p=128), mth[:], reads=[b_mth])
    fw.barrier()
    fw.release(m0)


def phase_kv(kb, hT, b_hT, w_kv, KT, V, KS, VS, KVS, KVG):
    fw = kb.fw
    m0 = fw.mark()
    kst = [fw.sbuf([128, NT], BF16) for _ in range(2)]; b_kst = [Buf(), Buf()]
    vst = [fw.sbuf([128, NCH, 512], BF16) for _ in range(2)]; b_vst = [Buf(), Buf()]
    ki = 0
    vi = 0
    b_KG = [Buf() for _ in range(16)]
    b_VG = [Buf() for _ in range(16)]
    b_kt = [[Buf() for _ in range(4)] for _ in range(3)]
    b_vt = [[Buf() for _ in range(4)] for _ in range(3)]

    def step(d):
        bks, bvs = Buf(), Buf()
        for g in range(3):
            hl = 128 * DILS[g]
            fw.dma(fw.sp, KS[d, :, HOFF[g]:HOFF[g] + hl], KT[g][d * 128:(d + 1) * 128, NT - hl:NT],
                   reads=[b_kt[g][d // 4]], writes=[bks], more=True)
            fw.dma(fw.sp, VS[d, :, HOFF[g]:HOFF[g] + hl].rearrange("j (r c) -> j r c", c=128),
                   V[g][NT - hl:NT, d * 128:(d + 1) * 128].rearrange("(j r) c -> j r c", r=DILS[g]),
                   reads=[b_vt[g][d // 4]], writes=[bvs], more=True)
        fw.allgather(KVS[d, :, :], KVG[d, :, :], [bks, bvs], [b_KG[d], b_VG[d]])
    pending = []

    def trickle():
        if pending:
            pending.pop(0)()
    for blk in range(4):
        for g in range(3):
            w, bw = kb.wnext(16, 512)
            kb.load_w(w, bw, w_kv, (2 * g) * 2048 + blk * 512, 512, 0)
            trickle()
            for j in range(4):
                s = ki % 2
                ki += 1
                for tb in range(4):
                    ps, bps = proj_fm(kb, w[:, :, j * 128:(j + 1) * 128], bw, hT, b_hT, tb)
                    kb.copy(kb.evac_eng(), kst[s][:, tb * 512:(tb + 1) * 512], ps[:, :], [bps], [b_kst[s]])
                r0 = (blk * 4 + j) * 128
                fw.dma(fw.sp, KT[g][r0:r0 + 128, :], kst[s][:], reads=[b_kst[s]], writes=[b_kt[g][blk]], more=True)
            w, bw = kb.wnext(16, 512)
            kb.load_w(w, bw, w_kv, (2 * g + 1) * 2048 + blk * 512, 512, 0)
            trickle()
            s = vi % 2
            vi += 1
            for c in range(NCH):
                ps, bps = proj_tm(kb, w, bw, 512, hT, b_hT, c)
                kb.copy(kb.evac_eng(), vst[s][:, c, :], ps[:, :], [bps], [b_vst[s]])
            fw.dma(fw.sp, V[g][:, blk * 512:(blk + 1) * 512].rearrange("(c p) f -> p c f", p=128), vst[s][:],
                   reads=[b_vst[s]], writes=[b_vt[g][blk]], more=True)
        pending.extend([(lambda d=d: step(d)) for d in range(4 * blk, 4 * blk + 4)])
    fw.barrier()
    fw.release(m0)
    return b_KG, b_VG, pending


def heads_B(kb, hT, b_hT, w_in, msk_in, KT, V, KG, VG, b_KG, b_VG, mixT):
    fw = kb.fw
    m0 = fw.mark()
    msk = fw.sbuf([128, 3, 512], BF16); b_msk = Buf()
    fw.dma(fw.sp, msk[:], msk_in, writes=[b_msk])
    qT = fw.sbuf([128, 3, NT], BF16); b_qT = Buf()
    gdT = [fw.sbuf([128, NT], BF16) for _ in range(2)]; b_gd = [Buf(), Buf()]
    KTt = [fw.sbuf([128, 128 * DILS[g] + NT], BF16) for g in range(3)]; b_K = [Buf() for _ in range(3)]
    Vt = [fw.sbuf([128, DILS[g], 16 // DILS[g] + 1, 128], BF16) for g in range(3)]; b_V = [Buf() for _ in range(3)]
    O32 = fw.sbuf([128, NT], F32); b_O = Buf()
    L32 = fw.sbuf([128, NT], F32); b_L = Buf()
    pT = [fw.sbuf([128, 512], BF16) for _ in range(2)]; b_pT = [Buf(), Buf()]
    mst = fw.sbuf([128, NT], BF16); b_mst = Buf()
    scale = 128.0 ** -0.5
    pi = 0

    def load_kv(d, g):
        hl = 128 * DILS[g]
        fw.dma(fw.sp, KTt[g][:, 0:hl], KG[d, 0:128, HOFF[g]:HOFF[g] + hl], reads=[b_KG[d]], writes=[b_K[g]])
        fw.dma(fw.sp, KTt[g][:, hl:hl + NT], KT[g][d * 128:(d + 1) * 128, :], writes=[b_K[g]], more=True)
        fw.dma(fw.sp, Vt[g][:, :, 0, :], VG[d, 0:128, HOFF[g]:HOFF[g] + hl].rearrange("j (r c) -> j r c", c=128),
               reads=[b_VG[d]], writes=[b_V[g]])
        for r in range(DILS[g]):
            src_o = V[g][:, d * 128:(d + 1) * 128].rearrange("(b j r) c -> j r b c", j=128, r=DILS[g])[:, r, :, :]
            fw.dma(fw.sp, Vt[g][:, r, 1:, :], src_o, writes=[b_V[g]], more=True)

    for d in range(16):
        w, bw = kb.wnext(16, 512)
        for g in range(3):
            kb.load_w(w, bw, w_in, g * 2048 + d * 128, 128, g * 128, more=(g > 0))
        kb.load_w(w, bw, w_in, 6144 + d * 128, 128, 384, more=True)
        if d == 0:
            for g in range(3):
                load_kv(0, g)
        for tb in range(4):
            tsl = slice(tb * 512, (tb + 1) * 512)
            for s in range(4):
                ps, bps = proj_fm(kb, w[:, :, s * 128:(s + 1) * 128], bw, hT, b_hT, tb)
                if s < 3:
                    fw.op(fw.act, lambda e: e.copy(out=qT[:, s, tsl], in_=ps[:, :]), reads=[bps], writes=[b_qT])
                else:
                    fw.op(fw.act, lambda e: e.activation(out=gdT[d % 2][:, tsl], in_=ps[:, :], func=AF.Silu), reads=[bps], writes=[b_gd[d % 2]])
        items = []
        for g in range(3):
            dil = DILS[g]
            nb = 16 // dil
            blocks = [(r, b) for r in range(dil) for b in range(nb)]
            for p0 in range(0, 16, 2):
                items.append((g, blocks[p0:p0 + 2]))
        state = {}

        def emit_S(i):
            g, pair = items[i]
            dil = DILS[g]
            qv = qT[:, g, :].rearrange("p (b j r) -> p r b j", j=128, r=dil)
            kv = KTt[g][:, :].rearrange("p (b j r) -> p r b j", j=128, r=dil)
            u = i % 2
            ps_s, bs = kb.bank()
            var = (1 if pair[0][1] == 0 else 0) + (1 if pair[1][1] == 0 else 0)

            def f(e):
                ins = e.matmul(ps_s[:, :], lhsT=kb.ident, rhs=msk[:, var, :], start=True, stop=False)
                n = 0
                for qi, (r, b) in enumerate(pair):
                    for sl, kb_ in ((2 * qi, b + 1), (2 * qi + 1, b)):
                        n += 1
                        ins = e.matmul(ps_s[:, sl * 128:(sl + 1) * 128], lhsT=kv[:, r, kb_, :], rhs=qv[:, r, b, :],
                                       start=False, stop=(n == 4))
                return ins
            fw.op(fw.pe, f, reads=[b_K[g], b_qT, b_msk, kb.b_cst], writes=[bs])
            fw.op(fw.act, lambda e: e.activation(out=pT[u][:], in_=ps_s[:, :], func=AF.Exp, scale=scale), reads=[bs], writes=[b_pT[u]])

        def emit_PV(i):
            g, pair = items[i]
            dil = DILS[g]
            Ov = O32[:, :].rearrange("p (b j r) -> p r b j", j=128, r=dil)
            Lv = L32[:, :].rearrange("p (b j r) -> p r b j", j=128, r=dil)
            u = i % 2
            ps_o, bo = kb.bank()
            ps_l, bl = kb.bank()
            for qi, (r, b) in enumerate(pair):
                kb.mm(ps_o[:, qi * 128:(qi + 1) * 128],
                      [(Vt[g][:, r, b + 1, :], pT[u][:, (2 * qi) * 128:(2 * qi + 1) * 128]),
                       (Vt[g][:, r, b, :], pT[u][:, (2 * qi + 1) * 128:(2 * qi + 2) * 128])], [b_V[g], b_pT[u]], bo)
                kb.mm(ps_l[:, qi * 128:(qi + 1) * 128],
                      [(kb.ones, pT[u][:, (2 * qi) * 128:(2 * qi + 1) * 128]),
                       (kb.ones, pT[u][:, (2 * qi + 1) * 128:(2 * qi + 2) * 128])], [kb.b_cst, b_pT[u]], bl)
            (r0, b0), (r1, b1) = pair
            if r0 == r1:
                oo = Ov[:, r0, b0:b0 + 2, :]
                ll = Lv[:, r0, b0:b0 + 2, :]
            else:
                oo = Ov[:, r0:r0 + 2, b0, :]
                ll = Lv[:, r0:r0 + 2, b0, :]
            pso = ps_o[:, 0:256].rearrange("p (a j) -> p a j", a=2)
            psl = ps_l[:, 0:256].rearrange("p (a j) -> p a j", a=2)
            if g == 0:
                fw.op(fw.act, lambda e: e.copy(out=oo, in_=pso), reads=[bo], writes=[b_O])
                fw.op(fw.act, lambda e: e.copy(out=ll, in_=psl), reads=[bl], writes=[b_L])
            else:
                fw.op(fw.dve, lambda e: e.tensor_tensor(out=oo, in0=oo, in1=pso, op=ALU.add), reads=[bo, b_O], writes=[b_O])
                fw.op(fw.dve, lambda e: e.tensor_tensor(out=ll, in0=ll, in1=psl, op=ALU.add), reads=[bl, b_L], writes=[b_L])
            if i % 8 == 7 and d + 1 < 16:
                load_kv(d + 1, g)

        for i in range(len(items) + 1):
            if i < len(items):
                emit_S(i)
            if i >= 1:
                emit_PV(i - 1)
        fw.op(fw.dve, lambda e: e.reciprocal(out=L32[:], in_=L32[:]), reads=[b_L], writes=[b_L])
        fw.op(fw.dve, lambda e: e.tensor_tensor(out=O32[:], in0=O32[:], in1=L32[:], op=ALU.mult), reads=[b_L, b_O], writes=[b_O])
        fw.op(fw.dve, lambda e: e.tensor_tensor(out=mst[:], in0=O32[:], in1=gdT[d % 2][:], op=ALU.mult), reads=[b_O, b_gd[d % 2]], writes=[b_mst])
        fw.dma(fw.sp, mixT[d * 128:(d + 1) * 128, :], mst[:], reads=[b_mst])
    fw.barrier()
    fw.release(m0)


def build_fused(gC, stages=5):
    nc = bass.Bass("TRN2", target_bir_lowering=False)
    dt = nc.dram_tensor

    def inp(name, shape, d=F32):
        return dt(name, shape, d, kind="ExternalInput").ap()
    x = inp("x", [NT, D])
    mem = inp("mem", [256, D])
    norm_a = inp("norm_a", [2, D])
    w_in_a = inp("w_in_a", [2, D, 8192])
    w_out_a = inp("w_out_a", [2, 3072, D])
    norm_b = inp("norm_b", [2, D])
    w_in_b = inp("w_in_b", [2, D, 10240])
    w_out_b = inp("w_out_b", [2, 3072, D])
    w_mem_kv = inp("w_mem_kv", [4, D, 2048])
    g_m = inp("g_m", [1, D])
    g_kv = inp("g_kv", [1, D])
    w_kv = inp("w_kv", [D, 12288])
    g_f = inp("g_f", [1, D])
    cs_in = inp("cs", [128, 2, NT])
    dm_in = inp("dm", [128, 17, 128])
    cst_in = inp("cst", [128, 3, 128], BF16)
    msk_in = inp("msk", [128, 3, 512], BF16)
    y = dt("y", [NT, D], F32, kind="ExternalOutput").ap()
    xp = [dt(f"xp{i}", [NT, D], F32).ap() for i in range(2)]
    mixT = dt("mixT", [3072, NT], BF16).ap()
    cc_src = [dt(f"ccs{l}", [8, 128, 256], F32).ap() for l in range(2)]
    cc_dst = [dt(f"ccd{l}", [8, 256, 256], F32).ap() for l in range(2)]
    KT = [dt(f"KT{g}", [D, NT], BF16).ap() for g in range(3)]
    V = [dt(f"V{g}", [NT, D], BF16).ap() for g in range(3)]
    KVS = dt("KVS", [16, 128, 2 * HSUM], BF16).ap()
    KVG = dt("KVG", [16, 256, 2 * HSUM], BF16).ap()
    KS = KVS[:, :, 0:HSUM]
    VS = KVS[:, :, HSUM:2 * HSUM]
    KG = KVG[:, :, 0:HSUM]
    VG = KVG[:, :, HSUM:2 * HSUM]

    kb = KB(nc, cst_in)
    fw = kb.fw
    out_toks = []
    hT = fw.sbuf([128, 16, NT], BF16)
    b_hT = [Buf() for _ in range(NCH)]

    xin = x
    for l in range(min(2, stages)):
        xout = xp[l]
        phase_norm(kb, xin, norm_a[l:l + 1, :], hT, b_hT, NCH)
        heads_A(kb, hT, b_hT, w_in_a[l], cs_in, dm_in, gC, mixT, cc_src[l], cc_dst[l])
        phase_mem(kb, mem, g_m, w_mem_kv[l], w_in_a[l], 6144, 7168, hT, b_hT, mixT)
        phase_out(kb, mixT, w_out_a[l], xin, xout, hT)
        xin = xout
    if stages >= 3:
        phase_norm(kb, xin, g_kv, hT, b_hT, NCH)
        b_KG, b_VG, xsteps = phase_kv(kb, hT, b_hT, w_kv, KT, V, KS, VS, KVS, KVG)
    for l in range(max(0, stages - 3)):
        xout = xp[l]
        hooks = None
        if l == 0:
            nx = len(xsteps)
            hooks = {m: xsteps[m * nx // 4:(m + 1) * nx // 4] for m in range(4)}
        phase_norm(kb, xin, norm_b[l:l + 1, :], hT, b_hT, NCH)
        phase_mem(kb, mem, g_m, w_mem_kv[2 + l], w_in_b[l], 8192, 9216, hT, b_hT, mixT, hooks)
        heads_B(kb, hT, b_hT, w_in_b[l], msk_in, KT, V, KG, VG, b_KG, b_VG, mixT)
        phase_out(kb, mixT, w_out_b[l], xin, xout, hT)
        xin = xout
    phase_norm(kb, xin, g_f, None, None, NCH, out_ap=y, out_toks=out_toks)
    finish(kb, out_toks)
    return nc


def _consts():
    bf = ml_dtypes.bfloat16
    cst = np.zeros((128, 3, 128), np.float32)
    cst[:, 0, :] = np.eye(128)
    cst[:, 1, :] = 1.0
    pm = np.zeros((128, 128), np.float32)
    for i in range(128):
        pm[(i + 64) % 128, i] = 1.0
    cst[:, 2, :] = pm
    half = 64
    inv = (1.0 / (np.float32(10000.0) ** np.linspace(0.0, 1.0, half, dtype=np.float32))).astype(np.float32)
    cs = []
    for hf in range(2):
        pos = (np.arange(NT, dtype=np.float32) + np.float32(hf * NT))
        ang = (pos[:, None] * inv[None, :]).astype(np.float32)
        c = np.cos(ang).astype(np.float32).T
        s = np.sin(ang).astype(np.float32).T
        t = np.zeros((128, 2, NT), np.float32)
        t[:64, 0] = c; t[64:, 0] = c
        t[:64, 1] = -s; t[64:, 1] = s
        cs.append(t)
    hh = np.arange(8, dtype=np.float64)
    log_g = np.log1p(-(2.0 ** (-5.0 - hh)))
    pos = np.arange(128, dtype=np.float64)
    scale = 128.0 ** -0.5
    dms = []
    for hf in range(2):
        dm = np.zeros((128, 17, 128), np.float32)
        for h in range(8):
            kk = pos[:, None]
            qq = pos[None, :]
            dm[:, h, :] = np.where(qq >= kk, scale * np.exp(-(kk + 1.0) * log_g[h]), 0.0)
            dm[:, 8 + h, :] = np.exp((pos + 1.0) * log_g[h])[None, :]
            dm[:, 16, h] = scale * np.exp((127.0 - pos) * log_g[h])
        dm[:, 16, 8] = float(hf)
        dms.append(dm)
    gC = np.exp(128.0 * log_g)
    kk = np.arange(128)[:, None]
    qq = np.arange(128)[None, :]
    NEG = -1.0e4
    cur = np.where(kk <= qq, 0.0, NEG).astype(np.float32)
    prev = np.where(kk >= qq, 0.0, NEG).astype(np.float32)
    dead = np.full((128, 128), NEG, np.float32)
    msk = []
    for hf in range(2):
        ph = prev if hf == 1 else dead
        m = np.stack([np.concatenate([cur, prev, cur, prev], axis=1),
                      np.concatenate([cur, ph, cur, prev], axis=1),
                      np.concatenate([cur, ph, cur, ph], axis=1)], axis=1)
        msk.append(m.astype(bf))
    return cst.astype(bf), cs, dms, gC, msk


def kernel(x, mem, norm_a, w_in_a, w_out_a, norm_b, w_in_b, w_out_b, w_mem_kv, mem_norm_g, kv_norm_g, w_kv, final_norm_g):
    f32 = np.float32
    x = np.asarray(x, f32); mem = np.asarray(mem, f32)
    shared = {
        "norm_a": np.asarray(norm_a, f32), "w_in_a": np.asarray(w_in_a, f32), "w_out_a": np.asarray(w_out_a, f32),
        "norm_b": np.asarray(norm_b, f32), "w_in_b": np.asarray(w_in_b, f32), "w_out_b": np.asarray(w_out_b, f32),
        "w_mem_kv": np.asarray(w_mem_kv, f32), "g_m": np.asarray(mem_norm_g, f32).reshape(1, D),
        "g_kv": np.asarray(kv_norm_g, f32).reshape(1, D), "w_kv": np.asarray(w_kv, f32),
        "g_f": np.asarray(final_norm_g, f32).reshape(1, D),
    }
    cst, cs, dms, gC, msk = _consts()
    B = x.shape[0]
    nc = build_fused(gC)
    maps = []
    for c in range(2 * B):
        b, hf = c // 2, c % 2
        maps.append(dict(shared, x=np.ascontiguousarray(x[b, hf * NT:(hf + 1) * NT]), mem=mem[b],
                         cs=cs[hf], dm=dms[hf], cst=cst, msk=msk[hf]))
    res = run_bass_kernel_spmd(nc, maps, core_ids=list(range(2 * B))).results
    out = np.stack([np.concatenate([res[2 * b + hf]["y"] for hf in range(2)], axis=0) for b in range(B)], axis=0)
    return out.astype(f32)
```

```python
import math
import numpy as np
import ml_dtypes
import concourse.bass as bass
import concourse.mybir as mybir
from concourse.bass_utils import run_bass_kernel_spmd

F32 = mybir.dt.float32
BF16 = mybir.dt.bfloat16
AF = mybir.ActivationFunctionType
ALU = mybir.AluOpType

D = 2048
NT = 2048
NCH = NT // 128
EPS = 1e-6
SEM_LIMIT = 8000
DILS = (1, 4, 16)
HOFF = (0, 128, 640)
HSUM = 2688


class Holder:
    def __init__(self, fw, name):
        self.fw = fw
        self.name = name
        self.gen = 0
        self.sem = fw.new_sem(f"{name}_{self.gen}")
        self.count = 0

    def bump(self, n):
        self.count += n
        return (self.sem, self.count)

    def maybe_rotate(self, n=1):
        if self.count + n > SEM_LIMIT:
            self.gen += 1
            self.sem = self.fw.new_sem(f"{self.name}_{self.gen}")
            self.count = 0


class Buf:
    __slots__ = ("w", "r")

    def __init__(self):
        self.w = []
        self.r = []


class Eng:
    def __init__(self, fw, e, name, self_sync):
        self.e = e
        self.name = name
        self.h = Holder(fw, "s_" + name)
        self.seen = {}
        self.self_sync = self_sync

    def wait(self, tok, same_ok=False):
        if tok is None:
            return
        sem, val = tok
        if sem is self.h.sem and (same_ok or not self.self_sync):
            return
        if self.seen.get(sem.num, 0) >= val:
            return
        self.e.wait_ge(sem, val)
        self.seen[sem.num] = val


class FW:
    def __init__(self, nc):
        self.nc = nc
        self._ctx = []
        self.pe = Eng(self, nc.tensor, "pe", False)
        self.act = Eng(self, nc.scalar, "act", True)
        self.dve = Eng(self, nc.vector, "dve", True)
        self.pool = Eng(self, nc.gpsimd, "pool", False)
        self.sp = Eng(self, nc.sync, "sp", False)
        self.lanes_q = {id(self.sp): [Holder(self, f"dls{i}") for i in range(16)],
                        id(self.pool): [Holder(self, f"dlp{i}") for i in range(8)]}
        self.lane_i = {id(self.sp): 0, id(self.pool): 0}
        self.lanes = self.lanes_q[id(self.sp)] + self.lanes_q[id(self.pool)]
        self.cch = Holder(self, "cc")
        self._n = 0

    def new_sem(self, name):
        cm = self.nc.semaphore(name)
        s = cm.__enter__()
        self._ctx.append(cm)
        return s

    def sbuf(self, shape, dt):
        self._n += 1
        cm = self.nc.sbuf_tensor(f"sb{self._n}", list(shape), dt)
        t = cm.__enter__()
        self._ctx.append(cm)
        return t

    def psum(self, shape, dt):
        self._n += 1
        cm = self.nc.psum_tensor(f"ps{self._n}", list(shape), dt)
        t = cm.__enter__()
        self._ctx.append(cm)
        return t

    def mark(self):
        return len(self._ctx)

    def release(self, mark):
        while len(self._ctx) > mark:
            self._ctx.pop().__exit__(None, None, None)

    def close(self):
        self.release(0)

    def op(self, eng, fn, reads=(), writes=()):
        for b in reads:
            for t in b.w:
                eng.wait(t)
        for b in writes:
            for t in b.w:
                eng.wait(t, same_ok=True)
            for t in b.r:
                eng.wait(t, same_ok=True)
        eng.h.maybe_rotate(1)
        ins = fn(eng.e)
        tok = eng.h.bump(1)
        ins.then_inc(tok[0], 1)
        for b in reads:
            b.r.append(tok)
        for b in writes:
            b.w = [tok]
            b.r = []
        return tok

    def dma(self, q, out, in_, reads=(), writes=(), more=False):
        ql = self.lanes_q[id(q)]
        lane = ql[self.lane_i[id(q)]]
        self.lane_i[id(q)] = (self.lane_i[id(q)] + 1) % len(ql)
        if lane.count:
            q.wait((lane.sem, lane.count))
        lane.maybe_rotate(16)
        for b in reads:
            for t in b.w:
                q.wait(t)
        for b in writes:
            if not more:
                for t in b.w:
                    q.wait(t)
            for t in b.r:
                q.wait(t)
        ins = q.e.dma_start(out=out, in_=in_)
        tok = lane.bump(16)
        ins.then_inc(tok[0], 16)
        for b in reads:
            b.r.append(tok)
        for b in writes:
            if more:
                b.w.append(tok)
            else:
                b.w = [tok]
            b.r = []
        return tok

    def allgather(self, in_ap, out_ap, reads, writes):
        q = self.pool
        for b in reads:
            for t in b.w:
                q.wait(t)
        for b in writes:
            for t in b.w + b.r:
                q.wait(t)
        if self.cch.count:
            q.wait((self.cch.sem, self.cch.count))
        self.cch.maybe_rotate(1)
        ins = q.e.collective_compute("AllGather", ALU.bypass, replica_groups=[[0, 1], [2, 3], [4, 5], [6, 7]],
                                     ins=[in_ap], outs=[out_ap])
        tok = self.cch.bump(1)
        ins.then_inc(tok[0], 1)
        for b in reads:
            b.r.append(tok)
        for b in writes:
            b.w = [tok]
            b.r = []
        return tok

    def barrier(self):
        engs = [self.pe, self.act, self.dve, self.sp, self.pool]
        toks = [(e.h.sem, e.h.count) for e in engs if e.h.count]
        toks += [(l.sem, l.count) for l in self.lanes if l.count]
        for e in engs:
            for t in toks:
                e.wait(t, same_ok=True)


class KB:
    def __init__(self, nc, cst_ap):
        self.nc = nc
        self.fw = FW(nc)
        fw = self.fw
        self.banks = [fw.psum([128, 512], F32) for _ in range(8)]
        self.bbufs = [Buf() for _ in range(8)]
        self.bi = 0
        self.cst = fw.sbuf([128, 3, 128], BF16)
        self.b_cst = Buf()
        fw.dma(fw.sp, self.cst[:], cst_ap, writes=[self.b_cst])
        self.ident = self.cst[:, 0, :]
        self.ones = self.cst[:, 1, :]
        self.pmat = self.cst[:, 2, :]
        self.wb = [fw.sbuf([128, 12288], BF16) for _ in range(2)]
        self.b_wb = [Buf(), Buf()]
        self.wi = 0
        self.ev = 0

    def bank(self):
        i = self.bi
        self.bi = (self.bi + 1) % 8
        return self.banks[i], self.bbufs[i]

    def wnext(self, K, ncols):
        i = self.wi
        self.wi = 1 - self.wi
        view = self.wb[i][:, 0:K * ncols].rearrange("p (k c) -> p k c", k=K)
        return view, self.b_wb[i]

    def load_w(self, wview, bw, w2d, col0, ncols, dst0, more=False):
        src = w2d[:, col0:col0 + ncols].rearrange("(k p) c -> p k c", p=128)
        self.fw.dma(self.fw.pool, wview[:, :, dst0:dst0 + ncols], src, writes=[bw], more=more)

    def mm(self, out_ap, pairs, reads, bbuf):
        n = len(pairs)

        def f(e):
            ins = None
            for i, (l, r) in enumerate(pairs):
                ins = e.matmul(out_ap, lhsT=l, rhs=r, start=(i == 0), stop=(i == n - 1))
            return ins
        return self.fw.op(self.fw.pe, f, reads=reads, writes=[bbuf])

    def evac_eng(self):
        self.ev ^= 1
        return self.fw.act if self.ev else self.fw.dve

    def copy(self, eng, out, in_, reads, writes):
        if eng is self.fw.act:
            return self.fw.op(eng, lambda e: e.copy(out=out, in_=in_), reads=reads, writes=writes)
        return self.fw.op(eng, lambda e: e.tensor_copy(out=out, in_=in_), reads=reads, writes=writes)


def phase_norm(kb, x_ap, g_ap, hT, b_hT, nchunks, out_ap=None, out_toks=None):
    fw = kb.fw
    m = fw.mark()
    gB = fw.sbuf([128, D], F32); b_gB = Buf()
    fw.dma(fw.sp, gB[:], g_ap.broadcast_to([128, D]), writes=[b_gB])
    NX = 3
    xt = [fw.sbuf([128, D], F32) for _ in range(NX)]; b_xt = [Buf() for _ in range(NX)]
    odt = BF16 if hT is not None else F32
    hb = [fw.sbuf([128, D], odt) for _ in range(2)]; b_hb = [Buf(), Buf()]
    junk = fw.sbuf([128, D], BF16); b_junk = Buf()
    st = fw.sbuf([128, nchunks, 4], F32); b_st = [Buf() for _ in range(nchunks)]

    def load(c):
        i = c % NX
        fw.dma(fw.sp, xt[i][:], x_ap[c * 128:(c + 1) * 128, :], writes=[b_xt[i]])

    def stage1(c):
        i = c % NX
        j = c % 2
        fw.op(fw.act, lambda e: e.activation(out=junk[:], in_=xt[i][:], func=AF.Square, accum_out=st[:, c, 0:1]),
              reads=[b_xt[i]], writes=[b_junk, b_st[c]])
        fw.op(fw.act, lambda e: e.activation(out=st[:, c, 1:2], in_=st[:, c, 0:1], func=AF.Sqrt, scale=1.0 / D, bias=EPS),
              reads=[b_st[c]], writes=[b_st[c]])
        fw.op(fw.dve, lambda e: e.reciprocal(out=st[:, c, 2:3], in_=st[:, c, 1:2]), reads=[b_st[c]], writes=[b_st[c]])
        fw.op(fw.dve, lambda e: e.scalar_tensor_tensor(out=hb[j][:], in0=xt[i][:], scalar=st[:, c, 2:3], in1=gB[:],
                                                       op0=ALU.mult, op1=ALU.mult),
              reads=[b_xt[i], b_st[c], b_gB], writes=[b_hb[j]])

    def stage2(c):
        j = c % 2
        if hT is not None:
            eng = kb.evac_eng()
            for b4 in range(4):
                ps, bps = kb.bank()
                for jj in range(4):
                    kk = 4 * b4 + jj
                    kb.mm(ps[:, jj * 128:(jj + 1) * 128], [(hb[j][:, kk * 128:(kk + 1) * 128], kb.ident)],
                          [b_hb[j], kb.b_cst], bps)
                kb.copy(eng, hT[:, 4 * b4:4 * b4 + 4, c * 128:(c + 1) * 128],
                        ps[:].rearrange("p (j t) -> p j t", j=4), [bps], [b_hT[c]])
        else:
            t = fw.dma(fw.sp, out_ap[c * 128:(c + 1) * 128, :], hb[j][:], reads=[b_hb[j]])
            out_toks.append(t)

    for c in range(min(NX, nchunks)):
        load(c)
    stage1(0)
    for c in range(nchunks):
        if c + 1 < nchunks:
            stage1(c + 1)
        if c + NX < nchunks:
            load(c + NX)
        stage2(c)
    fw.barrier()
    fw.release(m)


def proj_fm(kb, wcols, bw, hT, b_hT, tb):
    ps, bps = kb.bank()
    pairs = [(wcols[:, k, :], hT[:, k, tb * 512:(tb + 1) * 512]) for k in range(16)]
    kb.mm(ps[:, :], pairs, [bw] + b_hT[4 * tb:4 * tb + 4], bps)
    return ps, bps


def proj_tm(kb, w, bw, ncols, hT, b_hT, c):
    ps, bps = kb.bank()
    pairs = [(hT[:, k, c * 128:(c + 1) * 128], w[:, k, :]) for k in range(16)]
    kb.mm(ps[:, 0:ncols], pairs, [bw, b_hT[c]], bps)
    return ps, bps


def phase_mem(kb, mem_ap, gmem_ap, wmkv, w_in, cq, cg, hT, b_hT, mixT, hooks=None):
    fw = kb.fw
    m0 = fw.mark()
    memT = fw.sbuf([128, 16, 256], BF16); b_memT = [Buf(), Buf()]
    phase_norm(kb, mem_ap, gmem_ap, memT, b_memT, 2)
    mkT = fw.sbuf([128, 8, 256], BF16); b_mkT = Buf()
    mvt = fw.sbuf([128, 2, 1024], BF16); b_mvt = Buf()
    for blk in range(2):
        w, bw = kb.wnext(16, 512)
        kb.load_w(w, bw, wmkv, blk * 512, 512, 0)
        for j in range(4):
            ps, bps = kb.bank()
            kb.mm(ps[:, 0:256], [(w[:, k, j * 128:(j + 1) * 128], memT[:, k, :]) for k in range(16)], [bw] + b_memT, bps)
            kb.copy(kb.evac_eng(), mkT[:, blk * 4 + j, :], ps[:, 0:256], [bps], [b_mkT])
    for blk in range(2):
        w, bw = kb.wnext(16, 512)
        kb.load_w(w, bw, wmkv, 1024 + blk * 512, 512, 0)
        for jc in range(2):
            ps, bps = kb.bank()
            kb.mm(ps[:, :], [(memT[:, k, jc * 128:(jc + 1) * 128], w[:, k, :]) for k in range(16)], [bw] + b_memT, bps)
            kb.copy(kb.evac_eng(), mvt[:, jc, blk * 512:(blk + 1) * 512], ps[:, :], [bps], [b_mvt])
    mqT2 = [fw.sbuf([128, 2, NT], BF16) for _ in range(2)]; b_mq2 = [Buf(), Buf()]
    gmT2 = [fw.sbuf([128, 2, NT], BF16) for _ in range(2)]; b_gm2 = [Buf(), Buf()]
    pT = fw.sbuf([128, 2, 512], BF16); b_pT = Buf()
    rl = fw.sbuf([128, 512], F32); b_rl = Buf()
    tmp = fw.sbuf([128, 512], F32); b_tmp = Buf()
    mst = fw.sbuf([128, 2, NT], BF16); b_mst = Buf()
    for m in range(4):
        mqT, b_mq, gmT, b_gm = mqT2[m % 2], b_mq2[m % 2], gmT2[m % 2], b_gm2[m % 2]
        w, bw = kb.wnext(16, 512)
        kb.load_w(w, bw, w_in, cq + m * 256, 256, 0)
        kb.load_w(w, bw, w_in, cg + m * 256, 256, 256, more=True)
        if hooks:
            for hk in hooks[m]:
                hk()
        for tb in range(4):
            for i in range(2):
                ps, bps = proj_fm(kb, w[:, :, i * 128:(i + 1) * 128], bw, hT, b_hT, tb)
                fw.op(fw.act, lambda e: e.copy(out=mqT[:, i, tb * 512:(tb + 1) * 512], in_=ps[:, :]), reads=[bps], writes=[b_mq])
            for i in range(2):
                ps, bps = proj_fm(kb, w[:, :, 256 + i * 128:256 + (i + 1) * 128], bw, hT, b_hT, tb)
                fw.op(fw.act, lambda e: e.activation(out=gmT[:, i, tb * 512:(tb + 1) * 512], in_=ps[:, :], func=AF.Silu),
                      reads=[bps], writes=[b_gm])
        for tb in range(4):
            tsl = slice(tb * 512, (tb + 1) * 512)
            for j in range(2):
                ps, bps = kb.bank()
                kb.mm(ps[:, :], [(mkT[:, 2 * m + i, j * 128:(j + 1) * 128], mqT[:, i, tsl]) for i in range(2)],
                      [b_mkT, b_mq], bps)
                fw.op(fw.act, lambda e: e.activation(out=pT[:, j, :], in_=ps[:, :], func=AF.Exp, scale=1.0 / 16.0),
                      reads=[bps], writes=[b_pT])
            pl, bpl = kb.bank()
            kb.mm(pl[:, :], [(kb.ones, pT[:, j, :]) for j in range(2)], [kb.b_cst, b_pT], bpl)
            fw.op(fw.dve, lambda e: e.reciprocal(out=rl[:], in_=pl[:, :]), reads=[bpl], writes=[b_rl])
            for i in range(2):
                po, bpo = kb.bank()
                kb.mm(po[:, :], [(mvt[:, j, m * 256 + i * 128:m * 256 + (i + 1) * 128], pT[:, j, :]) for j in range(2)],
                      [b_mvt, b_pT], bpo)
                fw.op(fw.dve, lambda e: e.tensor_tensor(out=tmp[:], in0=po[:, :], in1=rl[:], op=ALU.mult),
                      reads=[bpo, b_rl], writes=[b_tmp])
                fw.op(fw.dve, lambda e: e.tensor_tensor(out=mst[:, i, tsl], in0=tmp[:], in1=gmT[:, i, tsl], op=ALU.mult),
                      reads=[b_tmp, b_gm], writes=[b_mst])
        r0 = 2048 + m * 256
        fw.dma(fw.sp, mixT[r0:r0 + 256, :].rearrange("(j p) t -> p j t", p=128), mst[:], reads=[b_mst])
    fw.barrier()
    fw.release(m0)


def phase_out(kb, mixT, w_out, x_in, x_out, hT, out_toks=None):
    fw = kb.fw
    m0 = fw.mark()
    mx = fw.sbuf([128, 8, NT], BF16)
    b_mr = [Buf() for _ in range(6)]
    for i in range(6):
        dst = hT[:, 4 * i:4 * i + 4, :] if i < 4 else mx[:, 4 * (i - 4):4 * (i - 4) + 4, :]
        fw.dma(fw.sp, dst, mixT[i * 512:(i + 1) * 512, :].rearrange("(k p) t -> p k t", p=128), writes=[b_mr[i]])

    def mk(k, c):
        return hT[:, k, c * 128:(c + 1) * 128] if k < 16 else mx[:, k - 16, c * 128:(c + 1) * 128]
    xo = [fw.sbuf([128, 512], F32) for _ in range(8)]; b_xo = [Buf() for _ in range(8)]
    xi = 0

    def epilogue(ps, bps, n, c, xb):
        fw.op(fw.dve, lambda e: e.tensor_tensor(out=xo[xb][:], in0=ps[:, :], in1=xo[xb][:], op=ALU.add),
              reads=[bps, b_xo[xb]], writes=[b_xo[xb]])
        t = fw.dma(fw.sp, x_out[c * 128:(c + 1) * 128, n * 512:(n + 1) * 512], xo[xb][:], reads=[b_xo[xb]])
        if out_toks is not None:
            out_toks.append(t)

    for n in range(4):
        w, bw = kb.wnext(24, 512)
        kb.load_w(w, bw, w_out, n * 512, 512, 0)
        c_start = 0
        if n == 0:
            c_start = 8
            b8 = [kb.bank() for _ in range(8)]
            for c in range(8):
                fw.dma(fw.sp, xo[c][:], x_in[c * 128:(c + 1) * 128, 0:512], writes=[b_xo[c]])
            for i in range(6):
                for c in range(8):
                    ps, bps = b8[c]

                    def f(e, i=i, c=c, ps=ps):
                        ins = None
                        for kk in range(4):
                            k = 4 * i + kk
                            ins = e.matmul(ps[:, :], lhsT=mk(k, c), rhs=w[:, k, :], start=(k == 0), stop=(k == 23))
                        return ins
                    fw.op(fw.pe, f, reads=[b_mr[i], bw], writes=[bps])
            for c in range(8):
                epilogue(b8[c][0], b8[c][1], 0, c, c)
            xi = 8
        for c in range(c_start, NCH):
            xb = xi % 8
            xi += 1
            fw.dma(fw.sp, xo[xb][:], x_in[c * 128:(c + 1) * 128, n * 512:(n + 1) * 512], writes=[b_xo[xb]])
            ps, bps = kb.bank()
            kb.mm(ps[:, :], [(mk(k, c), w[:, k, :]) for k in range(24)], b_mr + [bw], bps)
            epilogue(ps, bps, n, c, xb)
    fw.barrier()
    fw.release(m0)


def finish(kb, toks):
    fw = kb.fw
    fw.barrier()
    for t in toks:
        fw.sp.wait(t)
    fw.close()


def heads_A(kb, hT, b_hT, w_in, cs_in, dm_in, gC, mixT, cc_src, cc_dst):
    fw = kb.fw
    m0 = fw.mark()
    cs = fw.sbuf([128, 2, NT], F32); b_cs = Buf()
    fw.dma(fw.sp, cs[:], cs_in, writes=[b_cs])
    dm = fw.sbuf([128, 17, 128], F32); b_dm = Buf()
    fw.dma(fw.sp, dm[:], dm_in, writes=[b_dm])
    qT = fw.sbuf([128, NT], BF16); b_qT = Buf()
    kT = fw.sbuf([128, NT], BF16); b_kT = Buf()
    kz = fw.sbuf([128, NCH, 128], BF16); b_kz = Buf()
    vtm = fw.sbuf([128, NCH, 256], BF16); b_vc = [Buf() for _ in range(NCH)]
    gs = fw.sbuf([128, NCH, 256], BF16); b_gc = [Buf() for _ in range(NCH)]
    raw = [fw.sbuf([128, 512], BF16) for _ in range(2)]; b_raw = [Buf(), Buf()]
    t1 = [fw.sbuf([128, 512], F32) for _ in range(2)]; b_t1 = [Buf(), Buf()]
    t2 = [fw.sbuf([128, 512], F32) for _ in range(2)]; b_t2 = [Buf(), Buf()]
    mth = fw.sbuf([128, 2, NT], BF16); b_mth = Buf()
    R32 = [fw.sbuf([128, 256], F32) for _ in range(2)]; b_R32 = [Buf(), Buf()]
    Rbf = [fw.sbuf([128, 256], BF16) for _ in range(4)]; b_Rbf = [Buf() for _ in range(4)]
    Rg = fw.sbuf([128, 256], F32); b_Rg = Buf()
    sTm = [fw.sbuf([128, 128], BF16) for _ in range(4)]; b_sTm = [Buf() for _ in range(4)]
    mo = [fw.sbuf([128, 256], BF16) for _ in range(4)]; b_mo = [Buf() for _ in range(4)]
    junk = fw.sbuf([128, 256], BF16); b_junk = Buf()
    st = fw.sbuf([128, NCH, 4], F32); b_st = [Buf() for _ in range(NCH)]

    def qk_proj(h, which, w1, bw1):
        for tb in range(4):
            tsl = slice(tb * 512, (tb + 1) * 512)
            ps, bps = proj_fm(kb, w1[:, :, which * 128:(which + 1) * 128], bw1, hT, b_hT, tb)
            fw.op(fw.act, lambda e: e.copy(out=raw[which][:], in_=ps[:, :]), reads=[bps], writes=[b_raw[which]])
            p2, bp2 = kb.bank()
            kb.mm(p2[:, :], [(kb.pmat, raw[which][:])], [kb.b_cst, b_raw[which]], bp2)
            fw.op(fw.dve, lambda e: e.tensor_tensor(out=t1[which][:], in0=raw[which][:], in1=cs[:, 0, tsl], op=ALU.mult),
                  reads=[b_raw[which], b_cs], writes=[b_t1[which]])
            fw.op(fw.dve, lambda e: e.tensor_tensor(out=t2[which][:], in0=p2[:, :], in1=cs[:, 1, tsl], op=ALU.mult),
                  reads=[bp2, b_cs], writes=[b_t2[which]])
            if which == 0:
                fw.op(fw.dve, lambda e: e.tensor_tensor(out=t1[0][:], in0=t1[0][:], in1=t2[0][:], op=ALU.add),
                      reads=[b_t1[0], b_t2[0]], writes=[b_t1[0]])
                xib = dm[:, 8 + h, :].unsqueeze(1).broadcast_to([128, 4, 128])
                fw.op(fw.dve, lambda e: e.tensor_tensor(out=qT[:, tsl].rearrange("p (c t) -> p c t", c=4),
                                                        in0=t1[0][:].rearrange("p (c t) -> p c t", c=4), in1=xib, op=ALU.mult),
                      reads=[b_t1[0], b_dm], writes=[b_qT])
            else:
                fw.op(fw.dve, lambda e: e.tensor_tensor(out=kT[:, tsl], in0=t1[1][:], in1=t2[1][:], op=ALU.add),
                      reads=[b_t1[1], b_t2[1]], writes=[b_kT])

    for h in range(8):
        w1, bw1 = kb.wnext(16, 256)
        kb.load_w(w1, bw1, w_in, 1024 + h * 128, 128, 128)
        kb.load_w(w1, bw1, w_in, h * 128, 128, 0, more=True)
        w2, bw2 = kb.wnext(16, 512)
        kb.load_w(w2, bw2, w_in, 2048 + h * 256, 256, 0)
        kb.load_w(w2, bw2, w_in, 4096 + h * 256, 256, 256, more=True)
        qk_proj(h, 1, w1, bw1)
        for c4 in range(4):
            ps, bps = kb.bank()
            for j in range(4):
                c = 4 * c4 + j
                kb.mm(ps[:, j * 128:(j + 1) * 128], [(kT[:, c * 128:(c + 1) * 128], kb.ident)], [b_kT, kb.b_cst], bps)
            fw.op(fw.dve, lambda e: e.tensor_scalar_mul(out=kz[:, 4 * c4:4 * c4 + 4, :],
                                                        in0=ps[:].rearrange("p (j t) -> p j t", j=4),
                                                        scalar1=dm[:, 16, h:h + 1]),
                  reads=[bps, b_dm], writes=[b_kz])
        fw.op(fw.dve, lambda e: e.memset(R32[0][:], 0.0), writes=[b_R32[0]])

        def p1(c):
            cur, nxt = c % 2, 1 - (c % 2)
            ps_u, bu = kb.bank()
            kb.mm(ps_u[:, 0:256], [(kz[:, c, :], vtm[:, c, :])], [b_kz, b_vc[c]], bu)
            fw.op(fw.dve, lambda e: e.scalar_tensor_tensor(out=R32[nxt][:], in0=R32[cur][:], scalar=float(gC[h]), in1=ps_u[:, 0:256],
                                                           op0=ALU.mult, op1=ALU.add),
                  reads=[b_R32[cur], bu], writes=[b_R32[nxt]])
        for c in range(NCH):
            ps, bps = proj_tm(kb, w2, bw2, 512, hT, b_hT, c)
            fw.op(fw.act, lambda e: e.copy(out=vtm[:, c, :], in_=ps[:, 0:256]), reads=[bps], writes=[b_vc[c]])
            fw.op(fw.act, lambda e: e.activation(out=gs[:, c, :], in_=ps[:, 256:512], func=AF.Silu), reads=[bps], writes=[b_gc[c]])
            if c >= 1:
                p1(c - 1)
        p1(NCH - 1)
        b_src, b_dst = Buf(), Buf()
        fw.dma(fw.sp, cc_src[h, :, :], R32[0][:], reads=[b_R32[0]], writes=[b_src])
        fw.allgather(cc_src[h, :, :], cc_dst[h, :, :], [b_src], [b_dst])
        qk_proj(h, 0, w1, bw1)
        fw.dma(fw.sp, Rg[:], cc_dst[h, 0:128, :], reads=[b_dst], writes=[b_Rg])
        fw.op(fw.dve, lambda e: e.tensor_scalar_mul(out=R32[0][:], in0=Rg[:], scalar1=dm[:, 16, 8:9]),
              reads=[b_Rg, b_dm], writes=[b_R32[0]])
        fw.op(fw.act, lambda e: e.copy(out=Rbf[0][:], in_=R32[0][:]), reads=[b_R32[0]], writes=[b_Rbf[0]])
        pbanks = [kb.bank() for _ in range(8)]
        o_sl = [(pbanks[i][0][:, 0:256], pbanks[i][1]) for i in (0, 1, 2)]
        u_sl = [(pbanks[i][0][:, 0:256], pbanks[i][1]) for i in (3, 4)]
        t_sl = [(pbanks[i][0][:, 0:256], pbanks[i][1]) for i in (5,)]
        s_sl = [(pbanks[i][0][:, 0:128], pbanks[i][1]) for i in (6, 7)]

        def S1(c):
            csl = slice(c * 128, (c + 1) * 128)
            ps_s, bs = s_sl[c % len(s_sl)]
            kb.mm(ps_s, [(kT[:, csl], qT[:, csl])], [b_kT, b_qT], bs)
            fw.op(fw.dve, lambda e: e.tensor_tensor(out=sTm[c % 4][:], in0=ps_s, in1=dm[:, h, :], op=ALU.mult),
                  reads=[bs, b_dm], writes=[b_sTm[c % 4]])

        def RS(c):
            cur, nxt = c % 2, 1 - (c % 2)
            ps_u, bu = u_sl[c % len(u_sl)]
            kb.mm(ps_u, [(kz[:, c, :], vtm[:, c, :])], [b_kz, b_vc[c]], bu)
            fw.op(fw.dve, lambda e: e.scalar_tensor_tensor(out=R32[nxt][:], in0=R32[cur][:], scalar=float(gC[h]), in1=ps_u,
                                                           op0=ALU.mult, op1=ALU.add),
                  reads=[b_R32[cur], bu], writes=[b_R32[nxt]])
            fw.op(fw.dve, lambda e: e.tensor_copy(out=Rbf[(c + 1) % 4][:], in_=R32[nxt][:]), reads=[b_R32[nxt]], writes=[b_Rbf[(c + 1) % 4]])

        def S2(c):
            csl = slice(c * 128, (c + 1) * 128)
            ps_o, bo = o_sl[c % len(o_sl)]
            kb.mm(ps_o, [(sTm[c % 4][:], vtm[:, c, :]), (qT[:, csl], Rbf[c % 4][:])],
                  [b_sTm[c % 4], b_vc[c], b_qT, b_Rbf[c % 4]], bo)
            fw.op(fw.act, lambda e: e.activation(out=junk[:], in_=ps_o, func=AF.Square, accum_out=st[:, c, 0:1]),
                  reads=[bo], writes=[b_junk, b_st[c]])
            fw.op(fw.act, lambda e: e.activation(out=st[:, c, 1:2], in_=st[:, c, 0:1], func=AF.Sqrt, scale=1.0 / 256, bias=EPS),
                  reads=[b_st[c]], writes=[b_st[c]])
            fw.op(fw.dve, lambda e: e.reciprocal(out=st[:, c, 2:3], in_=st[:, c, 1:2]), reads=[b_st[c]], writes=[b_st[c]])
            fw.op(fw.dve, lambda e: e.scalar_tensor_tensor(out=mo[c % 4][:], in0=ps_o, scalar=st[:, c, 2:3], in1=gs[:, c, :],
                                                           op0=ALU.mult, op1=ALU.mult),
                  reads=[bo, b_st[c], b_gc[c]], writes=[b_mo[c % 4]])

        def S3(c):
            csl = slice(c * 128, (c + 1) * 128)
            ps_t, bt = t_sl[c % len(t_sl)]
            for i in range(2):
                kb.mm(ps_t[:, i * 128:(i + 1) * 128], [(mo[c % 4][:, i * 128:(i + 1) * 128], kb.ident)], [b_mo[c % 4], kb.b_cst], bt)
            fw.op(fw.act, lambda e: e.copy(out=mth[:, :, csl], in_=ps_t.rearrange("p (i t) -> p i t", i=2)),
                  reads=[bt], writes=[b_mth])

        S1(0)
        S1(1)
        RS(0)
        for c in range(NCH):
            if c + 2 < NCH:
                S1(c + 2)
            if c + 2 < NCH:
                RS(c + 1)
            S2(c)
            if c >= 2:
                S3(c - 2)
        S3(NCH - 2)
        S3(NCH - 1)
        fw.dma(fw.sp, mixT[h * 256:(h + 1) * 256, :].rearrange("(j p) t -> p j t", p=128), mth[:], reads=[b_mth])
    fw.barrier()
    fw.release(m0)


def phase_kv(kb, hT, b_hT, w_kv, KT, V, KS, VS, KVS, KVG):
    fw = kb.fw
    m0 = fw.mark()
    kst = [fw.sbuf([128, NT], BF16) for _ in range(2)]; b_kst = [Buf(), Buf()]
    vst = [fw.sbuf([128, NCH, 512], BF16) for _ in range(2)]; b_vst = [Buf(), Buf()]
    ki = 0
    vi = 0
    b_KG = [Buf() for _ in range(16)]
    b_VG = [Buf() for _ in range(16)]
    b_kt = [[Buf() for _ in range(4)] for _ in range(3)]
    b_vt = [[Buf() for _ in range(4)] for _ in range(3)]

    def step(d):
        bks, bvs = Buf(), Buf()
        for g in range(3):
            hl = 128 * DILS[g]
            fw.dma(fw.sp, KS[d, :, HOFF[g]:HOFF[g] + hl], KT[g][d * 128:(d + 1) * 128, NT - hl:NT],
                   reads=[b_kt[g][d // 4]], writes=[bks], more=True)
            fw.dma(fw.sp, VS[d, :, HOFF[g]:HOFF[g] + hl].rearrange("j (r c) -> j r c", c=128),
                   V[g][NT - hl:NT, d * 128:(d + 1) * 128].rearrange("(j r) c -> j r c", r=DILS[g]),
                   reads=[b_vt[g][d // 4]], writes=[bvs], more=True)
        fw.allgather(KVS[d, :, :], KVG[d, :, :], [bks, bvs], [b_KG[d], b_VG[d]])
    pending = []

    def trickle():
        if pending:
            pending.pop(0)()
    for blk in range(4):
        for g in range(3):
            w, bw = kb.wnext(16, 512)
            kb.load_w(w, bw, w_kv, (2 * g) * 2048 + blk * 512, 512, 0)
            trickle()
            for j in range(4):
                s = ki % 2
                ki += 1
                for tb in range(4):
                    ps, bps = proj_fm(kb, w[:, :, j * 128:(j + 1) * 128], bw, hT, b_hT, tb)
                    kb.copy(kb.evac_eng(), kst[s][:, tb * 512:(tb + 1) * 512], ps[:, :], [bps], [b_kst[s]])
                r0 = (blk * 4 + j) * 128
                fw.dma(fw.sp, KT[g][r0:r0 + 128, :], kst[s][:], reads=[b_kst[s]], writes=[b_kt[g][blk]], more=True)
            w, bw = kb.wnext(16, 512)
            kb.load_w(w, bw, w_kv, (2 * g + 1) * 2048 + blk * 512, 512, 0)
            trickle()
            s = vi % 2
            vi += 1
            for c in range(NCH):
                ps, bps = proj_tm(kb, w, bw, 512, hT, b_hT, c)
                kb.copy(kb.evac_eng(), vst[s][:, c, :], ps[:, :], [bps], [b_vst[s]])
            fw.dma(fw.sp, V[g][:, blk * 512:(blk + 1) * 512].rearrange("(c p) f -> p c f", p=128), vst[s][:],
                   reads=[b_vst[s]], writes=[b_vt[g][blk]], more=True)
        pending.extend([(lambda d=d: step(d)) for d in range(4 * blk, 4 * blk + 4)])
    fw.barrier()
    fw.release(m0)
    return b_KG, b_VG, pending


def heads_B(kb, hT, b_hT, w_in, msk_in, KT, V, KG, VG, b_KG, b_VG, mixT):
    fw = kb.fw
    m0 = fw.mark()
    msk = fw.sbuf([128, 3, 512], BF16); b_msk = Buf()
    fw.dma(fw.sp, msk[:], msk_in, writes=[b_msk])
    qT = fw.sbuf([128, 3, NT], BF16); b_qT = Buf()
    gdT = [fw.sbuf([128, NT], BF16) for _ in range(2)]; b_gd = [Buf(), Buf()]
    KTt = [fw.sbuf([128, 128 * DILS[g] + NT], BF16) for g in range(3)]; b_K = [Buf() for _ in range(3)]
    Vt = [fw.sbuf([128, DILS[g], 16 // DILS[g] + 1, 128], BF16) for g in range(3)]; b_V = [Buf() for _ in range(3)]
    O32 = fw.sbuf([128, NT], F32); b_O = Buf()
    L32 = fw.sbuf([128, NT], F32); b_L = Buf()
    pT = [fw.sbuf([128, 512], BF16) for _ in range(2)]; b_pT = [Buf(), Buf()]
    mst = fw.sbuf([128, NT], BF16); b_mst = Buf()
    scale = 128.0 ** -0.5
    pi = 0

    def load_kv(d, g):
        hl = 128 * DILS[g]
        fw.dma(fw.sp, KTt[g][:, 0:hl], KG[d, 0:128, HOFF[g]:HOFF[g] + hl], reads=[b_KG[d]], writes=[b_K[g]])
        fw.dma(fw.sp, KTt[g][:, hl:hl + NT], KT[g][d * 128:(d + 1) * 128, :], writes=[b_K[g]], more=True)
        fw.dma(fw.sp, Vt[g][:, :, 0, :], VG[d, 0:128, HOFF[g]:HOFF[g] + hl].rearrange("j (r c) -> j r c", c=128),
               reads=[b_VG[d]], writes=[b_V[g]])
        for r in range(DILS[g]):
            src_o = V[g][:, d * 128:(d + 1) * 128].rearrange("(b j r) c -> j r b c", j=128, r=DILS[g])[:, r, :, :]
            fw.dma(fw.sp, Vt[g][:, r, 1:, :], src_o, writes=[b_V[g]], more=True)

    for d in range(16):
        w, bw = kb.wnext(16, 512)
        for g in range(3):
            kb.load_w(w, bw, w_in, g * 2048 + d * 128, 128, g * 128, more=(g > 0))
        kb.load_w(w, bw, w_in, 6144 + d * 128, 128, 384, more=True)
        if d == 0:
            for g in range(3):
                load_kv(0, g)
        for tb in range(4):
            tsl = slice(tb * 512, (tb + 1) * 512)
            for s in range(4):
                ps, bps = proj_fm(kb, w[:, :, s * 128:(s + 1) * 128], bw, hT, b_hT, tb)
                if s < 3:
                    fw.op(fw.act, lambda e: e.copy(out=qT[:, s, tsl], in_=ps[:, :]), reads=[bps], writes=[b_qT])
                else:
                    fw.op(fw.act, lambda e: e.activation(out=gdT[d % 2][:, tsl], in_=ps[:, :], func=AF.Silu), reads=[bps], writes=[b_gd[d % 2]])
        items = []
        for g in range(3):
            dil = DILS[g]
            nb = 16 // dil
            blocks = [(r, b) for r in range(dil) for b in range(nb)]
            for p0 in range(0, 16, 2):
                items.append((g, blocks[p0:p0 + 2]))
        state = {}

        def emit_S(i):
            g, pair = items[i]
            dil = DILS[g]
            qv = qT[:, g, :].rearrange("p (b j r) -> p r b j", j=128, r=dil)
            kv = KTt[g][:, :].rearrange("p (b j r) -> p r b j", j=128, r=dil)
            u = i % 2
            ps_s, bs = kb.bank()
            var = (1 if pair[0][1] == 0 else 0) + (1 if pair[1][1] == 0 else 0)

            def f(e):
                ins = e.matmul(ps_s[:, :], lhsT=kb.ident, rhs=msk[:, var, :], start=True, stop=False)
                n = 0
                for qi, (r, b) in enumerate(pair):
                    for sl, kb_ in ((2 * qi, b + 1), (2 * qi + 1, b)):
                        n += 1
                        ins = e.matmul(ps_s[:, sl * 128:(sl + 1) * 128], lhsT=kv[:, r, kb_, :], rhs=qv[:, r, b, :],
                                       start=False, stop=(n == 4))
                return ins
            fw.op(fw.pe, f, reads=[b_K[g], b_qT, b_msk, kb.b_cst], writes=[bs])
            fw.op(fw.act, lambda e: e.activation(out=pT[u][:], in_=ps_s[:, :], func=AF.Exp, scale=scale), reads=[bs], writes=[b_pT[u]])

        def emit_PV(i):
            g, pair = items[i]
            dil = DILS[g]
            Ov = O32[:, :].rearrange("p (b j r) -> p r b j", j=128, r=dil)
            Lv = L32[:, :].rearrange("p (b j r) -> p r b j", j=128, r=dil)
            u = i % 2
            ps_o, bo = kb.bank()
            ps_l, bl = kb.bank()
            for qi, (r, b) in enumerate(pair):
                kb.mm(ps_o[:, qi * 128:(qi + 1) * 128],
                      [(Vt[g][:, r, b + 1, :], pT[u][:, (2 * qi) * 128:(2 * qi + 1) * 128]),
                       (Vt[g][:, r, b, :], pT[u][:, (2 * qi + 1) * 128:(2 * qi + 2) * 128])], [b_V[g], b_pT[u]], bo)
                kb.mm(ps_l[:, qi * 128:(qi + 1) * 128],
                      [(kb.ones, pT[u][:, (2 * qi) * 128:(2 * qi + 1) * 128]),
                       (kb.ones, pT[u][:, (2 * qi + 1) * 128:(2 * qi + 2) * 128])], [kb.b_cst, b_pT[u]], bl)
            (r0, b0), (r1, b1) = pair
            if r0 == r1:
                oo = Ov[:, r0, b0:b0 + 2, :]
                ll = Lv[:, r0, b0:b0 + 2, :]
            else:
                oo = Ov[:, r0:r0 + 2, b0, :]
                ll = Lv[:, r0:r0 + 2, b0, :]
            pso = ps_o[:, 0:256].rearrange("p (a j) -> p a j", a=2)
            psl = ps_l[:, 0:256].rearrange("p (a j) -> p a j", a=2)
            if g == 0:
                fw.op(fw.act, lambda e: e.copy(out=oo, in_=pso), reads=[bo], writes=[b_O])
                fw.op(fw.act, lambda e: e.copy(out=ll, in_=psl), reads=[bl], writes=[b_L])
            else:
                fw.op(fw.dve, lambda e: e.tensor_tensor(out=oo, in0=oo, in1=pso, op=ALU.add), reads=[bo, b_O], writes=[b_O])
                fw.op(fw.dve, lambda e: e.tensor_tensor(out=ll, in0=ll, in1=psl, op=ALU.add), reads=[bl, b_L], writes=[b_L])
            if i % 8 == 7 and d + 1 < 16:
                load_kv(d + 1, g)

        for i in range(len(items) + 1):
            if i < len(items):
                emit_S(i)
            if i >= 1:
                emit_PV(i - 1)
        fw.op(fw.dve, lambda e: e.reciprocal(out=L32[:], in_=L32[:]), reads=[b_L], writes=[b_L])
        fw.op(fw.dve, lambda e: e.tensor_tensor(out=O32[:], in0=O32[:], in1=L32[:], op=ALU.mult), reads=[b_L, b_O], writes=[b_O])
        fw.op(fw.dve, lambda e: e.tensor_tensor(out=mst[:], in0=O32[:], in1=gdT[d % 2][:], op=ALU.mult), reads=[b_O, b_gd[d % 2]], writes=[b_mst])
        fw.dma(fw.sp, mixT[d * 128:(d + 1) * 128, :], mst[:], reads=[b_mst])
    fw.barrier()
    fw.release(m0)


def build_fused(gC, stages=5):
    nc = bass.Bass("TRN2", target_bir_lowering=False)
    dt = nc.dram_tensor

    def inp(name, shape, d=F32):
        return dt(name, shape, d, kind="ExternalInput").ap()
    x = inp("x", [NT, D])
    mem = inp("mem", [256, D])
    norm_a = inp("norm_a", [2, D])
    w_in_a = inp("w_in_a", [2, D, 8192])
    w_out_a = inp("w_out_a", [2, 3072, D])
    norm_b = inp("norm_b", [2, D])
    w_in_b = inp("w_in_b", [2, D, 10240])
    w_out_b = inp("w_out_b", [2, 3072, D])
    w_mem_kv = inp("w_mem_kv", [4, D, 2048])
    g_m = inp("g_m", [1, D])
    g_kv = inp("g_kv", [1, D])
    w_kv = inp("w_kv", [D, 12288])
    g_f = inp("g_f", [1, D])
    cs_in = inp("cs", [128, 2, NT])
    dm_in = inp("dm", [128, 17, 128])
    cst_in = inp("cst", [128, 3, 128], BF16)
    msk_in = inp("msk", [128, 3, 512], BF16)
    y = dt("y", [NT, D], F32, kind="ExternalOutput").ap()
    xp = [dt(f"xp{i}", [NT, D], F32).ap() for i in range(2)]
    mixT = dt("mixT", [3072, NT], BF16).ap()
    cc_src = [dt(f"ccs{l}", [8, 128, 256], F32).ap() for l in range(2)]
    cc_dst = [dt(f"ccd{l}", [8, 256, 256], F32).ap() for l in range(2)]
    KT = [dt(f"KT{g}", [D, NT], BF16).ap() for g in range(3)]
    V = [dt(f"V{g}", [NT, D], BF16).ap() for g in range(3)]
    KVS = dt("KVS", [16, 128, 2 * HSUM], BF16).ap()
    KVG = dt("KVG", [16, 256, 2 * HSUM], BF16).ap()
    KS = KVS[:, :, 0:HSUM]
    VS = KVS[:, :, HSUM:2 * HSUM]
    KG = KVG[:, :, 0:HSUM]
    VG = KVG[:, :, HSUM:2 * HSUM]

    kb = KB(nc, cst_in)
    fw = kb.fw
    out_toks = []
    hT = fw.sbuf([128, 16, NT], BF16)
    b_hT = [Buf() for _ in range(NCH)]

    xin = x
    for l in range(min(2, stages)):
        xout = xp[l]
        phase_norm(kb, xin, norm_a[l:l + 1, :], hT, b_hT, NCH)
        heads_A(kb, hT, b_hT, w_in_a[l], cs_in, dm_in, gC, mixT, cc_src[l], cc_dst[l])
        phase_mem(kb, mem, g_m, w_mem_kv[l], w_in_a[l], 6144, 7168, hT, b_hT, mixT)
        phase_out(kb, mixT, w_out_a[l], xin, xout, hT)
        xin = xout
    if stages >= 3:
        phase_norm(kb, xin, g_kv, hT, b_hT, NCH)
        b_KG, b_VG, xsteps = phase_kv(kb, hT, b_hT, w_kv, KT, V, KS, VS, KVS, KVG)
    for l in range(max(0, stages - 3)):
        xout = xp[l]
        hooks = None
        if l == 0:
            nx = len(xsteps)
            hooks = {m: xsteps[m * nx // 4:(m + 1) * nx // 4] for m in range(4)}
        phase_norm(kb, xin, norm_b[l:l + 1, :], hT, b_hT, NCH)
        phase_mem(kb, mem, g_m, w_mem_kv[2 + l], w_in_b[l], 8192, 9216, hT, b_hT, mixT, hooks)
        heads_B(kb, hT, b_hT, w_in_b[l], msk_in, KT, V, KG, VG, b_KG, b_VG, mixT)
        phase_out(kb, mixT, w_out_b[l], xin, xout, hT)
        xin = xout
    phase_norm(kb, xin, g_f, None, None, NCH, out_ap=y, out_toks=out_toks)
    finish(kb, out_toks)
    return nc


def _consts():
    bf = ml_dtypes.bfloat16
    cst = np.zeros((128, 3, 128), np.float32)
    cst[:, 0, :] = np.eye(128)
    cst[:, 1, :] = 1.0
    pm = np.zeros((128, 128), np.float32)
    for i in range(128):
        pm[(i + 64) % 128, i] = 1.0
    cst[:, 2, :] = pm
    half = 64
    inv = (1.0 / (np.float32(10000.0) ** np.linspace(0.0, 1.0, half, dtype=np.float32))).astype(np.float32)
    cs = []
    for hf in range(2):
        pos = (np.arange(NT, dtype=np.float32) + np.float32(hf * NT))
        ang = (pos[:, None] * inv[None, :]).astype(np.float32)
        c = np.cos(ang).astype(np.float32).T
        s = np.sin(ang).astype(np.float32).T
        t = np.zeros((128, 2, NT), np.float32)
        t[:64, 0] = c; t[64:, 0] = c
        t[:64, 1] = -s; t[64:, 1] = s
        cs.append(t)
    hh = np.arange(8, dtype=np.float64)
    log_g = np.log1p(-(2.0 ** (-5.0 - hh)))
    pos = np.arange(128, dtype=np.float64)
    scale = 128.0 ** -0.5
    dms = []
    for hf in range(2):
        dm = np.zeros((128, 17, 128), np.float32)
        for h in range(8):
            kk = pos[:, None]
            qq = pos[None, :]
            dm[:, h, :] = np.where(qq >= kk, scale * np.exp(-(kk + 1.0) * log_g[h]), 0.0)
            dm[:, 8 + h, :] = np.exp((pos + 1.0) * log_g[h])[None, :]
            dm[:, 16, h] = scale * np.exp((127.0 - pos) * log_g[h])
        dm[:, 16, 8] = float(hf)
        dms.append(dm)
    gC = np.exp(128.0 * log_g)
    kk = np.arange(128)[:, None]
    qq = np.arange(128)[None, :]
    NEG = -1.0e4
    cur = np.where(kk <= qq, 0.0, NEG).astype(np.float32)
    prev = np.where(kk >= qq, 0.0, NEG).astype(np.float32)
    dead = np.full((128, 128), NEG, np.float32)
    msk = []
    for hf in range(2):
        ph = prev if hf == 1 else dead
        m = np.stack([np.concatenate([cur, prev, cur, prev], axis=1),
                      np.concatenate([cur, ph, cur, prev], axis=1),
                      np.concatenate([cur, ph, cur, ph], axis=1)], axis=1)
        msk.append(m.astype(bf))
    return cst.astype(bf), cs, dms, gC, msk


def kernel(x, mem, norm_a, w_in_a, w_out_a, norm_b, w_in_b, w_out_b, w_mem_kv, mem_norm_g, kv_norm_g, w_kv, final_norm_g):
    f32 = np.float32
    x = np.asarray(x, f32); mem = np.asarray(mem, f32)
    shared = {
        "norm_a": np.asarray(norm_a, f32), "w_in_a": np.asarray(w_in_a, f32), "w_out_a": np.asarray(w_out_a, f32),
        "norm_b": np.asarray(norm_b, f32), "w_in_b": np.asarray(w_in_b, f32), "w_out_b": np.asarray(w_out_b, f32),
        "w_mem_kv": np.asarray(w_mem_kv, f32), "g_m": np.asarray(mem_norm_g, f32).reshape(1, D),
        "g_kv": np.asarray(kv_norm_g, f32).reshape(1, D), "w_kv": np.asarray(w_kv, f32),
        "g_f": np.asarray(final_norm_g, f32).reshape(1, D),
    }
    cst, cs, dms, gC, msk = _consts()
    B = x.shape[0]
    nc = build_fused(gC)
    maps = []
    for c in range(2 * B):
        b, hf = c // 2, c % 2
        maps.append(dict(shared, x=np.ascontiguousarray(x[b, hf * NT:(hf + 1) * NT]), mem=mem[b],
                         cs=cs[hf], dm=dms[hf], cst=cst, msk=msk[hf]))
    res = run_bass_kernel_spmd(nc, maps, core_ids=list(range(2 * B))).results
    out = np.stack([np.concatenate([res[2 * b + hf]["y"] for hf in range(2)], axis=0) for b in range(B)], axis=0)
    return out.astype(f32)
```
